# Optimizing a Trainium2 kernel written in Bass

```python
import jax, jax.numpy as jnp
from jax import lax
import numpy as np

D_MODEL = 1024
BATCH = 8
SEQ = 8192
DEPTH = 1
DEC_BATCH = 8
DEC_SEQ = 32
PAST_LEN = 1024

CHUNK = 64
Q_BLOCK = 128
MIX_WIDTH = D_MODEL
A_WIDTH = MIX_WIDTH // 2
B_WIDTH = MIX_WIDTH - A_WIDTH
A_HEADS = 4
A_DK = A_WIDTH // A_HEADS
A_DV = A_WIDTH // A_HEADS
B_HEADS = 8
B_DH = B_WIDTH // B_HEADS
D_FF = 4 * D_MODEL
EPS = 1e-6
SPLITS = [A_WIDTH, 2 * A_WIDTH, 3 * A_WIDTH, 4 * A_WIDTH,
          4 * A_WIDTH + B_WIDTH, 4 * A_WIDTH + 2 * B_WIDTH, 4 * A_WIDTH + 3 * B_WIDTH]
IN_COLS = 4 * A_WIDTH + 3 * B_WIDTH + B_HEADS

kernel_name = 'hymba_hgrn2_fox_streaming_step'


def rms_norm(x, g):
    xf = x.astype(jnp.float32)
    y = xf * lax.rsqrt(jnp.mean(xf * xf, axis=-1, keepdims=True) + EPS)
    return (y * g.astype(jnp.float32)).astype(x.dtype)


def mixer_inputs(x, norm1, w_in, b_fox_f, q_gain, k_gain, lb):
    B, T, _ = x.shape
    h = rms_norm(x, norm1)
    z = h @ w_in
    qa, fa, ia, ga, qb, kb, vb, fb = jnp.split(z, SPLITS, axis=-1)
    f = lb + (1.0 - lb) * jax.nn.sigmoid(fa.astype(jnp.float32))
    hq = qa.astype(jnp.float32).reshape(B, T, A_HEADS, A_DK)
    hk = (1.0 - f).reshape(B, T, A_HEADS, A_DK)
    hlogf = jnp.log(f).reshape(B, T, A_HEADS, A_DK)
    hv = ia.astype(jnp.float32).reshape(B, T, A_HEADS, A_DV)
    fq = rms_norm(qb.reshape(B, T, B_HEADS, B_DH), q_gain)
    fk = rms_norm(kb.reshape(B, T, B_HEADS, B_DH), k_gain)
    fv = vb.reshape(B, T, B_HEADS, B_DH)
    flogf = jax.nn.log_sigmoid(fb.astype(jnp.float32) + b_fox_f.astype(jnp.float32))
    return (hq, hk, hv, hlogf, ga), (fq, fk, fv, flogf)


def hgrn2_chunk(S0, q, k, v, logf):
    C = q.shape[1]
    b = jnp.cumsum(logf, axis=1)
    b_ref = b[:, C // 2:C // 2 + 1]
    inter = jnp.einsum('bchk,bhkv->bchv', q * jnp.exp(b), S0)
    qr = q * jnp.exp(b - b_ref)
    kr = k * jnp.exp(b_ref - b)
    A = jnp.einsum('bchk,bshk->bhcs', qr, kr)
    mask = jnp.tril(jnp.ones((C, C), dtype=bool))
    A = jnp.where(mask[None, None], A, 0.0)
    o = inter + jnp.einsum('bhcs,bshv->bchv', A, v)
    b_last = b[:, -1:]
    S_new = jnp.exp(b_last[:, 0])[..., None] * S0 + jnp.einsum('bshk,bshv->bhkv', k * jnp.exp(b_last - b), v)
    return S_new, o


def hgrn2_prompt(q, k, v, logf):
    B, T, H, _ = q.shape
    nc = T // CHUNK

    def to_chunks(t):
        return jnp.moveaxis(t.reshape(B, nc, CHUNK, H, t.shape[-1]), 1, 0)

    def step(S_c, xs):
        return hgrn2_chunk(S_c, *xs)

    S0 = jnp.zeros((B, H, A_DK, A_DV), jnp.float32)
    S_fin, o = lax.scan(step, S0, (to_chunks(q), to_chunks(k), to_chunks(v), to_chunks(logf)))
    return S_fin, jnp.moveaxis(o, 0, 1).reshape(B, T, H, A_DV)


def fox_prompt_attention(q, k, v, logf):
    B, T, H, Dh = q.shape
    nb = T // Q_BLOCK
    c = jnp.cumsum(logf, axis=1)
    cT = jnp.transpose(c, (0, 2, 1))
    pos = jnp.arange(T)
    qblk = jnp.moveaxis(q.reshape(B, nb, Q_BLOCK, H, Dh), 1, 0)
    cblk = jnp.moveaxis(c.reshape(B, nb, Q_BLOCK, H), 1, 0)
    pblk = pos.reshape(nb, Q_BLOCK)
    scale = Dh ** -0.5

    def block(args):
        qi, ci, pi = args
        s = jnp.einsum('bqhd,bkhd->bhqk', qi, k, preferred_element_type=jnp.float32) * scale
        s = s + jnp.transpose(ci, (0, 2, 1))[..., :, None] - cT[..., None, :]
        s = jnp.where(pi[:, None] >= pos[None, :], s, -jnp.inf)
        p = jax.nn.softmax(s, axis=-1)
        return jnp.einsum('bhqk,bkhd->bqhd', p.astype(v.dtype), v)

    o = lax.map(block, (qblk, cblk, pblk))
    return jnp.moveaxis(o, 0, 1).reshape(B, T, H * Dh)


def fox_sample_attention(q, k_new, v_new, logf_new, k_cache, v_cache, logf_cache):
    B, T, H, Dh = q.shape
    P = k_cache.shape[1]
    k = jnp.concatenate([k_cache.astype(k_new.dtype), k_new], axis=1)
    v = jnp.concatenate([v_cache.astype(v_new.dtype), v_new], axis=1)
    c = jnp.cumsum(jnp.concatenate([logf_cache.astype(jnp.float32), logf_new], axis=1), axis=1)
    cT = jnp.transpose(c, (0, 2, 1))
    s = jnp.einsum('bqhd,bkhd->bhqk', q, k, preferred_element_type=jnp.float32) * (Dh ** -0.5)
    s = s + cT[..., P:, None] - cT[..., None, :]
    qpos = P + jnp.arange(T)
    kpos = jnp.arange(P + T)
    s = jnp.where(qpos[:, None] >= kpos[None, :], s, -jnp.inf)
    p = jax.nn.softmax(s, axis=-1)
    o = jnp.einsum('bhqk,bkhd->bqhd', p.astype(v.dtype), v)
    return o.reshape(B, T, H * Dh)


def mixer_output_and_ffn(x, o_a, ga, o_b, hgrn_out_norm, w_out, norm2, w_up, w_down):
    B, T, _ = x.shape
    oa = rms_norm(o_a, hgrn_out_norm.reshape(A_HEADS, A_DV))
    oa = (oa.reshape(B, T, A_WIDTH) * jax.nn.silu(ga.astype(jnp.float32))).astype(x.dtype)
    mix = jnp.concatenate([oa, o_b.astype(x.dtype)], axis=-1)
    x = x + mix @ w_out
    h = rms_norm(x, norm2)
    return x + jnp.square(jax.nn.relu(h @ w_up)) @ w_down


def setup_inputs(seed: int = 0) -> dict:
    key = jax.random.key(seed)
    ks = jax.random.split(key, 20)
    nrm = jax.random.normal
    f32 = jnp.float32
    return {
        'x_prompt': nrm(ks[0], (BATCH, SEQ, D_MODEL), f32),
        'x_sample': nrm(ks[1], (DEC_BATCH, DEC_SEQ, D_MODEL), f32),
        'cache_fox_k': nrm(ks[2], (DEPTH, DEC_BATCH, PAST_LEN, B_HEADS, B_DH), f32),
        'cache_fox_v': nrm(ks[3], (DEPTH, DEC_BATCH, PAST_LEN, B_HEADS, B_DH), f32),
        'cache_fox_logf': jax.nn.log_sigmoid(2.0 + nrm(ks[4], (DEPTH, DEC_BATCH, PAST_LEN, B_HEADS), f32)),
        'state_hgrn': 0.5 * nrm(ks[5], (DEPTH, DEC_BATCH, A_HEADS, A_DK, A_DV), f32),
        'norm1': 1.0 + 0.05 * nrm(ks[6], (DEPTH, D_MODEL), f32),
        'w_in': nrm(ks[7], (DEPTH, D_MODEL, IN_COLS), f32) * D_MODEL ** -0.5,
        'b_fox_f': 2.0 + 0.1 * nrm(ks[8], (DEPTH, B_HEADS), f32),
        'q_norm_gain': 1.0 + 0.05 * nrm(ks[9], (DEPTH, B_DH), f32),
        'k_norm_gain': 1.0 + 0.05 * nrm(ks[10], (DEPTH, B_DH), f32),
        'hgrn_lb_logits': 0.1 * nrm(ks[11], (DEPTH + 1, A_WIDTH), f32),
        'hgrn_out_norm': 1.0 + 0.05 * nrm(ks[12], (DEPTH, A_WIDTH), f32),
        'w_out': nrm(ks[13], (DEPTH, MIX_WIDTH, D_MODEL), f32) * MIX_WIDTH ** -0.5,
        'norm2': 1.0 + 0.05 * nrm(ks[14], (DEPTH, D_MODEL), f32),
        'w_up': nrm(ks[15], (DEPTH, D_MODEL, D_FF), f32) * D_MODEL ** -0.5,
        'w_down': nrm(ks[16], (DEPTH, D_FF, D_MODEL), f32) * D_FF ** -0.5,
    }


def reference(x_prompt, x_sample, cache_fox_k, cache_fox_v, cache_fox_logf, state_hgrn,
              norm1, w_in, b_fox_f, q_norm_gain, k_norm_gain, hgrn_lb_logits, hgrn_out_norm,
              w_out, norm2, w_up, w_down):
    lb_all = jnp.cumsum(jax.nn.softmax(hgrn_lb_logits.astype(jnp.float32), axis=0), axis=0)
    xp, xs = x_prompt, x_sample
    kp, vp, lp, sp = [], [], [], []
    ksl, vsl, lsl, ssl = [], [], [], []
    for l in range(DEPTH):
        lb = lb_all[l]
        (hq, hk, hv, hlf, ga), (fq, fk, fv, flf) = mixer_inputs(
            xp, norm1[l], w_in[l], b_fox_f[l], q_norm_gain[l], k_norm_gain[l], lb)
        S_p, o_a = hgrn2_prompt(hq, hk, hv, hlf)
        o_b = fox_prompt_attention(fq, fk, fv, flf)
        xp = mixer_output_and_ffn(xp, o_a, ga, o_b, hgrn_out_norm[l], w_out[l], norm2[l], w_up[l], w_down[l])
        kp.append(fk); vp.append(fv); lp.append(flf); sp.append(S_p.astype(x_prompt.dtype))
        (hq, hk, hv, hlf, ga), (fq, fk, fv, flf) = mixer_inputs(
            xs, norm1[l], w_in[l], b_fox_f[l], q_norm_gain[l], k_norm_gain[l], lb)
        S_s, o_a = hgrn2_chunk(state_hgrn[l].astype(jnp.float32), hq, hk, hv, hlf)
        o_b = fox_sample_attention(fq, fk, fv, flf, cache_fox_k[l], cache_fox_v[l], cache_fox_logf[l])
        xs = mixer_output_and_ffn(xs, o_a, ga, o_b, hgrn_out_norm[l], w_out[l], norm2[l], w_up[l], w_down[l])
        ksl.append(fk); vsl.append(fv); lsl.append(flf); ssl.append(S_s.astype(state_hgrn.dtype))
    return (xp, xs, jnp.stack(kp), jnp.stack(vp), jnp.stack(lp), jnp.stack(sp),
            jnp.stack(ksl), jnp.stack(vsl), jnp.stack(lsl), jnp.stack(ssl))
```

```python
import numpy as np
from contextlib import ExitStack
import concourse.bass as bass
import concourse.mybir as mybir
from concourse.bass_utils import run_bass_kernel_spmd

F32 = mybir.dt.float32
BF16 = mybir.dt.bfloat16
AF = mybir.ActivationFunctionType
ALU = mybir.AluOpType
AX = mybir.AxisListType

SEM_CAP = 12000
N_DMA_SEMS = 40
EPS = 1e-6
DBG = None
REORDER = True
SIM_PARITY = None
CP_PRIO = False
SEQ_P1 = False
COST_SCALE = {}
PIPE = True
DBG2 = None
D = 1024
INC = 3592
DFF = 4096


import types


def _freeze(fn):
    if fn.__closure__ is None:
        return fn
    cells = []
    for c in fn.__closure__:
        try:
            cells.append(types.CellType(c.cell_contents))
        except ValueError:
            cells.append(c)
    return types.FunctionType(fn.__code__, fn.__globals__, fn.__name__, fn.__defaults__, tuple(cells))


class Op:
    __slots__ = ("id", "eng", "fn", "is_dma", "deps", "dur", "lat", "is_output", "sem", "val", "waits", "inc")


class Tok:
    __slots__ = ("w", "r")

    def __init__(self):
        self.w = None
        self.r = []


class _Fake:
    def __init__(self):
        self.rec = None

    def __getattr__(self, name):
        def f(*a, **k):
            self.rec = (name, a, k)
            return self
        return f


def _prod(xs):
    r = 1
    for x in xs:
        r *= int(x)
    return r


def _estimate(eng, fn):
    fk = _Fake()
    try:
        fn(fk)
        name, a, k = fk.rec
        out = k.get("out", a[0] if a else None)
        shape = tuple(out.shape)
        free = _prod(shape[1:]) if len(shape) > 1 else 1
    except Exception:
        return 500.0
    if eng == "pe":
        if name == "transpose":
            return 70.0
        lhsT = k.get("lhsT", a[1] if len(a) > 1 else None)
        mult = 4.0 if (lhsT is not None and lhsT.dtype == F32) else 1.0
        return (max(free, 48) / 2.4 + 12.0) * mult
    if eng == "act":
        return 224.0 + 0.833 * free
    if eng == "dve":
        if name == "reciprocal":
            return 60.0 + 6.2 * free
        if name == "tensor_tensor_scan":
            return 60.0 + 2.1 * free
        return 60.0 + 1.04 * free
    if name == "memset":
        return 100.0 + 0.5 * free
    return 100.0 + 2.0 * free


class Sched:
    ENGS = ("pe", "act", "dve", "pool", "sp")

    def __init__(self, nc, stack, reorder=True):
        self.nc = nc
        self.stack = stack
        self.reorder = reorder
        self.ops = []
        self.toks = {}

    def _tok(self, t):
        k = self.toks.get(t)
        if k is None:
            k = self.toks[t] = Tok()
        return k

    def _record(self, eng, fn, reads, writes, is_dma, dur, lat, is_output):
        if SIM_PARITY is not None:
            def tr(t):
                if isinstance(t, tuple) and t and t[0] in ("A", "B", "Bq", "Bk", "xb", "hb", "ss1", "ss8", "lf", "krT", "qrT", "krTok", "AmT", "vtok", "sc_e", "sc_a", "sc_g", "wbuf", "ktst"):
                    return t + ("par", SIM_PARITY[0])
                if isinstance(t, tuple) and t and t[0] == "ps" and SIM_PARITY[1]:
                    return t + ("p1", SIM_PARITY[0])
                if t in ("hT", "qT", "lft", "utmp", "Sbf", "ktst"):
                    return (t, "par", SIM_PARITY[0])
                return t
            reads = [tr(t) for t in reads]
            writes = [tr(t) for t in writes]
        deps = set()
        for t in reads:
            k = self._tok(t)
            if k.w is not None:
                deps.add(k.w)
        for t in writes:
            k = self._tok(t)
            if k.w is not None:
                deps.add(k.w)
            deps.update(k.r)
        o = Op()
        o.id = len(self.ops)
        o.eng, o.fn, o.is_dma, o.deps, o.dur, o.lat, o.is_output = eng, fn, is_dma, deps, dur, lat, is_output
        self.ops.append(o)
        for t in reads:
            self._tok(t).r.append(o.id)
        for t in writes:
            k = self._tok(t)
            k.w = o.id
            k.r = []
        return o

    def op(self, eng, fn, reads=(), writes=()):
        fn = _freeze(fn)
        return self._record(eng, fn, reads, writes, False, _estimate(eng, fn) * COST_SCALE.get(eng, 1.0), 0.0, False)

    def dma(self, q, out, in_, reads=(), writes=(), is_output=False):
        nbytes = _prod(out.shape) * (2 if out.dtype == BF16 else 4)
        fn = lambda e, o=out, i=in_: e.dma_start(out=o, in_=i)
        return self._record(q, fn, reads, writes, True, 60.0, (2000.0 + nbytes / 150.0) * COST_SCALE.get("dma", 1.0), is_output)

    def _schedule(self):
        import heapq
        ops = self.ops
        n = len(ops)
        order = {e: [] for e in self.ENGS}
        if not self.reorder:
            for o in ops:
                order[o.eng].append(o.id)
            return order
        succ = [[] for _ in range(n)]
        indeg = [0] * n
        for o in ops:
            indeg[o.id] = len(o.deps)
            for d in o.deps:
                succ[d].append(o.id)
        prio = [0.0] * n
        if CP_PRIO:
            for o in reversed(ops):
                best = 0.0
                for sid in succ[o.id]:
                    so = ops[sid]
                    v = prio[sid] + (60.0 if so.eng == o.eng else 160.0)
                    if v > best:
                        best = v
                prio[o.id] = best + o.dur + o.lat
        ready_t = [0.0] * n
        avail = {e: [] for e in self.ENGS}
        free_at = {e: 0.0 for e in self.ENGS}
        busy = {e: False for e in self.ENGS}
        events = []

        def try_start(e, t):
            if busy[e] or not avail[e]:
                return
            cand = [x for x in avail[e] if x[0] <= t]
            if cand:
                if CP_PRIO:
                    pick = min(cand, key=lambda x: (-prio[x[1]], x[1]))
                else:
                    pick = min(cand, key=lambda x: x[1])
            else:
                pick = min(avail[e], key=lambda x: (x[0], x[1]))
                heapq.heappush(events, (pick[0], 2, e))
                return
            avail[e].remove(pick)
            oid = pick[1]
            o = ops[oid]
            order[e].append(oid)
            busy[e] = True
            tend = t + o.dur
            heapq.heappush(events, (tend, 1, oid))
            heapq.heappush(events, (tend + o.lat, 0, oid))

        for o in ops:
            if indeg[o.id] == 0:
                avail[o.eng].append((0.0, o.id))
        for e in self.ENGS:
            try_start(e, 0.0)
        while events:
            t, kind, x = heapq.heappop(events)
            if kind == 1:
                e = ops[x].eng
                busy[e] = False
                try_start(e, t)
            elif kind == 2:
                try_start(x, t)
            else:
                o = ops[x]
                for sid in succ[x]:
                    so = ops[sid]
                    lat = 60.0 if so.eng == o.eng else 160.0
                    if t + lat > ready_t[sid]:
                        ready_t[sid] = t + lat
                    indeg[sid] -= 1
                    if indeg[sid] == 0:
                        avail[so.eng].append((ready_t[sid], sid))
                        try_start(so.eng, t)
        import os as _os
        if _os.environ.get("SCHED_DEBUG"):
            print("SIM makespan us", max(free_at.values()) / 1000.0 if False else t / 1000.0, "ops", n, {e: len(v) for e, v in order.items()})
        assert sum(len(v) for v in order.values()) == n, "scheduler lost ops"
        return order

    def finish(self):
        order = self._schedule()
        ops = self.ops
        nc = self.nc
        self.sem_id = {}

        def newsem(name):
            sm = self.stack.enter_context(nc.semaphore(name))
            self.sem_id[id(sm)] = len(self.sem_id)
            return sm

        dma_sems = {q: [newsem(f"d{q}{i}") for i in range(N_DMA_SEMS)] for q in ("sp", "pool")}
        self.order = order
        for e in self.ENGS:
            ci = 0
            dj = 0
            esems = []
            for oid in order[e]:
                o = ops[oid]
                if o.is_dma:
                    o.sem = dma_sems[e][dj % N_DMA_SEMS]
                    o.val = 16 * (dj // N_DMA_SEMS + 1)
                    o.inc = 16
                    dj += 1
                else:
                    ep, v = divmod(ci, SEM_CAP)
                    while len(esems) <= ep:
                        esems.append(newsem(f"e{e}{len(esems)}"))
                    o.sem = esems[ep]
                    o.val = v + 1
                    o.inc = 1
                    ci += 1
        for e in self.ENGS:
            wd = {}
            for oid in order[e]:
                o = ops[oid]
                waits = []
                for d in sorted(o.deps):
                    do = ops[d]
                    if (not do.is_dma) and (not o.is_dma) and do.eng == e and e == "pe":
                        continue
                    sid = self.sem_id[id(do.sem)]
                    if wd.get(sid, 0) >= do.val:
                        continue
                    wd[sid] = do.val
                    waits.append((do.sem, do.val))
                if o.is_dma and o.val > 16:
                    sid = self.sem_id[id(o.sem)]
                    if wd.get(sid, 0) < o.val - 16:
                        wd[sid] = o.val - 16
                        waits.append((o.sem, o.val - 16))
                o.waits = waits
        fin = {}
        for o in ops:
            if o.is_output:
                sid = self.sem_id[id(o.sem)]
                if fin.get(sid, (None, 0))[1] < o.val:
                    fin[sid] = (o.sem, o.val)
        self.final_waits = list(fin.values())

    def emit(self, block):
        S = self

        def run(engobj, name, final=False):
            for oid in S.order[name]:
                o = S.ops[oid]
                for (sm, v) in o.waits:
                    engobj.wait_ge(sm, v)
                o.fn(engobj).then_inc(o.sem, o.inc)
            if final:
                for (sm, v) in S.final_waits:
                    engobj.wait_ge(sm, v)

        @block.tensor
        def _(e):
            run(e, "pe")

        @block.scalar
        def _(e):
            run(e, "act")

        @block.vector
        def _(e):
            run(e, "dve")

        @block.gpsimd
        def _(e):
            run(e, "pool")

        @block.sync
        def _(e):
            run(e, "sp", final=True)


class _Stop(Exception):
    pass


def build_program(NT, with_sample=True, stop=None):
    SEQ = NT * 512
    NKB = max(4 * NT, 9)
    nc = bass.Bass("TRN2", target_bir_lowering=False)

    def din(name, shape, dt=F32):
        return nc.dram_tensor(name, shape, dt, kind="ExternalInput").ap()

    def dout(name, shape):
        return nc.dram_tensor(name, shape, F32, kind="ExternalOutput").ap()

    xp = din("xp", [SEQ, D]); xs = din("xs", [32, D])
    ck = din("ck", [1024, 512]); cv = din("cv", [1024, 512]); cl = din("cl", [1024, 8])
    sth = din("sth", [4, 128, 128])
    w_in = din("w_in", [D, INC]); w_out = din("w_out", [D, D]); w_up = din("w_up", [D, DFF]); w_down = din("w_down", [DFF, D])
    n1T = din("n1T", [128, 8]); n2T = din("n2T", [128, 8]); gnT = din("gnT", [128, 4]); lbl = din("lbl", [128, 8])
    qg = din("qg", [1, 64]); kg = din("kg", [1, 64]); bfx = din("bfx", [1, 8])
    c_ident = din("c_ident", [128, 128]); c_tri = din("c_tri", [128, 128]); c_maskA = din("c_maskA", [128, 128])
    c_maskD = din("c_maskD", [128, 2048])

    yp = dout("yp", [SEQ, D]); ys = dout("ys", [32, D])
    nk = dout("nk", [SEQ, 512]); nv = dout("nv", [SEQ, 512]); nl = dout("nl", [SEQ, 8]); nh = dout("nh", [4, 128, 128])
    nks = dout("nks", [32, 512]); nvs = dout("nvs", [32, 512]); nls = dout("nls", [32, 8]); nhs = dout("nhs", [4, 128, 128])

    wib = nc.dram_tensor("wib", [7, 128, 4096], BF16).ap()
    wif = nc.dram_tensor("wif", [128, 64], BF16).ap()
    woA = nc.dram_tensor("woA", [2, 128, 2048], BF16).ap()
    woB = nc.dram_tensor("woB", [2, 64, 4096], BF16).ap()
    wub = nc.dram_tensor("wub", [8, 128, 4096], BF16).ap()
    wdb = nc.dram_tensor("wdb", [8, 128, 4096], BF16).ap()
    kTp = nc.dram_tensor("kTp", [4, 128, SEQ], BF16).ap()
    kTs = nc.dram_tensor("kTs", [4, 128, 1536], BF16).ap()

    with ExitStack() as st:
        def sb(name, shape, dt=F32):
            return st.enter_context(nc.sbuf_tensor(name, shape, dt))

        S = Sched(nc, st, reorder=REORDER)
        vaug = sb("vaug", [128, NKB, 8, 65], BF16)
        cT = sb("cT", [128, NKB, 8])
        carry = sb("carry", [128, 8]); cref = sb("cref", [128, 8])
        NKTB = 3
        ktb = [sb(f"ktb{i}", [128, 512], BF16) for i in range(NKTB)]
        qT = sb("qT", [128, 2, 4, 512], BF16)
        ktst = sb("ktst", [128, 4, 512], BF16)
        xblk = [sb(f"xblk{i}", [128, D]) for i in range(2)]
        hb0_ = sb("hb0", [128, D], BF16)
        hb = [hb0_, hb0_]
        hT = sb("hT", [128, 8, 512], BF16)
        NW = 3
        wbuf = [sb(f"wbuf{i}", [128, 4096], BF16) for i in range(NW)]
        scrA = sb("scrA", [128, 8, 512])
        scrB = sb("scrB", [128, 7, 512])
        qrT = sb("qrT", [128, 4, 512], BF16); krT = sb("krT", [128, 4, 512], BF16)
        krTok = sb("krTok", [128, 4, 512], BF16)
        AmT = sb("AmT", [128, 4, 4, 128], BF16)
        vtok = sb("vtok", [128, 4, 512], BF16)
        S32 = sb("S32", [128, 4, 128]); Sbf = sb("Sbf", [128, 4, 128], BF16); utmp = sb("utmp", [128, 4, 128])
        mixA2 = sb("mixA", [128, 2, 4, 512], BF16)
        mixB = sb("mixB", [128, 8, 512], BF16)
        pT = [sb(f"pT{i}", [128, 512], BF16) for i in range(3)]
        identb = sb("identb", [128, 128], BF16); tri = sb("tri", [128, 128]); onesf = sb("onesf", [128, 128])
        maskA = sb("maskA", [128, 128]); maskD = sb("maskD", [128, 4, 512], BF16); rmask = sb("rmask", [128, 512])
        n1s = sb("n1s", [128, 8]); n2s = sb("n2s", [128, 8]); gns = sb("gns", [128, 4]); lbls = sb("lbls", [128, 8])
        lb = sb("lb", [128, 4]); oml = sb("oml", [128, 4])
        qgb = sb("qgb", [128, 64]); kgb = sb("kgb", [128, 64]); bfb = sb("bfb", [128, 8])
        sc_e = sb("sc_e", [128, 4, 8]); sc_a = sb("sc_a", [128, 4, 8]); sc_g = sb("sc_g", [128, 4, 8])
        ss1 = [sb(f"ss1_{i}", [128, 1]) for i in range(2)]
        ss8 = [sb(f"ss8_{i}", [128, 8]) for i in range(2)]
        lf = [sb(f"lf{i}", [128, 8]) for i in range(2)]
        lft = sb("lft", [128, 8])

        PS = [st.enter_context(nc.psum_tensor(f"ps{i}", [128, 512], F32)) for i in range(8)]
        PSB = [p.bitcast(BF16) for p in PS]
        rot_state = [0]

        def rot():
            i = 4 + rot_state[0] % 4
            rot_state[0] += 1
            return i

        rotH_state = [0]
        rotX_state = [0]

        def rotH():
            i = (4, 5)[rotH_state[0] % 2]
            rotH_state[0] += 1
            return i

        def rotX():
            i = (6, 7)[rotX_state[0] % 2]
            rotX_state[0] += 1
            return i

        rotA_state = [0]
        rotF_state = [0]

        def rotA():
            i = (4, 5, 6)[rotA_state[0] % 3]
            rotA_state[0] += 1
            return i

        def rotF():
            i = (2, 3, 7)[rotF_state[0] % 3]
            rotF_state[0] += 1
            return i

        block = st.enter_context(nc.Block())

        def chk(name):
            if stop == name:
                raise _Stop()
        cnt = {"w": 0, "kt": 0, "x": 0, "p": 0, "s": 0, "cast": 0}

        XT = lambda i: [("xb", i, 0), ("xb", i, 1)]

        def A(i):
            return scrA[:, i, :]

        def B(i):
            return scrB[:, i, :]

        S.dma("sp", tri[:, :], c_tri, writes=["tri"])
        S.dma("sp", maskA[:, :], c_maskA, writes=["maskA"])
        S.dma("sp", A(0)[:, 0:128], c_ident, writes=[("A", 0)])
        S.op("dve", lambda e: e.tensor_copy(identb[:, :], A(0)[:, 0:128]), reads=[("A", 0)], writes=["identb"])
        for j in range(4):
            S.dma("sp", A(1 + j), c_maskD[:, j * 512:(j + 1) * 512], writes=[("A", 1 + j)])
            S.op("dve", lambda e, j=j: e.tensor_copy(maskD[:, j, :], A(1 + j)), reads=[("A", 1 + j)], writes=["maskD"])
        S.op("pool", lambda e: e.memset(onesf[:, :], 1.0), writes=["onesf"])
        S.op("pool", lambda e: e.memset(cT[:, :, :].rearrange("p a b -> p (a b)"), 0.0), writes=[("cT", k) for k in range(NKB)])
        S.op("pool", lambda e: e.memset(mixB[:, :, :].rearrange("p a b -> p (a b)"), 0.0), writes=[("mixB", h) for h in range(8)])
        for wi_ in (1, 2):
            S.op("pool", lambda e, wi_=wi_: e.memset(wbuf[wi_][:, :], 0.0), writes=[("wbuf", wi_)])
        S.op("pool", lambda e: e.memset(qT[:, :, :, :].rearrange("p a b c -> p (a b c)"), 0.0), writes=["qT"])
        S.op("pool", lambda e: e.memset(rmask[:, :], 1.0), writes=["rmask"])
        S.op("pool", lambda e: e.memset(rmask[:, :].rearrange("p (c s) -> p c s", s=64)[:, :, 0:1], 0.0),
             reads=["rmask"], writes=["rmask"])
        S.op("pool", lambda e: e.memset(vaug[:, :, :, :].rearrange("p a b c -> p (a b c)"), 1.0),
             writes=[("vaug", k) for k in range(NKB)])
        for (t, src, nm) in [(n1s, n1T, "n1s"), (n2s, n2T, "n2s"), (gns, gnT, "gns"), (lbls, lbl, "lbls")]:
            S.dma("sp", t[:, :], src, writes=[nm])
        S.dma("sp", qgb[:, :], qg.partition_broadcast(128), writes=["qgb"])
        S.dma("sp", kgb[:, :], kg.partition_broadcast(128), writes=["kgb"])
        S.dma("sp", bfb[:, :], bfx.partition_broadcast(128), writes=["bfb"])
        S.op("dve", lambda e: e.tensor_scalar(qgb[:, :], qgb[:, :], 0.125, None, ALU.mult), reads=["qgb"], writes=["qgb"])
        S.op("dve", lambda e: e.tensor_tensor(lb[:, :], lbls[:, 4:8], lbls[:, 0:4], ALU.subtract), reads=["lbls"], writes=["lb"])
        S.op("act", lambda e: e.activation(lb[:, :], lb[:, :], AF.Exp), reads=["lb"], writes=["lb"])
        S.op("dve", lambda e: e.tensor_scalar(lb[:, :], lb[:, :], 1.0, None, ALU.add), reads=["lb"], writes=["lb"])
        S.op("dve", lambda e: e.reciprocal(lb[:, :], lb[:, :]), reads=["lb"], writes=["lb"])
        S.op("dve", lambda e: e.tensor_scalar(oml[:, :], lb[:, :], -1.0, 1.0, ALU.mult, ALU.add), reads=["lb"], writes=["oml"])

        wtoks = {}

        def convert(src, rows, cols, tokname, stores):
            for r0 in range(0, rows, 128):
                for c0 in range(0, cols, 1024):
                    n = min(1024, cols - c0)
                    i = cnt["cast"]; cnt["cast"] += 1
                    sa = i % 4
                    fst = scrA[:, 2 * sa:2 * sa + 2, :].rearrange("p a b -> p (a b)")
                    bst = wbuf[0][:, sa * 1024:(sa + 1) * 1024]
                    S.dma("sp", fst[:, 0:n], src[r0:r0 + 128, c0:c0 + n], writes=[("A", 2 * sa), ("A", 2 * sa + 1)])
                    eng = ("dve", "act")[i % 2]
                    if eng == "act":
                        fn = lambda e, fst=fst, bst=bst, n=n: e.activation(bst[:, 0:n], fst[:, 0:n], AF.Copy)
                    else:
                        fn = lambda e, fst=fst, bst=bst, n=n: e.tensor_copy(bst[:, 0:n], fst[:, 0:n])
                    S.op(eng, fn, reads=[("A", 2 * sa), ("A", 2 * sa + 1)], writes=[("wst", sa)])
                    for j, (d_ap, s_ap) in enumerate(stores(r0, c0, n, bst)):
                        S.dma("pool", d_ap, s_ap, reads=[("wst", sa)], writes=[(tokname, i, j)])
                        wtoks.setdefault(tokname, []).append((tokname, i, j))

        def st_in(r0, c0, n, bst):
            kc = r0 // 128
            g0 = c0 // 512
            res = [(wib[g0][:, kc * 512:(kc + 1) * 512], bst[:, 0:512])]
            if n == 1024:
                res.append((wib[g0 + 1][:, kc * 512:(kc + 1) * 512], bst[:, 512:1024]))
            else:
                res.append((wif[:, kc * 8:(kc + 1) * 8], bst[:, 512:520]))
            return res

        def st_out(r0, c0, n, bst):
            if r0 < 512:
                h = r0 // 128
                return [(woA[nh_][:, h * 512:(h + 1) * 512], bst[:, nh_ * 512:(nh_ + 1) * 512]) for nh_ in range(2)]
            j = (r0 - 512) // 128
            res = []
            for nh_ in range(2):
                res.append((woB[nh_][:, (2 * j) * 512:(2 * j + 1) * 512], bst[0:64, nh_ * 512:(nh_ + 1) * 512]))
                res.append((woB[nh_][:, (2 * j + 1) * 512:(2 * j + 2) * 512], bst[64:128, nh_ * 512:(nh_ + 1) * 512]))
            return res

        def st_up(r0, c0, n, bst):
            kc = r0 // 128
            g0 = c0 // 512
            return [(wub[g0 + t][:, kc * 512:(kc + 1) * 512], bst[:, t * 512:(t + 1) * 512]) for t in range(2)]

        def st_down(r0, c0, n, bst):
            fg, c = divmod(r0 // 128, 4)
            return [(wdb[fg][:, c * 1024:(c + 1) * 1024], bst[:, 0:1024])]

        if stop == "setup":
            wtoks.update({k: [] for k in ("wib", "wob", "wub", "wdb")})
        else:
            convert(w_in, D, INC, "wib", st_in)
            convert(w_out, D, D, "wob", st_out)
            convert(w_up, D, DFF, "wub", st_up)
            convert(w_down, DFF, D, "wdb", st_down)
        WT = lambda i: [("wbuf", i)] + ([("wst", s) for s in range(4)] if i == 0 else [])

        WPOOLS = {"all": [0, 1, 2], "h": [0, 1], "f": [2]}
        wcnt = {"all": 0, "h": 0, "f": 0}

        def load_w(view_out, view_in, srctok, pool="all"):
            i = WPOOLS[pool][wcnt[pool] % len(WPOOLS[pool])]
            wcnt[pool] += 1
            S.dma("sp", view_out(wbuf[i]), view_in, reads=wtoks[srctok], writes=WT(i))
            return i

        def rstd_small(t, n_parts, cols, inv_n):
            pass

        def fox_cumsum(lfb, TB, kb, tokl):
            p = rotX()
            S.op("pe", lambda e: e.matmul(PS[p][0:TB, 0:8], tri[0:TB, 0:TB], lfb[0:TB, :], start=True, stop=True),
                 reads=["tri", tokl], writes=[("ps", p)])
            S.op("pe", lambda e: e.matmul(PS[p][:, 8:16], onesf[0:TB, :], lfb[0:TB, :], start=True, stop=True),
                 reads=["onesf", tokl], writes=[("ps", p)])
            S.op("dve", lambda e: e.tensor_tensor(cT[0:TB, kb, :], PS[p][0:TB, 0:8], carry[0:TB, :], ALU.add),
                 reads=[("ps", p), "carry"], writes=[("cT", kb)])
            S.op("dve", lambda e: e.tensor_tensor(carry[:, :], PS[p][:, 8:16], carry[:, :], ALU.add),
                 reads=[("ps", p), "carry"], writes=["carry"])

        def kT_transposes(knb_ap, TB, col0, tokk):
            p = rotX()
            for pr in range(4):
                S.op("pe", lambda e, pr=pr: e.transpose(PSB[p][:, pr * 128:pr * 128 + TB], knb_ap[0:TB, pr * 128:(pr + 1) * 128],
                                                      identb[0:TB, 0:TB]),
                     reads=[tokk, "identb"], writes=[("ps", p)])
            S.op("act", lambda e: e.activation(ktst[:, :, col0:col0 + TB],
                                               PSB[p][:, 0:512].rearrange("p (a b) -> p a b", b=128)[:, :, 0:TB], AF.Copy),
                 reads=[("ps", p)], writes=["ktst"])

        def norm_qk(src_ps, p, TB, gain, gtok, out32, out32tok, outbf, outbftok):
            i = cnt["s"] % 2; cnt["s"] += 1
            sq = B(0)
            S.op("act", lambda e: e.activation(sq[0:TB, :], src_ps[0:TB, :], AF.Square), reads=[("ps", p)], writes=[("B", 0)])
            S.op("dve", lambda e: e.tensor_reduce(ss8[i][0:TB, :], sq[0:TB, :].rearrange("p (h d) -> p h d", d=64), AX.X, ALU.add),
                 reads=[("B", 0)], writes=[("ss8", i)])
            S.op("act", lambda e: e.activation(ss8[i][0:TB, :], ss8[i][0:TB, :], AF.Ln, bias=EPS, scale=1.0 / 64),
                 reads=[("ss8", i)], writes=[("ss8", i)])
            S.op("act", lambda e: e.activation(ss8[i][0:TB, :], ss8[i][0:TB, :], AF.Exp, scale=-0.5),
                 reads=[("ss8", i)], writes=[("ss8", i)])
            S.op("dve", lambda e: e.tensor_tensor(sq[0:TB, :].rearrange("p (h d) -> p h d", d=64),
                                                  src_ps[0:TB, :].rearrange("p (h d) -> p h d", d=64),
                                                  ss8[i][0:TB, :].unsqueeze(2).to_broadcast([TB, 8, 64]), ALU.mult),
                 reads=[("ps", p), ("ss8", i)], writes=[("B", 0)])
            dst = out32 if out32 is not None else outbf
            dtok = out32tok if out32 is not None else outbftok
            S.op("dve", lambda e: e.tensor_tensor(dst[0:TB, :].rearrange("p (h d) -> p h d", d=64),
                                                  sq[0:TB, :].rearrange("p (h d) -> p h d", d=64),
                                                  gain[0:TB, :].unsqueeze(1).to_broadcast([TB, 8, 64]), ALU.mult),
                 reads=[("B", 0), gtok], writes=[dtok])
            if out32 is not None:
                S.op("pool", lambda e: e.tensor_copy(outbf[0:TB, :], out32[0:TB, :]), reads=[out32tok], writes=[outbftok])

        def tile(T, x_src, t0, kb0, kT_dram, key0, outs, g, sample):
            y_dst, nk_dst, nv_dst, nl_dst = outs
            TB = min(T, 128)
            nblk = T // TB
            CL = 64 if T >= 64 else T
            nch = T // CL
            tg = ("t", g)
            mb = (g % 2) if isinstance(g, int) else 0
            if SIM_PARITY is not None and not isinstance(g, int):
                SIM_PARITY[0] = 0
            if SIM_PARITY is not None:
                SIM_PARITY[1] = True
            mixA = mixA2[:, mb, :, :]

            for tb in range(nblk):
                i = cnt["x"] % 2; cnt["x"] += 1
                r0 = t0 + tb * TB
                S.dma("sp", xblk[i][0:TB, :], x_src[r0:r0 + TB, :], writes=XT(i))
                S.op("pool", lambda e, i=i: e.memset(ss1[i][:, :], 0.0), reads=[("ss1", i)], writes=[("ss1", i)])
                S.op("act", lambda e, i=i: e.activation(hb[i][0:TB, :], xblk[i][0:TB, :], AF.Square, accum_out=ss1[i][0:TB, :]),
                     reads=XT(i), writes=[("hb", 0), ("ss1", i)])
                S.op("act", lambda e, i=i: e.activation(ss1[i][0:TB, :], ss1[i][0:TB, :], AF.Ln, bias=EPS, scale=1.0 / D),
                     reads=[("ss1", i)], writes=[("ss1", i)])
                S.op("act", lambda e, i=i: e.activation(ss1[i][0:TB, :], ss1[i][0:TB, :], AF.Exp, scale=-0.5),
                     reads=[("ss1", i)], writes=[("ss1", i)])
                S.op("dve", lambda e, i=i: e.tensor_scalar(hb[i][0:TB, :], xblk[i][0:TB, :], ss1[i][0:TB, 0:1], None, ALU.mult),
                     reads=XT(i) + [("ss1", i)], writes=[("hb", 0)])
                p = rot()
                for kc in range(8):
                    S.op("pe", lambda e, i=i, kc=kc, p=p: e.transpose(PSB[p][:, kc * 128:kc * 128 + TB], hb[i][0:TB, kc * 128:(kc + 1) * 128],
                                                                    identb[0:TB, 0:TB]),
                         reads=[("hb", 0), "identb"], writes=[("ps", p)])
                S.op("dve", lambda e, p=p, tb=tb: e.tensor_tensor(hT[:, :, tb * 128:tb * 128 + TB],
                                                                  PSB[p][:, :].rearrange("p (a b) -> p a b", b=128)[:, :, 0:TB],
                                                                  n1s[:, :].unsqueeze(2).to_broadcast([128, 8, TB]), ALU.mult),
                     reads=[("ps", p), "n1s"], writes=["hT"])

            chk("p1a")

            def w_group(c0, ncol, pool="h"):
                if SEQ_P1:
                    pool = "all"
                if ncol == 8:
                    return load_w(lambda w: w[:, 0:64], wif[:, :], "wib", pool)
                return load_w(lambda w: w[:, 0:4096], wib[c0 // 512][:, :], "wib", pool)

            def mm_feat(wi, h, p):
                wv = wbuf[wi][:, 0:4096].rearrange("p (k n) -> p k n", n=512)
                for kc in range(8):
                    S.op("pe", lambda e, kc=kc: e.matmul(PS[p][:, 0:T], wv[:, kc, h * 128:(h + 1) * 128], hT[:, kc, 0:T],
                                                         start=(kc == 0), stop=(kc == 7)),
                         reads=[("wbuf", wi), "hT"], writes=[("ps", p)])

            def mm_tok(wi, tb, p, ncol):
                wv = wbuf[wi][:, 0:8 * ncol].rearrange("p (k n) -> p k n", n=ncol)
                for kc in range(8):
                    S.op("pe", lambda e, kc=kc: e.matmul(PS[p][0:TB, 0:ncol], hT[:, kc, tb * 128:tb * 128 + TB], wv[:, kc, 0:ncol],
                                                         start=(kc == 0), stop=(kc == 7)),
                         reads=[("wbuf", wi), "hT"], writes=[("ps", p)])

            TS = [[(A(4), ("A", 4)), (A(5), ("A", 5)), (A(6), ("A", 6)), (A(7), ("A", 7))],
                  [(xblk[0][:, 0:512], ("xb", 0, 0)), (xblk[0][:, 512:1024], ("xb", 0, 1)),
                   (xblk[1][:, 0:512], ("xb", 1, 0)), (xblk[1][:, 512:1024], ("xb", 1, 1))]]
            def hgrn_pre():
                wi = w_group(512, 512)
                for hp in range(2):
                    hs = [2 * hp, 2 * hp + 1]
                    pp = {}
                    for h in hs:
                        pp[h] = rotH()
                        mm_feat(wi, h, pp[h])

                    def st(fn):
                        for h in hs:
                            (ta, tka), (tb_, tkb), (tc, tkc), (td, tkd) = TS[h % 2]
                            fn(h, pp[h], ta, tka, tb_, tkb, tc, tkc, td, tkd)

                    def s1(h, p, ta, tka, tb_, tkb, tc, tkc, td, tkd):
                        S.op("act", lambda e: e.activation(ta[:, 0:T], PS[p][:, 0:T], AF.Exp, scale=-1.0), reads=[("ps", p)], writes=[tka])
                        S.op("act", lambda e: e.activation(ta[:, 0:T], ta[:, 0:T], AF.Ln, bias=1.0), reads=[tka], writes=[tka])
                        S.op("act", lambda e: e.activation(ta[:, 0:T], ta[:, 0:T], AF.Exp, scale=-1.0), reads=[tka], writes=[tka])
                    st(s1)
                    yield

                    def s2(h, p, ta, tka, tb_, tkb, tc, tkc, td, tkd):
                        S.op("pool", lambda e: e.tensor_scalar(tb_[:, 0:T], ta[:, 0:T], oml[:, h:h + 1], lb[:, h:h + 1], ALU.mult, ALU.add),
                             reads=[tka, "oml", "lb"], writes=[tkb])
                    st(s2)
                    yield

                    def s3(h, p, ta, tka, tb_, tkb, tc, tkc, td, tkd):
                        S.op("act", lambda e: e.activation(tc[:, 0:T], tb_[:, 0:T], AF.Ln), reads=[tkb], writes=[tkc])
                        S.op("pool", lambda e: e.tensor_scalar(ta[:, 0:T], tb_[:, 0:T], -1.0, 1.0, ALU.mult, ALU.add), reads=[tkb], writes=[tka])
                    st(s3)
                    yield

                    def s4(h, p, ta, tka, tb_, tkb, tc, tkc, td, tkd):
                        S.op("dve", lambda e: e.tensor_tensor_scan(td[:, 0:T], rmask[:, 0:T], tc[:, 0:T], 0.0, ALU.mult, ALU.add),
                             reads=[tkc, "rmask"], writes=[tkd])
                        tdv = td[:, 0:T].rearrange("p (c s) -> p c s", s=CL)
                        bref = tdv[:, :, CL // 2:CL // 2 + 1]
                        S.op("dve", lambda e: e.tensor_tensor(tb_[:, 0:T].rearrange("p (c s) -> p c s", s=CL), tdv,
                                                              bref.to_broadcast([128, nch, CL]), ALU.subtract),
                             reads=[tkd], writes=[tkb])
                    st(s4)
                    yield

                    def s5(h, p, ta, tka, tb_, tkb, tc, tkc, td, tkd):
                        E1 = A(h)
                        S.op("act", lambda e: e.activation(E1[:, 0:T], tb_[:, 0:T], AF.Exp), reads=[tkb], writes=[("A", h)])
                        S.op("act", lambda e: e.activation(tc[:, 0:T], tb_[:, 0:T], AF.Exp, scale=-1.0), reads=[tkb], writes=[tkc])
                    st(s5)
                    yield

                    def s6(h, p, ta, tka, tb_, tkb, tc, tkc, td, tkd):
                        tdv = td[:, 0:T].rearrange("p (c s) -> p c s", s=CL)
                        bref = tdv[:, :, CL // 2:CL // 2 + 1]
                        blast = tdv[:, :, CL - 1:CL]
                        S.op("pool", lambda e: e.tensor_tensor(krT[:, h, 0:T], ta[:, 0:T], tc[:, 0:T], ALU.mult),
                             reads=[tka, tkc], writes=[("krT", h)])
                        S.op("act", lambda e: e.activation(sc_e[:, h, 0:nch].unsqueeze(2), bref, AF.Exp), reads=[tkd], writes=[("sc_e", h)])
                        S.op("act", lambda e: e.activation(sc_a[:, h, 0:nch].unsqueeze(2), blast, AF.Exp), reads=[tkd], writes=[("sc_a", h)])
                        S.op("dve", lambda e: e.tensor_tensor(sc_g[:, h, 0:nch].unsqueeze(2), blast, bref, ALU.subtract),
                             reads=[tkd], writes=[("sc_g", h)])
                        S.op("act", lambda e: e.activation(sc_g[:, h, 0:nch], sc_g[:, h, 0:nch], AF.Exp), reads=[("sc_g", h)], writes=[("sc_g", h)])
                    st(s6)
                    yield
                wi = w_group(0, 512)
                for h in range(4):
                    p = rotH()
                    mm_feat(wi, h, p)
                    S.op("dve", lambda e, h=h, p=p: e.tensor_tensor(qrT[:, h, 0:T], PS[p][:, 0:T], A(h)[:, 0:T], ALU.mult),
                         reads=[("ps", p), ("A", h)], writes=[("qrT", h)])
                    yield
                wi = w_group(1024, 512)
                for tb in range(nblk):
                    p = rotH()
                    mm_tok(wi, tb, p, 512)
                    S.op("act", lambda e, tb=tb, p=p: e.activation(vtok[0:TB, tb, :], PS[p][0:TB, :], AF.Copy),
                         reads=[("ps", p)], writes=[("vtok", tb)])
                    yield
                for tb in range(nblk):
                    p = rotH()
                    for h in range(4):
                        S.op("pe", lambda e, h=h, p=p, tb=tb: e.transpose(PSB[p][0:TB, h * 128:(h + 1) * 128], krT[:, h, tb * 128:tb * 128 + TB],
                                                                        identb[:, :]),
                             reads=[("krT", h), "identb"], writes=[("ps", p)])
                    S.op("act", lambda e, p=p, tb=tb: e.activation(krTok[0:TB, tb, :], PSB[p][0:TB, 0:512], AF.Copy),
                         reads=[("ps", p)], writes=[("krTok", tb)])
                    p2 = rotH()
                    for h in range(4):
                        S.op("pe", lambda e, h=h, p2=p2, tb=tb: e.matmul(PS[p2][0:TB, h * 128:h * 128 + TB], krT[:, h, tb * 128:tb * 128 + TB],
                                                                       qrT[:, h, tb * 128:tb * 128 + TB], start=True, stop=True),
                             reads=[("krT", h), ("qrT", h)], writes=[("ps", p2)])
                    S.op("dve", lambda e, p2=p2, tb=tb: e.tensor_tensor(AmT[0:TB, tb, :, 0:TB],
                                                                        PS[p2][0:TB, :].rearrange("p (h c) -> p h c", c=128)[:, :, 0:TB],
                                                                        maskA[0:TB, 0:TB].unsqueeze(1).to_broadcast([TB, 4, TB]), ALU.mult),
                         reads=[("ps", p2), "maskA"], writes=[("AmT", tb)])
                    yield

            chk("p1b")
            def recurrence():
                for c in range(nch):
                    tb = (c * CL) // 128
                    off = (c * CL) % 128
                    S.op("pool", lambda e, c=c: e.tensor_tensor(Sbf[:, :, :], S32[:, :, :],
                                                                sc_e[:, :, c:c + 1].to_broadcast([128, 4, 128]), ALU.mult),
                         reads=["S32"] + [("sc_e", h) for h in range(4)], writes=["Sbf"])
                    for h in range(4):
                        S.op("pe", lambda e, h=h, c=c: e.matmul(PS[h][:, c * CL:(c + 1) * CL], Sbf[:, h, :], qrT[:, h, c * CL:(c + 1) * CL],
                                                                start=True, stop=False),
                             reads=["Sbf", ("qrT", h)], writes=[("ps", h)])
                        S.op("pe", lambda e, h=h, c=c, tb=tb, off=off: e.matmul(PS[h][:, c * CL:(c + 1) * CL],
                                                                               vtok[off:off + CL, tb, h * 128:(h + 1) * 128],
                                                                               AmT[off:off + CL, tb, h, off:off + CL], start=False, stop=True),
                             reads=[("vtok", tb), ("AmT", tb)], writes=[("ps", h)])
                    pu = rotH()
                    for h in range(4):
                        S.op("pe", lambda e, h=h, pu=pu, tb=tb, off=off: e.matmul(PS[pu][:, h * 128:(h + 1) * 128],
                                                                                 krTok[off:off + CL, tb, h * 128:(h + 1) * 128],
                                                                                 vtok[off:off + CL, tb, h * 128:(h + 1) * 128], start=True, stop=True),
                             reads=[("krTok", tb), ("vtok", tb)], writes=[("ps", pu)])
                    S.op("dve", lambda e, pu=pu, c=c: e.tensor_tensor(utmp[:, :, :], PS[pu][:, :].rearrange("p (h v) -> p h v", v=128),
                                                                      sc_g[:, :, c:c + 1].to_broadcast([128, 4, 128]), ALU.mult),
                         reads=[("ps", pu)] + [("sc_g", h) for h in range(4)], writes=["utmp"])
                    S.op("dve", lambda e, c=c: e.tensor_tensor(S32[:, :, :], S32[:, :, :], sc_a[:, :, c:c + 1].to_broadcast([128, 4, 128]), ALU.mult),
                         reads=["S32"] + [("sc_a", h) for h in range(4)], writes=["S32"])
                    S.op("dve", lambda e: e.tensor_tensor(S32[:, :, :], S32[:, :, :], utmp[:, :, :], ALU.add),
                         reads=["S32", "utmp"], writes=["S32"])
                    yield

            def fox_proj():
                kb_of = lambda tb: kb0 + tb
                wi = w_group(2048, 512, "f")
                for tb in range(nblk):
                    p = rotX()
                    mm_tok(wi, tb, p, 512)
                    j = cnt["p"] % 2; cnt["p"] += 1
                    qn = scrB[:, 5 + j, 0:256].bitcast(BF16)
                    norm_qk(PS[p], p, TB, qgb, "qgb", None, None, qn, ("Bq", 5 + j))
                    p2 = rotX()
                    for pr in range(4):
                        S.op("pe", lambda e, pr=pr, p2=p2, qn=qn: e.transpose(PSB[p2][:, pr * 128:pr * 128 + TB], qn[0:TB, pr * 128:(pr + 1) * 128],
                                                                            identb[0:TB, 0:TB]),
                             reads=[("Bq", 5 + j), "identb"], writes=[("ps", p2)])
                    S.op("act", lambda e, p2=p2, tb=tb: e.activation(qT[0:64, 0, :, tb * 128:tb * 128 + TB],
                                                                     PSB[p2][0:64, 0:512].rearrange("p (a b) -> p a b", b=128)[:, :, 0:TB], AF.Copy),
                         reads=[("ps", p2)], writes=["qT"])
                    S.op("pool" if False else "dve", lambda e, p2=p2, tb=tb: e.tensor_copy(qT[64:128, 1, :, tb * 128:tb * 128 + TB],
                                                                     PSB[p2][64:128, 0:512].rearrange("p (a b) -> p a b", b=128)[:, :, 0:TB]),
                         reads=[("ps", p2)], writes=["qT"])
                    yield
                wi = w_group(2560, 512, "f")
                for tb in range(nblk):
                    p = rotX()
                    mm_tok(wi, tb, p, 512)
                    j = cnt["p"] % 2; cnt["p"] += 1
                    kn32 = B(1 + j)
                    knb = scrB[:, 5 + j, 256:512].bitcast(BF16)
                    norm_qk(PS[p], p, TB, kgb, "kgb", kn32, ("B", 1 + j), knb, ("Bk", 5 + j))
                    r0 = t0 + tb * TB
                    S.dma("pool", nk_dst[r0:r0 + TB, :], kn32[0:TB, :], reads=[("B", 1 + j)], is_output=True)
                    kT_transposes(knb, TB, tb * 128, ("Bk", 5 + j))
                    yield
                if stop == "foxn3a":
                    raise _Stop()
                S.dma("sp" if stop == "foxn3b" else "pool", kT_dram[:, :, key0:key0 + T].rearrange("q p k -> p q k"), ktst[:, :, 0:T], reads=["ktst"],
                      writes=[("kTd", sample, key0 // 512)])
                if stop == "foxn3c":
                    raise _Stop()
                wi = w_group(3072, 512, "f")
                for tb in range(nblk):
                    p = rotX()
                    mm_tok(wi, tb, p, 512)
                    j = cnt["p"] % 2; cnt["p"] += 1
                    v32 = B(3 + j)
                    S.op("act", lambda e, p=p, v32=v32: e.activation(v32[0:TB, :], PS[p][0:TB, :], AF.Copy), reads=[("ps", p)], writes=[("B", 3 + j)])
                    r0 = t0 + tb * TB
                    S.dma("pool", nv_dst[r0:r0 + TB, :], v32[0:TB, :], reads=[("B", 3 + j)], is_output=True)
                    S.op("dve", lambda e, v32=v32, tb=tb: e.tensor_copy(vaug[0:TB, kb_of(tb), :, 0:64],
                                                                        v32[0:TB, :].rearrange("p (h d) -> p h d", d=64)),
                         reads=[("B", 3 + j)], writes=[("vaug", kb_of(tb))])
                    yield
                wi = w_group(3584, 8, "f")
                for tb in range(nblk):
                    p = rotX()
                    mm_tok(wi, tb, p, 8)
                    j = cnt["p"] % 2; cnt["p"] += 1
                    S.op("dve", lambda e, p=p: e.tensor_tensor(lft[0:TB, :], PS[p][0:TB, 0:8], bfb[0:TB, :], ALU.add),
                         reads=[("ps", p), "bfb"], writes=["lft"])
                    S.op("act", lambda e: e.activation(lft[0:TB, :], lft[0:TB, :], AF.Exp, scale=-1.0), reads=["lft"], writes=["lft"])
                    S.op("act", lambda e: e.activation(lft[0:TB, :], lft[0:TB, :], AF.Ln, bias=1.0), reads=["lft"], writes=["lft"])
                    S.op("dve", lambda e, j=j: e.tensor_scalar(lf[j][0:TB, :], lft[0:TB, :], -1.0, None, ALU.mult),
                         reads=["lft"], writes=[("lf", j)])
                    r0 = t0 + tb * TB
                    S.dma("pool", nl_dst[r0:r0 + TB, :], lf[j][0:TB, :], reads=[("lf", j)], is_output=True)
                    fox_cumsum(lf[j], TB, kb_of(tb), ("lf", j))
                    if (not sample) and tb == 1:
                        S.op("pool", lambda e: e.tensor_copy(cref[:, :], carry[:, :]), reads=["carry"], writes=["cref"])
                    yield

            gh, gr, gf = hgrn_pre(), recurrence(), fox_proj()
            if stop == "p1b":
                for _ in gh:
                    pass
                raise _Stop()
            fox_left = [4 * nblk]

            def fox_step():
                if fox_left[0] > 0:
                    try:
                        next(gf)
                    except StopIteration:
                        fox_left[0] = 0
                        return
                    fox_left[0] -= 1

            nstep = 0
            if SEQ_P1:
                for _ in gh:
                    pass
                for _ in gr:
                    pass
            for _ in gh:
                nstep += 1
                if nstep % 3 == 0 and fox_left[0] > nch:
                    fox_step()
            for _ in gr:
                fox_step()
            for _ in gf:
                pass
            chk("p1c")
            if stop in ("rec_only", "fox_only") or (stop is not None and stop.startswith("foxn")):
                raise _Stop()
            wi = w_group(1536, 512)
            for hp in range(2):
                hs = [2 * hp, 2 * hp + 1]
                pgs = {}

                def st(fn):
                    for h in hs:
                        (ta, tka), (tb_, tkb), (tc, tkc), (td, tkd) = TS[h % 2]
                        fn(h, ta, tka, tb_, tkb, tc, tkc, td, tkd)

                def o1(h, ta, tka, tb_, tkb, tc, tkc, td, tkd):
                    S.op("act", lambda e: e.activation(ta[:, 0:T], PS[h][:, 0:T], AF.Square), reads=[("ps", h)], writes=[tka])
                    pn = rot()
                    S.op("pe", lambda e: e.matmul(PS[pn][:, 0:T], onesf[:, :], ta[:, 0:T], start=True, stop=True),
                         reads=["onesf", tka], writes=[("ps", pn)])
                    S.op("act", lambda e: e.activation(tb_[:, 0:T], PS[pn][:, 0:T], AF.Ln, bias=EPS, scale=1.0 / 128),
                         reads=[("ps", pn)], writes=[tkb])
                    S.op("act", lambda e: e.activation(tb_[:, 0:T], tb_[:, 0:T], AF.Exp, scale=-0.5), reads=[tkb], writes=[tkb])
                    pg = rot()
                    pgs[h] = pg
                    mm_feat(wi, h, pg)
                    S.op("act", lambda e: e.activation(tc[:, 0:T], PS[pg][:, 0:T], AF.Exp, scale=-1.0), reads=[("ps", pg)], writes=[tkc])
                    S.op("act", lambda e: e.activation(tc[:, 0:T], tc[:, 0:T], AF.Ln, bias=1.0), reads=[tkc], writes=[tkc])
                    S.op("act", lambda e: e.activation(tc[:, 0:T], tc[:, 0:T], AF.Exp, scale=-1.0), reads=[tkc], writes=[tkc])
                st(o1)

                def o2(h, ta, tka, tb_, tkb, tc, tkc, td, tkd):
                    pg = pgs[h]
                    S.op("dve", lambda e: e.tensor_tensor(tc[:, 0:T], PS[pg][:, 0:T], tc[:, 0:T], ALU.mult),
                         reads=[("ps", pg), tkc], writes=[tkc])
                    S.op("dve", lambda e: e.tensor_tensor(td[:, 0:T], PS[h][:, 0:T], tb_[:, 0:T], ALU.mult),
                         reads=[("ps", h), tkb], writes=[tkd])
                    S.op("dve", lambda e: e.scalar_tensor_tensor(mixA[:, h, 0:T], td[:, 0:T], gns[:, h:h + 1], tc[:, 0:T], ALU.mult, ALU.mult),
                         reads=[tkd, tkc, "gns"], writes=[("mixA", mb, h)])
                st(o2)

            if DBG == "dumpA" and sample:
                for h in range(4):
                    S.op("act", lambda e, h=h: e.activation(A(4)[:, h * 32:(h + 1) * 32], mixA[:, h, 0:32], AF.Copy),
                         reads=[("mixA", mb, h)], writes=[("A", 4)])
                S.dma("pool", y_dst.rearrange("t (a b) -> (t a) b", b=256)[:, 0:128], A(4)[:, 0:128], reads=[("A", 4)], is_output=True)
                raise _Stop()
            chk("hout")
            if SIM_PARITY is not None:
                SIM_PARITY[1] = False
            yield "p1"
            nkb = kb0 + nblk
            TB5 = [("B", 5), ("Bq", 5), ("Bk", 5)]
            TB6 = [("B", 6), ("Bq", 6), ("Bk", 6)]
            biasg = scrB[:, 6, :].rearrange("p (k h) -> p k h", h=8)
            S.op("dve", lambda e: e.tensor_tensor(biasg[:, 0:nkb, :], cref[:, :].unsqueeze(1).to_broadcast([128, nkb, 8]),
                                                  cT[:, 0:nkb, :], ALU.subtract),
                 reads=["cref"] + [("cT", k) for k in range(nkb)], writes=TB6)
            nkt = (nkb + 3) // 4
            rden = B(4); rdb = B(5)

            LOOK = 2
            loads = [(pr, kt) for pr in range(4) for kt in range(nkt)]
            ktbuf = {}
            nl = [0]

            def ensure_loaded(idx):
                while nl[0] <= min(idx, len(loads) - 1):
                    pr_, kt_ = loads[nl[0]]
                    bi = cnt["kt"] % NKTB; cnt["kt"] += 1
                    kcols = min(512, (nkb - kt_ * 4) * 128)
                    if sample and kt_ == nkt - 1:
                        kcols = T
                    S.dma("sp", ktb[bi][:, 0:kcols], kT_dram[pr_, :, kt_ * 512:kt_ * 512 + kcols],
                          reads=[("kTd", sample, kt_)], writes=[("ktb", bi)])
                    ktbuf[(pr_, kt_)] = bi
                    nl[0] += 1

            def fin(pr):
                accb = (0, 1)
                for e2 in range(2):
                    h = 2 * pr + e2
                    ab = accb[e2]
                    slot = 4 + e2
                    stok = [("B", 4)] if e2 == 0 else TB5
                    rden_ = B(slot); rdb_ = B(slot)
                    S.op("dve", lambda e: e.reciprocal(rden_[64:65, 0:T], PS[ab][64:65, 0:T]), reads=[("ps", ab)], writes=stok)
                    pb_ = rotA()
                    S.op("pe", lambda e: e.matmul(PS[pb_][0:64, 0:T], onesf[64:65, 0:64], rden_[64:65, 0:T], start=True, stop=True),
                         reads=["onesf"] + stok, writes=[("ps", pb_)])
                    S.op("dve", lambda e: e.tensor_copy(rdb_[0:64, 0:T], PS[pb_][0:64, 0:T]), reads=[("ps", pb_)], writes=stok)
                    S.op("dve", lambda e: e.tensor_tensor(mixB[0:64, h, 0:T], PS[ab][0:64, 0:T], rdb_[0:64, 0:T], ALU.mult),
                         reads=[("ps", ab)] + stok, writes=[("mixB", h)])

            pending = None
            for pr in range(4):
                accb = (0, 1)
                units = []
                for kt in range(nkt):
                    for kbl in range(4):
                        kb = kt * 4 + kbl
                        if kb >= nkb:
                            break
                        for e2 in range(2):
                            units.append((kt, kbl, kb, e2))

                def emit_qk(u):
                    kt, kbl, kb, e2 = u
                    li = loads.index((pr, kt))
                    ensure_loaded(li + 1)
                    bi = ktbuf[(pr, kt)]
                    nkeys = TB if kb >= kb0 else 128
                    diag = kb - kb0 if kb >= kb0 else -1
                    h = 2 * pr + e2
                    ps_ = rotA()
                    c0 = 128 * diag if (diag > 0 and T == 512) else 0
                    S.op("pe", lambda e: e.matmul(PS[ps_][0:nkeys, c0:T], ktb[bi][:, kbl * 128:kbl * 128 + nkeys],
                                                  qT[:, e2, pr, c0:T], start=True, stop=True),
                         reads=[("ktb", bi), "qT"], writes=[("ps", ps_)])
                    pi = cnt["p"] % 3; cnt["p"] += 1
                    S.op("act", lambda e: e.activation(pT[pi][0:nkeys, c0:T], PS[ps_][0:nkeys, c0:T], AF.Exp,
                                                       bias=biasg[0:nkeys, kb, h:h + 1], scale=1.0),
                         reads=[("ps", ps_)] + TB6, writes=[("pT", pi)])
                    if diag >= 0:
                        S.op("dve", lambda e: e.tensor_tensor(pT[pi][0:nkeys, c0:T], pT[pi][0:nkeys, c0:T], maskD[0:nkeys, diag, c0:T], ALU.mult),
                             reads=[("pT", pi), "maskD"], writes=[("pT", pi)])
                    return (pi, nkeys, c0)

                def emit_pv(u, info):
                    kt, kbl, kb, e2 = u
                    pi, nkeys, c0 = info
                    h = 2 * pr + e2
                    last = (kb == nkb - 1)
                    ab = accb[e2]
                    S.op("pe", lambda e: e.matmul(PS[ab][0:65, c0:T], vaug[0:nkeys, kb, h, :], pT[pi][0:nkeys, c0:T],
                                                  start=(kb == 0), stop=last),
                         reads=[("vaug", kb), ("pT", pi)], writes=[("ps", ab)])

                infos = []
                for i, u in enumerate(units):
                    infos.append(emit_qk(u))
                    if i >= LOOK:
                        emit_pv(units[i - LOOK], infos[i - LOOK])
                for i in range(max(0, len(units) - LOOK), len(units)):
                    emit_pv(units[i], infos[i])
                fin(pr)

            if DBG == "dumpB" and sample:
                for h in range(8):
                    S.op("act", lambda e, h=h: e.activation(A(4)[0:64, h * 32:(h + 1) * 32], mixB[0:64, h, 0:32], AF.Copy),
                         reads=[("mixB", h)], writes=[("A", 4)])
                S.dma("pool", y_dst.rearrange("t (a b) -> (t a) b", b=256)[0:64, :], A(4)[0:64, 0:256], reads=[("A", 4)], is_output=True)
                raise _Stop()
            chk("attn")
            yield "att"
            x2 = lambda tb: scrA[:, 2 * tb:2 * tb + 2, :].rearrange("p a b -> p (a b)")
            x2t = lambda tb: [("A", 2 * tb), ("A", 2 * tb + 1)]
            for tb in range(nblk):
                r0 = t0 + tb * TB
                S.dma("sp", x2(tb)[0:TB, :], x_src[r0:r0 + TB, :], writes=x2t(tb))
            for nh_ in range(2):
                wa = load_w(lambda w: w[:, 0:2048], woA[nh_][:, :], "wob")
                wb = load_w(lambda w: w[0:64, 0:4096], woB[nh_][:, :], "wob")
                wav = wbuf[wa][:, 0:2048].rearrange("p (k n) -> p k n", n=512)
                wbv = wbuf[wb][:, 0:4096].rearrange("p (k n) -> p k n", n=512)
                for tb in range(nblk):
                    p = rotF()
                    hsA = [] if DBG in ("noA", "noAB") else list(range(4))
                    hsB = [] if DBG in ("noB", "noAB") else list(range(8))
                    if DBG == "noAB":
                        continue
                    allh = [("A", h) for h in hsA] + [("B", h) for h in hsB]
                    for ii, (kind, h) in enumerate(allh):
                        if kind == "A":
                            S.op("pe", lambda e, h=h, p=p, tb=tb, ii=ii: e.matmul(PS[p][0:TB, :], mixA[:, h, tb * 128:tb * 128 + TB], wav[:, h, :],
                                                                                 start=(ii == 0), stop=(ii == len(allh) - 1)),
                                 reads=[("mixA", mb, h), ("wbuf", wa)], writes=[("ps", p)])
                        else:
                            S.op("pe", lambda e, h=h, p=p, tb=tb, ii=ii: e.matmul(PS[p][0:TB, :], mixB[:, h, tb * 128:tb * 128 + TB], wbv[:, h, :],
                                                                                 start=(ii == 0), stop=(ii == len(allh) - 1)),
                                 reads=[("mixB", h), ("wbuf", wb)], writes=[("ps", p)])
                    S.op("dve", lambda e, p=p, tb=tb, nh_=nh_: e.tensor_tensor(x2(tb)[0:TB, nh_ * 512:(nh_ + 1) * 512], PS[p][0:TB, :],
                                                                                x2(tb)[0:TB, nh_ * 512:(nh_ + 1) * 512], ALU.add),
                         reads=[("ps", p)] + x2t(tb), writes=x2t(tb))
            for tb in range(nblk):
                i = cnt["x"] % 2; cnt["x"] += 1
                S.op("pool", lambda e, i=i: e.memset(ss1[i][:, :], 0.0), reads=[("ss1", i)], writes=[("ss1", i)])
                S.op("act", lambda e, i=i, tb=tb: e.activation(hb[i][0:TB, :], x2(tb)[0:TB, :], AF.Square, accum_out=ss1[i][0:TB, :]),
                     reads=x2t(tb), writes=[("hb", 0), ("ss1", i)])
                S.op("act", lambda e, i=i: e.activation(ss1[i][0:TB, :], ss1[i][0:TB, :], AF.Ln, bias=EPS, scale=1.0 / D),
                     reads=[("ss1", i)], writes=[("ss1", i)])
                S.op("act", lambda e, i=i: e.activation(ss1[i][0:TB, :], ss1[i][0:TB, :], AF.Exp, scale=-0.5),
                     reads=[("ss1", i)], writes=[("ss1", i)])
                S.op("dve", lambda e, i=i, tb=tb: e.tensor_scalar(hb[i][0:TB, :], x2(tb)[0:TB, :], ss1[i][0:TB, 0:1], None, ALU.mult),
                     reads=x2t(tb) + [("ss1", i)], writes=[("hb", 0)])
                p = rotF()
                for kc in range(8):
                    S.op("pe", lambda e, i=i, kc=kc, p=p: e.transpose(PSB[p][:, kc * 128:kc * 128 + TB], hb[i][0:TB, kc * 128:(kc + 1) * 128],
                                                                    identb[0:TB, 0:TB]),
                         reads=[("hb", 0), "identb"], writes=[("ps", p)])
                S.op("dve", lambda e, p=p, tb=tb: e.tensor_tensor(hT[:, :, tb * 128:tb * 128 + TB],
                                                                  PSB[p][:, :].rearrange("p (a b) -> p a b", b=128)[:, :, 0:TB],
                                                                  n2s[:, :].unsqueeze(2).to_broadcast([128, 8, TB]), ALU.mult),
                     reads=[("ps", p), "n2s"], writes=["hT"])
            aT = [scrB[:, 0:2, :].rearrange("p a b -> p (a b)").bitcast(BF16).rearrange("p (c t) -> p c t", t=512),
                  scrB[:, 2:4, :].rearrange("p a b -> p (a b)").bitcast(BF16).rearrange("p (c t) -> p c t", t=512)]
            aTt = [[("B", 0), ("B", 1)], [("B", 2), ("B", 3)]]

            def ffn_up(fg):
                wu = load_w(lambda w: w[:, 0:4096], wub[fg][:, :], "wub")
                wv = wbuf[wu][:, 0:4096].rearrange("p (k n) -> p k n", n=512)
                a = fg % 2
                for fc in range(4):
                    p = rotF()
                    for kc in range(8):
                        S.op("pe", lambda e, kc=kc, p=p, fc=fc: e.matmul(PS[p][:, 0:T], wv[:, kc, fc * 128:(fc + 1) * 128], hT[:, kc, 0:T],
                                                                        start=(kc == 0), stop=(kc == 7)),
                             reads=[("wbuf", wu), "hT"], writes=[("ps", p)])
                    S.op("dve", lambda e, p=p, fc=fc, a=a: e.tensor_scalar(aT[a][:, fc, 0:T], PS[p][:, 0:T], 0.0, None, ALU.max),
                         reads=[("ps", p)], writes=aTt[a])
                    S.op("pool", lambda e, fc=fc, a=a: e.tensor_tensor(aT[a][:, fc, 0:T], aT[a][:, fc, 0:T], aT[a][:, fc, 0:T], ALU.mult),
                         reads=aTt[a], writes=aTt[a])

            def ffn_down(fg):
                wd = load_w(lambda w: w[:, 0:4096], wdb[fg][:, :], "wdb")
                wv = wbuf[wd][:, 0:4096].rearrange("p (c n) -> p c n", n=1024)
                a = fg % 2
                for tb in range(nblk):
                    for nh_ in range(2):
                        p = rotF()
                        for fc in range(4):
                            S.op("pe", lambda e, fc=fc, p=p, tb=tb, nh_=nh_: e.matmul(PS[p][0:TB, :], aT[a][:, fc, tb * 128:tb * 128 + TB],
                                                                                     wv[:, fc, nh_ * 512:(nh_ + 1) * 512],
                                                                                     start=(fc == 0), stop=(fc == 3)),
                                 reads=aTt[a] + [("wbuf", wd)], writes=[("ps", p)])
                        S.op("dve", lambda e, p=p, tb=tb, nh_=nh_: e.tensor_tensor(x2(tb)[0:TB, nh_ * 512:(nh_ + 1) * 512], PS[p][0:TB, :],
                                                                                    x2(tb)[0:TB, nh_ * 512:(nh_ + 1) * 512], ALU.add),
                             reads=[("ps", p)] + x2t(tb), writes=x2t(tb))

            chk("wout")
            yield "wout"
            ffn_up(0)
            for fg in range(8):
                if fg + 1 < 8:
                    ffn_up(fg + 1)
                ffn_down(fg)
            for tb in range(nblk):
                r0 = t0 + tb * TB
                S.dma("pool", y_dst[r0:r0 + TB, :], x2(tb)[0:TB, :], reads=x2t(tb), is_output=True)

        def main_seq():
            chk("conv")
            if with_sample:
                S.op("pool", lambda e: e.memset(carry[:, :], 0.0), writes=["carry"])
                S.dma("sp", S32[:, :, :], sth.rearrange("h k v -> k h v"), writes=["S32"])
                for kb in range(8):
                    j = kb % 2
                    kn32 = B(1 + j); knb = scrB[:, 5 + j, 256:512].bitcast(BF16); v32 = B(3 + j)
                    S.dma("sp", kn32[:, :], ck[kb * 128:(kb + 1) * 128, :], writes=[("B", 1 + j)])
                    S.op("pool", lambda e, kn32=kn32, knb=knb: e.tensor_copy(knb[:, :], kn32[:, :]), reads=[("B", 1 + j)], writes=[("Bk", 5 + j)])
                    kT_transposes(knb, 128, (kb % 4) * 128, ("Bk", 5 + j))
                    if kb % 4 == 3:
                        k0 = (kb // 4) * 512
                        S.dma("pool", kTs[:, :, k0:k0 + 512].rearrange("q p k -> p q k"), ktst[:, :, :], reads=["ktst"],
                              writes=[("kTd", True, kb // 4)])
                    S.dma("sp", v32[:, :], cv[kb * 128:(kb + 1) * 128, :], writes=[("B", 3 + j)])
                    S.op("dve", lambda e, kb=kb, v32=v32: e.tensor_copy(vaug[:, kb, :, 0:64], v32[:, :].rearrange("p (h d) -> p h d", d=64)),
                         reads=[("B", 3 + j)], writes=[("vaug", kb)])
                    S.dma("sp", lf[j][:, :], cl[kb * 128:(kb + 1) * 128, :], writes=[("lf", j)])
                    fox_cumsum(lf[j], 128, kb, ("lf", j))
                S.op("pool", lambda e: e.tensor_copy(cref[:, :], carry[:, :]), reads=["carry"], writes=["cref"])
                chk("sprep")
                for _ in tile(32, xs, 0, 8, kTs, 1024, (ys, nks, nvs, nls), "s", True):
                    pass
                S.dma("pool", nhs.rearrange("h k v -> k h v"), S32[:, :, :], reads=["S32"], is_output=True)
                chk("sample")
            S.op("pool", lambda e: e.memset(carry[:, :], 0.0), reads=["carry"], writes=["carry"])
            S.op("pool", lambda e: e.memset(S32[:, :, :].rearrange("p a b -> p (a b)"), 0.0), reads=["S32"], writes=["S32"])
            gens = [tile(512, xp, g * 512, 4 * g, kTp, g * 512, (yp, nk, nv, nl), g, False) for g in range(NT)]

            def fin_gen(gen):
                for _ in gen:
                    pass

            if not PIPE:
                for gen in gens:
                    fin_gen(gen)
            else:
                def nx(i, fin_=False):
                    if SIM_PARITY is not None:
                        SIM_PARITY[0] = i % 2
                    if fin_:
                        fin_gen(gens[i])
                    else:
                        next(gens[i])

                nx(0)
                nx(0)
                for g in range(1, NT):
                    nx(g)
                    nx(g - 1)
                    nx(g)
                    nx(g - 1, True)
                nx(NT - 1, True)
            S.dma("pool", nh.rearrange("h k v -> k h v"), S32[:, :, :], reads=["S32"], is_output=True)

        try:
            main_seq()
        except _Stop:
            pass
        S.finish()
        S.emit(block)
    return nc


def host_consts():
    ident = np.eye(128, dtype=np.float32)
    ii = np.arange(128)
    tri = (ii[:, None] <= ii[None, :]).astype(np.float32)
    maskA = ((ii[:, None] <= ii[None, :]) & ((ii[:, None] // 64) == (ii[None, :] // 64))).astype(np.float32)
    q = np.arange(512)
    maskD = np.concatenate([(q[None, :] >= (128 * j + ii[:, None])).astype(np.float32) for j in range(4)], axis=1)
    return ident, tri, maskA, maskD


def make_in_maps(inp, n_cores, NT):
    SEQ = NT * 512
    ident, tri, maskA, maskD = host_consts()
    f = lambda a: np.ascontiguousarray(np.asarray(a, dtype=np.float32))
    lbl = f(inp["hgrn_lb_logits"])
    lblT = np.concatenate([lbl[0].reshape(4, 128).T, lbl[1].reshape(4, 128).T], axis=1)
    shared = {
        "w_in": f(inp["w_in"][0]), "w_out": f(inp["w_out"][0]), "w_up": f(inp["w_up"][0]), "w_down": f(inp["w_down"][0]),
        "n1T": f(np.asarray(inp["norm1"][0]).reshape(8, 128).T), "n2T": f(np.asarray(inp["norm2"][0]).reshape(8, 128).T),
        "gnT": f(np.asarray(inp["hgrn_out_norm"][0]).reshape(4, 128).T), "lbl": f(lblT),
        "qg": f(np.asarray(inp["q_norm_gain"][0]).reshape(1, 64)), "kg": f(np.asarray(inp["k_norm_gain"][0]).reshape(1, 64)),
        "bfx": f(np.asarray(inp["b_fox_f"][0]).reshape(1, 8)),
        "c_ident": ident, "c_tri": tri, "c_maskA": maskA, "c_maskD": maskD,
    }
    maps = []
    for c in range(n_cores):
        m = dict(shared)
        m["xp"] = f(inp["x_prompt"][c][:SEQ])
        m["xs"] = f(inp["x_sample"][c])
        m["ck"] = f(np.asarray(inp["cache_fox_k"][0, c]).reshape(1024, 512))
        m["cv"] = f(np.asarray(inp["cache_fox_v"][0, c]).reshape(1024, 512))
        m["cl"] = f(inp["cache_fox_logf"][0, c])
        m["sth"] = f(inp["state_hgrn"][0, c])
        maps.append(m)
    return maps


def run(inp, n_cores=8, NT=16, stop=None):
    nc = build_program(NT, stop=stop)
    maps = make_in_maps(inp, n_cores, NT)
    res = run_bass_kernel_spmd(nc, maps, core_ids=list(range(n_cores)))
    R = res.results
    SEQ = NT * 512
    st = lambda k: np.stack([np.asarray(R[c][k], dtype=np.float32) for c in range(n_cores)], axis=0)
    y_p = st("yp"); y_s = st("ys")
    nk_ = st("nk").reshape(1, n_cores, SEQ, 8, 64); nv_ = st("nv").reshape(1, n_cores, SEQ, 8, 64)
    nl_ = st("nl").reshape(1, n_cores, SEQ, 8); nh_ = st("nh").reshape(1, n_cores, 4, 128, 128)
    nks_ = st("nks").reshape(1, n_cores, 32, 8, 64); nvs_ = st("nvs").reshape(1, n_cores, 32, 8, 64)
    nls_ = st("nls").reshape(1, n_cores, 32, 8); nhs_ = st("nhs").reshape(1, n_cores, 4, 128, 128)
    return (y_p, y_s, nk_, nv_, nl_, nh_, nks_, nvs_, nls_, nhs_)


def kernel(**inputs):
    return run(inputs, n_cores=8, NT=16)
```

```python
import numpy as np
from contextlib import ExitStack
import concourse.bass as bass
import concourse.mybir as mybir
from concourse.bass_utils import run_bass_kernel_spmd

F32 = mybir.dt.float32
BF16 = mybir.dt.bfloat16
AF = mybir.ActivationFunctionType
ALU = mybir.AluOpType
AX = mybir.AxisListType

SEM_CAP = 12000
N_DMA_SEMS = 40
EPS = 1e-6
DBG = None
REORDER = True
BANKS = {"rot": (4, 5, 6, 7), "H": (4, 5), "X": (6, 7), "A": (4, 5, 6), "F": (2, 3, 7)}
ATT_LOOK = 2
SIM_PARITY = None
CP_PRIO = True
SEQ_P1 = False
COST_SCALE = {}
PIPE = True
DBG2 = None
D = 1024
INC = 3592
DFF = 4096


import types


def _freeze(fn):
    if fn.__closure__ is None:
        return fn
    cells = []
    for c in fn.__closure__:
        try:
            cells.append(types.CellType(c.cell_contents))
        except ValueError:
            cells.append(c)
    return types.FunctionType(fn.__code__, fn.__globals__, fn.__name__, fn.__defaults__, tuple(cells))


class Op:
    __slots__ = ("id", "eng", "fn", "is_dma", "deps", "dur", "lat", "is_output", "sem", "val", "waits", "inc")


class Tok:
    __slots__ = ("w", "r")

    def __init__(self):
        self.w = None
        self.r = []


class _Fake:
    def __init__(self):
        self.rec = None

    def __getattr__(self, name):
        def f(*a, **k):
            self.rec = (name, a, k)
            return self
        return f


def _prod(xs):
    r = 1
    for x in xs:
        r *= int(x)
    return r


def _estimate(eng, fn):
    fk = _Fake()
    try:
        fn(fk)
        name, a, k = fk.rec
        out = k.get("out", a[0] if a else None)
        shape = tuple(out.shape)
        free = _prod(shape[1:]) if len(shape) > 1 else 1
    except Exception:
        return 500.0
    if eng == "pe":
        if name == "transpose":
            return 90.0
        lhsT = k.get("lhsT", a[1] if len(a) > 1 else None)
        mult = 3.0 if (lhsT is not None and lhsT.dtype == F32) else 1.0
        return max(90.0, free / 2.4 + 4.0) * mult
    if eng == "act":
        return 130.0 + 0.833 * free
    if eng == "dve":
        if name == "reciprocal":
            return 60.0 + 6.2 * free
        if name == "tensor_tensor_scan":
            return 60.0 + 2.1 * free
        return 60.0 + 1.08 * free
    if name == "memset":
        return 100.0 + 0.5 * free
    return 80.0 + 1.8 * free


class Sched:
    ENGS = ("pe", "act", "dve", "pool", "sp")

    def __init__(self, nc, stack, reorder=True):
        self.nc = nc
        self.stack = stack
        self.reorder = reorder
        self.ops = []
        self.toks = {}

    def _tok(self, t):
        k = self.toks.get(t)
        if k is None:
            k = self.toks[t] = Tok()
        return k

    def _record(self, eng, fn, reads, writes, is_dma, dur, lat, is_output):
        if SIM_PARITY is not None:
            def tr(t):
                if isinstance(t, tuple) and t and t[0] in ("A", "B", "Bq", "Bk", "xb", "hb", "ss1", "ss8", "lf", "krT", "qrT", "krTok", "AmT", "vtok", "sc_e", "sc_a", "sc_g", "wbuf", "ktst"):
                    return t + ("par", SIM_PARITY[0])
                if isinstance(t, tuple) and t and t[0] == "ps" and SIM_PARITY[1]:
                    return t + ("p1", SIM_PARITY[0])
                if t in ("hT", "qT", "lft", "utmp", "Sbf", "ktst"):
                    return (t, "par", SIM_PARITY[0])
                return t
            reads = [tr(t) for t in reads]
            writes = [tr(t) for t in writes]
        deps = set()
        for t in reads:
            k = self._tok(t)
            if k.w is not None:
                deps.add(k.w)
        for t in writes:
            k = self._tok(t)
            if k.w is not None:
                deps.add(k.w)
            deps.update(k.r)
        o = Op()
        o.id = len(self.ops)
        o.eng, o.fn, o.is_dma, o.deps, o.dur, o.lat, o.is_output = eng, fn, is_dma, deps, dur, lat, is_output
        self.ops.append(o)
        for t in reads:
            self._tok(t).r.append(o.id)
        for t in writes:
            k = self._tok(t)
            k.w = o.id
            k.r = []
        return o

    def op(self, eng, fn, reads=(), writes=()):
        fn = _freeze(fn)
        return self._record(eng, fn, reads, writes, False, _estimate(eng, fn) * COST_SCALE.get(eng, 1.0), 0.0, False)

    def dma(self, q, out, in_, reads=(), writes=(), is_output=False):
        nbytes = _prod(out.shape) * (2 if out.dtype == BF16 else 4)
        fn = lambda e, o=out, i=in_: e.dma_start(out=o, in_=i)
        return self._record(q, fn, reads, writes, True, 60.0, (2000.0 + nbytes / 250.0) * COST_SCALE.get("dma", 1.0), is_output)

    def _schedule(self):
        import heapq
        ops = self.ops
        n = len(ops)
        order = {e: [] for e in self.ENGS}
        if not self.reorder:
            for o in ops:
                order[o.eng].append(o.id)
            return order
        succ = [[] for _ in range(n)]
        indeg = [0] * n
        for o in ops:
            indeg[o.id] = len(o.deps)
            for d in o.deps:
                succ[d].append(o.id)
        prio = [0.0] * n
        if CP_PRIO:
            for o in reversed(ops):
                best = 0.0
                for sid in succ[o.id]:
                    so = ops[sid]
                    v = prio[sid] + (60.0 if so.eng == o.eng else 160.0)
                    if v > best:
                        best = v
                prio[o.id] = best + o.dur + o.lat
        ready_t = [0.0] * n
        avail = {e: [] for e in self.ENGS}
        free_at = {e: 0.0 for e in self.ENGS}
        busy = {e: False for e in self.ENGS}
        events = []

        def try_start(e, t):
            if busy[e] or not avail[e]:
                return
            cand = [x for x in avail[e] if x[0] <= t]
            if cand:
                if CP_PRIO:
                    pick = min(cand, key=lambda x: (-prio[x[1]], x[1]))
                else:
                    pick = min(cand, key=lambda x: x[1])
            else:
                pick = min(avail[e], key=lambda x: (x[0], x[1]))
                heapq.heappush(events, (pick[0], 2, e))
                return
            avail[e].remove(pick)
            oid = pick[1]
            o = ops[oid]
            order[e].append(oid)
            busy[e] = True
            tend = t + o.dur
            heapq.heappush(events, (tend, 1, oid))
            heapq.heappush(events, (tend + o.lat, 0, oid))

        for o in ops:
            if indeg[o.id] == 0:
                avail[o.eng].append((0.0, o.id))
        for e in self.ENGS:
            try_start(e, 0.0)
        while events:
            t, kind, x = heapq.heappop(events)
            if kind == 1:
                e = ops[x].eng
                busy[e] = False
                try_start(e, t)
            elif kind == 2:
                try_start(x, t)
            else:
                o = ops[x]
                for sid in succ[x]:
                    so = ops[sid]
                    lat = 60.0 if so.eng == o.eng else 160.0
                    if t + lat > ready_t[sid]:
                        ready_t[sid] = t + lat
                    indeg[sid] -= 1
                    if indeg[sid] == 0:
                        avail[so.eng].append((ready_t[sid], sid))
                        try_start(so.eng, t)
        import os as _os
        if _os.environ.get("SCHED_DEBUG"):
            print("SIM makespan us", max(free_at.values()) / 1000.0 if False else t / 1000.0, "ops", n, {e: len(v) for e, v in order.items()})
        assert sum(len(v) for v in order.values()) == n, "scheduler lost ops"
        return order

    def finish(self):
        order = self._schedule()
        ops = self.ops
        nc = self.nc
        self.sem_id = {}

        def newsem(name):
            sm = self.stack.enter_context(nc.semaphore(name))
            self.sem_id[id(sm)] = len(self.sem_id)
            return sm

        dma_sems = {q: [newsem(f"d{q}{i}") for i in range(N_DMA_SEMS)] for q in ("sp", "pool")}
        self.order = order
        for e in self.ENGS:
            ci = 0
            dj = 0
            esems = []
            for oid in order[e]:
                o = ops[oid]
                if o.is_dma:
                    o.sem = dma_sems[e][dj % N_DMA_SEMS]
                    o.val = 16 * (dj // N_DMA_SEMS + 1)
                    o.inc = 16
                    dj += 1
                else:
                    ep, v = divmod(ci, SEM_CAP)
                    while len(esems) <= ep:
                        esems.append(newsem(f"e{e}{len(esems)}"))
                    o.sem = esems[ep]
                    o.val = v + 1
                    o.inc = 1
                    ci += 1
        for e in self.ENGS:
            wd = {}
            for oid in order[e]:
                o = ops[oid]
                waits = []
                for d in sorted(o.deps):
                    do = ops[d]
                    if (not do.is_dma) and (not o.is_dma) and do.eng == e and e == "pe":
                        continue
                    sid = self.sem_id[id(do.sem)]
                    if wd.get(sid, 0) >= do.val:
                        continue
                    wd[sid] = do.val
                    waits.append((do.sem, do.val))
                if o.is_dma and o.val > 16:
                    sid = self.sem_id[id(o.sem)]
                    if wd.get(sid, 0) < o.val - 16:
                        wd[sid] = o.val - 16
                        waits.append((o.sem, o.val - 16))
                o.waits = waits
        fin = {}
        for o in ops:
            if o.is_output:
                sid = self.sem_id[id(o.sem)]
                if fin.get(sid, (None, 0))[1] < o.val:
                    fin[sid] = (o.sem, o.val)
        self.final_waits = list(fin.values())

    def emit(self, block):
        S = self

        def run(engobj, name, final=False):
            for oid in S.order[name]:
                o = S.ops[oid]
                for (sm, v) in o.waits:
                    engobj.wait_ge(sm, v)
                o.fn(engobj).then_inc(o.sem, o.inc)
            if final:
                for (sm, v) in S.final_waits:
                    engobj.wait_ge(sm, v)

        @block.tensor
        def _(e):
            run(e, "pe")

        @block.scalar
        def _(e):
            run(e, "act")

        @block.vector
        def _(e):
            run(e, "dve")

        @block.gpsimd
        def _(e):
            run(e, "pool")

        @block.sync
        def _(e):
            run(e, "sp", final=True)


class _Stop(Exception):
    pass


def build_program(NT, with_sample=True, stop=None):
    SEQ = NT * 512
    NKB = max(4 * NT, 9)
    nc = bass.Bass("TRN2", target_bir_lowering=False)

    def din(name, shape, dt=F32):
        return nc.dram_tensor(name, shape, dt, kind="ExternalInput").ap()

    def dout(name, shape):
        return nc.dram_tensor(name, shape, F32, kind="ExternalOutput").ap()

    xp = din("xp", [SEQ, D]); xs = din("xs", [32, D])
    ck = din("ck", [1024, 512]); cv = din("cv", [1024, 512]); cl = din("cl", [1024, 8])
    sth = din("sth", [4, 128, 128])
    w_in = din("w_in", [D, INC]); w_out = din("w_out", [D, D]); w_up = din("w_up", [D, DFF]); w_down = din("w_down", [DFF, D])
    n1T = din("n1T", [128, 8]); n2T = din("n2T", [128, 8]); gnT = din("gnT", [128, 4]); lbl = din("lbl", [128, 8])
    qg = din("qg", [1, 64]); kg = din("kg", [1, 64]); bfx = din("bfx", [1, 8])
    c_ident = din("c_ident", [128, 128]); c_tri = din("c_tri", [128, 128]); c_maskA = din("c_maskA", [128, 128])
    c_maskD = din("c_maskD", [128, 2048])

    yp = dout("yp", [SEQ, D]); ys = dout("ys", [32, D])
    nk = dout("nk", [SEQ, 512]); nv = dout("nv", [SEQ, 512]); nl = dout("nl", [SEQ, 8]); nh = dout("nh", [4, 128, 128])
    nks = dout("nks", [32, 512]); nvs = dout("nvs", [32, 512]); nls = dout("nls", [32, 8]); nhs = dout("nhs", [4, 128, 128])

    wib = nc.dram_tensor("wib", [7, 128, 4096], BF16).ap()
    wif = nc.dram_tensor("wif", [128, 64], BF16).ap()
    woA = nc.dram_tensor("woA", [2, 128, 2048], BF16).ap()
    woB = nc.dram_tensor("woB", [2, 64, 4096], BF16).ap()
    wub = nc.dram_tensor("wub", [8, 128, 4096], BF16).ap()
    wdb = nc.dram_tensor("wdb", [8, 128, 4096], BF16).ap()
    kTp = nc.dram_tensor("kTp", [4, 128, SEQ], BF16).ap()
    kTs = nc.dram_tensor("kTs", [4, 128, 1536], BF16).ap()

    with ExitStack() as st:
        def sb(name, shape, dt=F32):
            return st.enter_context(nc.sbuf_tensor(name, shape, dt))

        S = Sched(nc, st, reorder=REORDER)
        vaug = sb("vaug", [128, NKB, 8, 65], BF16)
        cT = sb("cT", [128, NKB, 8])
        carry = sb("carry", [128, 8]); cref = sb("cref", [128, 8])
        NKTB = 3
        ktb = [sb(f"ktb{i}", [128, 512], BF16) for i in range(NKTB)]
        qT = sb("qT", [128, 2, 4, 512], BF16)
        ktst = sb("ktst", [128, 4, 512], BF16)
        xblk = [sb(f"xblk{i}", [128, D]) for i in range(2)]
        hb0_ = sb("hb0", [128, D], BF16)
        hb = [hb0_, hb0_]
        hT = sb("hT", [128, 8, 512], BF16)
        NW = 3
        wbuf = [sb(f"wbuf{i}", [128, 4096], BF16) for i in range(NW)]
        scrA = sb("scrA", [128, 8, 512])
        scrB = sb("scrB", [128, 7, 512])
        qrT = sb("qrT", [128, 4, 512], BF16); krT = sb("krT", [128, 4, 512], BF16)
        krTok = sb("krTok", [128, 4, 512], BF16)
        AmT = sb("AmT", [128, 4, 4, 128], BF16)
        vtok = sb("vtok", [128, 4, 512], BF16)
        S32 = sb("S32", [128, 4, 128]); Sbf = sb("Sbf", [128, 4, 128], BF16); utmp = sb("utmp", [128, 4, 128])
        mixA2 = sb("mixA", [128, 2, 4, 512], BF16)
        mixB = sb("mixB", [128, 8, 512], BF16)
        pT = [sb(f"pT{i}", [128, 512], BF16) for i in range(3)]
        identb = sb("identb", [128, 128], BF16); tri = sb("tri", [128, 128]); onesf = sb("onesf", [128, 128])
        maskA = sb("maskA", [128, 128]); maskD = sb("maskD", [128, 4, 512], BF16); rmask = sb("rmask", [128, 512])
        n1s = sb("n1s", [128, 8]); n2s = sb("n2s", [128, 8]); gns = sb("gns", [128, 4]); lbls = sb("lbls", [128, 8])
        lb = sb("lb", [128, 4]); oml = sb("oml", [128, 4])
        qgb = sb("qgb", [128, 64]); kgb = sb("kgb", [128, 64]); bfb = sb("bfb", [128, 8])
        sc_e = sb("sc_e", [128, 4, 8]); sc_a = sb("sc_a", [128, 4, 8]); sc_g = sb("sc_g", [128, 4, 8])
        ss1 = [sb(f"ss1_{i}", [128, 1]) for i in range(2)]
        ss8 = [sb(f"ss8_{i}", [128, 8]) for i in range(2)]
        lf = [sb(f"lf{i}", [128, 8]) for i in range(2)]
        lft = sb("lft", [128, 8])

        PS = [st.enter_context(nc.psum_tensor(f"ps{i}", [128, 512], F32)) for i in range(8)]
        PSB = [p.bitcast(BF16) for p in PS]
        rot_state = [0]

        def rot():
            i = BANKS["rot"][rot_state[0] % len(BANKS["rot"])]
            rot_state[0] += 1
            return i

        rotH_state = [0]
        rotX_state = [0]

        def rotH():
            i = BANKS["H"][rotH_state[0] % len(BANKS["H"])]
            rotH_state[0] += 1
            return i

        def rotX():
            i = BANKS["X"][rotX_state[0] % len(BANKS["X"])]
            rotX_state[0] += 1
            return i

        rotA_state = [0]
        rotF_state = [0]

        def rotA():
            i = BANKS["A"][rotA_state[0] % len(BANKS["A"])]
            rotA_state[0] += 1
            return i

        def rotF():
            i = BANKS["F"][rotF_state[0] % len(BANKS["F"])]
            rotF_state[0] += 1
            return i

        block = st.enter_context(nc.Block())

        def chk(name):
            if stop == name:
                raise _Stop()
        cnt = {"w": 0, "kt": 0, "x": 0, "p": 0, "s": 0, "cast": 0}

        XT = lambda i: [("xb", i, 0), ("xb", i, 1)]

        def A(i):
            return scrA[:, i, :]

        def B(i):
            return scrB[:, i, :]

        S.dma("sp", tri[:, :], c_tri, writes=["tri"])
        S.dma("sp", maskA[:, :], c_maskA, writes=["maskA"])
        S.dma("sp", A(0)[:, 0:128], c_ident, writes=[("A", 0)])
        S.op("dve", lambda e: e.tensor_copy(identb[:, :], A(0)[:, 0:128]), reads=[("A", 0)], writes=["identb"])
        for j in range(4):
            S.dma("sp", A(1 + j), c_maskD[:, j * 512:(j + 1) * 512], writes=[("A", 1 + j)])
            S.op("dve", lambda e, j=j: e.tensor_copy(maskD[:, j, :], A(1 + j)), reads=[("A", 1 + j)], writes=["maskD"])
        S.op("pool", lambda e: e.memset(onesf[:, :], 1.0), writes=["onesf"])
        S.op("pool", lambda e: e.memset(cT[:, :, :].rearrange("p a b -> p (a b)"), 0.0), writes=[("cT", k) for k in range(NKB)])
        S.op("pool", lambda e: e.memset(mixB[:, :, :].rearrange("p a b -> p (a b)"), 0.0), writes=[("mixB", h) for h in range(8)])
        for wi_ in (1, 2):
            S.op("pool", lambda e, wi_=wi_: e.memset(wbuf[wi_][:, :], 0.0), writes=[("wbuf", wi_)])
        S.op("pool", lambda e: e.memset(qT[:, :, :, :].rearrange("p a b c -> p (a b c)"), 0.0), writes=["qT"])
        S.op("pool", lambda e: e.memset(rmask[:, :], 1.0), writes=["rmask"])
        S.op("pool", lambda e: e.memset(rmask[:, :].rearrange("p (c s) -> p c s", s=64)[:, :, 0:1], 0.0),
             reads=["rmask"], writes=["rmask"])
        S.op("pool", lambda e: e.memset(vaug[:, :, :, :].rearrange("p a b c -> p (a b c)"), 1.0),
             writes=[("vaug", k) for k in range(NKB)])
        for (t, src, nm) in [(n1s, n1T, "n1s"), (n2s, n2T, "n2s"), (gns, gnT, "gns"), (lbls, lbl, "lbls")]:
            S.dma("sp", t[:, :], src, writes=[nm])
        S.dma("sp", qgb[:, :], qg.partition_broadcast(128), writes=["qgb"])
        S.dma("sp", kgb[:, :], kg.partition_broadcast(128), writes=["kgb"])
        S.dma("sp", bfb[:, :], bfx.partition_broadcast(128), writes=["bfb"])
        S.op("dve", lambda e: e.tensor_scalar(qgb[:, :], qgb[:, :], 0.125, None, ALU.mult), reads=["qgb"], writes=["qgb"])
        S.op("dve", lambda e: e.tensor_tensor(lb[:, :], lbls[:, 4:8], lbls[:, 0:4], ALU.subtract), reads=["lbls"], writes=["lb"])
        S.op("act", lambda e: e.activation(lb[:, :], lb[:, :], AF.Exp), reads=["lb"], writes=["lb"])
        S.op("dve", lambda e: e.tensor_scalar(lb[:, :], lb[:, :], 1.0, None, ALU.add), reads=["lb"], writes=["lb"])
        S.op("dve", lambda e: e.reciprocal(lb[:, :], lb[:, :]), reads=["lb"], writes=["lb"])
        S.op("dve", lambda e: e.tensor_scalar(oml[:, :], lb[:, :], -1.0, 1.0, ALU.mult, ALU.add), reads=["lb"], writes=["oml"])

        wtoks = {}

        def convert(src, rows, cols, tokname, stores):
            for r0 in range(0, rows, 128):
                for c0 in range(0, cols, 1024):
                    n = min(1024, cols - c0)
                    i = cnt["cast"]; cnt["cast"] += 1
                    sa = i % 4
                    fst = scrA[:, 2 * sa:2 * sa + 2, :].rearrange("p a b -> p (a b)")
                    bst = wbuf[0][:, sa * 1024:(sa + 1) * 1024]
                    S.dma("sp", fst[:, 0:n], src[r0:r0 + 128, c0:c0 + n], writes=[("A", 2 * sa), ("A", 2 * sa + 1)])
                    eng = ("dve", "act")[i % 2]
                    if eng == "act":
                        fn = lambda e, fst=fst, bst=bst, n=n: e.activation(bst[:, 0:n], fst[:, 0:n], AF.Copy)
                    else:
                        fn = lambda e, fst=fst, bst=bst, n=n: e.tensor_copy(bst[:, 0:n], fst[:, 0:n])
                    S.op(eng, fn, reads=[("A", 2 * sa), ("A", 2 * sa + 1)], writes=[("wst", sa)])
                    for j, (d_ap, s_ap) in enumerate(stores(r0, c0, n, bst)):
                        S.dma("pool", d_ap, s_ap, reads=[("wst", sa)], writes=[(tokname, i, j)])
                        wtoks.setdefault(tokname, []).append((tokname, i, j))

        def st_in(r0, c0, n, bst):
            kc = r0 // 128
            g0 = c0 // 512
            res = [(wib[g0][:, kc * 512:(kc + 1) * 512], bst[:, 0:512])]
            if n == 1024:
                res.append((wib[g0 + 1][:, kc * 512:(kc + 1) * 512], bst[:, 512:1024]))
            else:
                res.append((wif[:, kc * 8:(kc + 1) * 8], bst[:, 512:520]))
            return res

        def st_out(r0, c0, n, bst):
            if r0 < 512:
                h = r0 // 128
                return [(woA[nh_][:, h * 512:(h + 1) * 512], bst[:, nh_ * 512:(nh_ + 1) * 512]) for nh_ in range(2)]
            j = (r0 - 512) // 128
            res = []
            for nh_ in range(2):
                res.append((woB[nh_][:, (2 * j) * 512:(2 * j + 1) * 512], bst[0:64, nh_ * 512:(nh_ + 1) * 512]))
                res.append((woB[nh_][:, (2 * j + 1) * 512:(2 * j + 2) * 512], bst[64:128, nh_ * 512:(nh_ + 1) * 512]))
            return res

        def st_up(r0, c0, n, bst):
            kc = r0 // 128
            g0 = c0 // 512
            return [(wub[g0 + t][:, kc * 512:(kc + 1) * 512], bst[:, t * 512:(t + 1) * 512]) for t in range(2)]

        def st_down(r0, c0, n, bst):
            fg, c = divmod(r0 // 128, 4)
            return [(wdb[fg][:, c * 1024:(c + 1) * 1024], bst[:, 0:1024])]

        if stop == "setup":
            wtoks.update({k: [] for k in ("wib", "wob", "wub", "wdb")})
        else:
            convert(w_in, D, INC, "wib", st_in)
            convert(w_out, D, D, "wob", st_out)
            convert(w_up, D, DFF, "wub", st_up)
            convert(w_down, DFF, D, "wdb", st_down)
        WT = lambda i: [("wbuf", i)] + ([("wst", s) for s in range(4)] if i == 0 else [])

        WPOOLS = {"all": [0, 1, 2], "h": [0, 1], "f": [2]}
        wcnt = {"all": 0, "h": 0, "f": 0}

        def load_w(view_out, view_in, srctok, pool="all"):
            i = WPOOLS[pool][wcnt[pool] % len(WPOOLS[pool])]
            wcnt[pool] += 1
            S.dma("sp", view_out(wbuf[i]), view_in, reads=wtoks[srctok], writes=WT(i))
            return i

        def rstd_small(t, n_parts, cols, inv_n):
            pass

        def fox_cumsum(lfb, TB, kb, tokl):
            p = rotX()
            S.op("pe", lambda e: e.matmul(PS[p][0:TB, 0:8], tri[0:TB, 0:TB], lfb[0:TB, :], start=True, stop=True),
                 reads=["tri", tokl], writes=[("ps", p)])
            S.op("pe", lambda e: e.matmul(PS[p][:, 8:16], onesf[0:TB, :], lfb[0:TB, :], start=True, stop=True),
                 reads=["onesf", tokl], writes=[("ps", p)])
            S.op("dve", lambda e: e.tensor_tensor(cT[0:TB, kb, :], PS[p][0:TB, 0:8], carry[0:TB, :], ALU.add),
                 reads=[("ps", p), "carry"], writes=[("cT", kb)])
            S.op("dve", lambda e: e.tensor_tensor(carry[:, :], PS[p][:, 8:16], carry[:, :], ALU.add),
                 reads=[("ps", p), "carry"], writes=["carry"])

        def kT_transposes(knb_ap, TB, col0, tokk):
            p = rotX()
            for pr in range(4):
                S.op("pe", lambda e, pr=pr: e.transpose(PSB[p][:, pr * 128:pr * 128 + TB], knb_ap[0:TB, pr * 128:(pr + 1) * 128],
                                                      identb[0:TB, 0:TB]),
                     reads=[tokk, "identb"], writes=[("ps", p)])
            S.op("act", lambda e: e.activation(ktst[:, :, col0:col0 + TB],
                                               PSB[p][:, 0:512].rearrange("p (a b) -> p a b", b=128)[:, :, 0:TB], AF.Copy),
                 reads=[("ps", p)], writes=["ktst"])

        def norm_qk(src_ps, p, TB, gain, gtok, out32, out32tok, outbf, outbftok):
            i = cnt["s"] % 2; cnt["s"] += 1
            sq = B(0)
            S.op("act", lambda e: e.activation(sq[0:TB, :], src_ps[0:TB, :], AF.Square), reads=[("ps", p)], writes=[("B", 0)])
            S.op("dve", lambda e: e.tensor_reduce(ss8[i][0:TB, :], sq[0:TB, :].rearrange("p (h d) -> p h d", d=64), AX.X, ALU.add),
                 reads=[("B", 0)], writes=[("ss8", i)])
            S.op("act", lambda e: e.activation(ss8[i][0:TB, :], ss8[i][0:TB, :], AF.Ln, bias=EPS, scale=1.0 / 64),
                 reads=[("ss8", i)], writes=[("ss8", i)])
            S.op("act", lambda e: e.activation(ss8[i][0:TB, :], ss8[i][0:TB, :], AF.Exp, scale=-0.5),
                 reads=[("ss8", i)], writes=[("ss8", i)])
            S.op("dve", lambda e: e.tensor_tensor(sq[0:TB, :].rearrange("p (h d) -> p h d", d=64),
                                                  src_ps[0:TB, :].rearrange("p (h d) -> p h d", d=64),
                                                  ss8[i][0:TB, :].unsqueeze(2).to_broadcast([TB, 8, 64]), ALU.mult),
                 reads=[("ps", p), ("ss8", i)], writes=[("B", 0)])
            dst = out32 if out32 is not None else outbf
            dtok = out32tok if out32 is not None else outbftok
            S.op("dve", lambda e: e.tensor_tensor(dst[0:TB, :].rearrange("p (h d) -> p h d", d=64),
                                                  sq[0:TB, :].rearrange("p (h d) -> p h d", d=64),
                                                  gain[0:TB, :].unsqueeze(1).to_broadcast([TB, 8, 64]), ALU.mult),
                 reads=[("B", 0), gtok], writes=[dtok])
            if out32 is not None:
                S.op("pool", lambda e: e.tensor_copy(outbf[0:TB, :], out32[0:TB, :]), reads=[out32tok], writes=[outbftok])

        def tile(T, x_src, t0, kb0, kT_dram, key0, outs, g, sample):
            y_dst, nk_dst, nv_dst, nl_dst = outs
            TB = min(T, 128)
            nblk = T // TB
            CL = 64 if T >= 64 else T
            nch = T // CL
            tg = ("t", g)
            mb = (g % 2) if isinstance(g, int) else 0
            if SIM_PARITY is not None and not isinstance(g, int):
                SIM_PARITY[0] = 0
            if SIM_PARITY is not None:
                SIM_PARITY[1] = True
            mixA = mixA2[:, mb, :, :]

            for tb in range(nblk):
                i = cnt["x"] % 2; cnt["x"] += 1
                r0 = t0 + tb * TB
                S.dma("sp", xblk[i][0:TB, :], x_src[r0:r0 + TB, :], writes=XT(i))
                S.op("pool", lambda e, i=i: e.memset(ss1[i][:, :], 0.0), reads=[("ss1", i)], writes=[("ss1", i)])
                S.op("act", lambda e, i=i: e.activation(hb[i][0:TB, :], xblk[i][0:TB, :], AF.Square, accum_out=ss1[i][0:TB, :]),
                     reads=XT(i), writes=[("hb", 0), ("ss1", i)])
                S.op("act", lambda e, i=i: e.activation(ss1[i][0:TB, :], ss1[i][0:TB, :], AF.Ln, bias=EPS, scale=1.0 / D),
                     reads=[("ss1", i)], writes=[("ss1", i)])
                S.op("act", lambda e, i=i: e.activation(ss1[i][0:TB, :], ss1[i][0:TB, :], AF.Exp, scale=-0.5),
                     reads=[("ss1", i)], writes=[("ss1", i)])
                S.op("dve", lambda e, i=i: e.tensor_scalar(hb[i][0:TB, :], xblk[i][0:TB, :], ss1[i][0:TB, 0:1], None, ALU.mult),
                     reads=XT(i) + [("ss1", i)], writes=[("hb", 0)])
                p = rot()
                for kc in range(8):
                    S.op("pe", lambda e, i=i, kc=kc, p=p: e.transpose(PSB[p][:, kc * 128:kc * 128 + TB], hb[i][0:TB, kc * 128:(kc + 1) * 128],
                                                                    identb[0:TB, 0:TB]),
                         reads=[("hb", 0), "identb"], writes=[("ps", p)])
                S.op("dve", lambda e, p=p, tb=tb: e.tensor_tensor(hT[:, :, tb * 128:tb * 128 + TB],
                                                                  PSB[p][:, :].rearrange("p (a b) -> p a b", b=128)[:, :, 0:TB],
                                                                  n1s[:, :].unsqueeze(2).to_broadcast([128, 8, TB]), ALU.mult),
                     reads=[("ps", p), "n1s"], writes=["hT"])

            chk("p1a")

            def w_group(c0, ncol, pool="h"):
                if SEQ_P1:
                    pool = "all"
                if ncol == 8:
                    return load_w(lambda w: w[:, 0:64], wif[:, :], "wib", pool)
                return load_w(lambda w: w[:, 0:4096], wib[c0 // 512][:, :], "wib", pool)

            def mm_feat(wi, h, p):
                wv = wbuf[wi][:, 0:4096].rearrange("p (k n) -> p k n", n=512)
                for kc in range(8):
                    S.op("pe", lambda e, kc=kc: e.matmul(PS[p][:, 0:T], wv[:, kc, h * 128:(h + 1) * 128], hT[:, kc, 0:T],
                                                         start=(kc == 0), stop=(kc == 7)),
                         reads=[("wbuf", wi), "hT"], writes=[("ps", p)])

            def mm_tok(wi, tb, p, ncol):
                wv = wbuf[wi][:, 0:8 * ncol].rearrange("p (k n) -> p k n", n=ncol)
                for kc in range(8):
                    S.op("pe", lambda e, kc=kc: e.matmul(PS[p][0:TB, 0:ncol], hT[:, kc, tb * 128:tb * 128 + TB], wv[:, kc, 0:ncol],
                                                         start=(kc == 0), stop=(kc == 7)),
                         reads=[("wbuf", wi), "hT"], writes=[("ps", p)])

            TS = [[(A(4), ("A", 4)), (A(5), ("A", 5)), (A(6), ("A", 6)), (A(7), ("A", 7))],
                  [(xblk[0][:, 0:512], ("xb", 0, 0)), (xblk[0][:, 512:1024], ("xb", 0, 1)),
                   (xblk[1][:, 0:512], ("xb", 1, 0)), (xblk[1][:, 512:1024], ("xb", 1, 1))]]
            def hgrn_pre():
                wi = w_group(512, 512)
                for hp in range(2):
                    hs = [2 * hp, 2 * hp + 1]
                    pp = {}
                    for h in hs:
                        pp[h] = rotH()
                        mm_feat(wi, h, pp[h])

                    def st(fn):
                        for h in hs:
                            (ta, tka), (tb_, tkb), (tc, tkc), (td, tkd) = TS[h % 2]
                            fn(h, pp[h], ta, tka, tb_, tkb, tc, tkc, td, tkd)

                    def s1(h, p, ta, tka, tb_, tkb, tc, tkc, td, tkd):
                        S.op("act", lambda e: e.activation(ta[:, 0:T], PS[p][:, 0:T], AF.Exp, scale=-1.0), reads=[("ps", p)], writes=[tka])
                        S.op("act", lambda e: e.activation(ta[:, 0:T], ta[:, 0:T], AF.Ln, bias=1.0), reads=[tka], writes=[tka])
                        S.op("act", lambda e: e.activation(ta[:, 0:T], ta[:, 0:T], AF.Exp, scale=-1.0), reads=[tka], writes=[tka])
                    st(s1)
                    yield

                    def s2(h, p, ta, tka, tb_, tkb, tc, tkc, td, tkd):
                        S.op("pool", lambda e: e.tensor_scalar(tb_[:, 0:T], ta[:, 0:T], oml[:, h:h + 1], lb[:, h:h + 1], ALU.mult, ALU.add),
                             reads=[tka, "oml", "lb"], writes=[tkb])
                    st(s2)
                    yield

                    def s3(h, p, ta, tka, tb_, tkb, tc, tkc, td, tkd):
                        S.op("act", lambda e: e.activation(tc[:, 0:T], tb_[:, 0:T], AF.Ln), reads=[tkb], writes=[tkc])
                        S.op("pool", lambda e: e.tensor_scalar(ta[:, 0:T], tb_[:, 0:T], -1.0, 1.0, ALU.mult, ALU.add), reads=[tkb], writes=[tka])
                    st(s3)
                    yield

                    def s4(h, p, ta, tka, tb_, tkb, tc, tkc, td, tkd):
                        S.op("dve", lambda e: e.tensor_tensor_scan(td[:, 0:T], rmask[:, 0:T], tc[:, 0:T], 0.0, ALU.mult, ALU.add),
                             reads=[tkc, "rmask"], writes=[tkd])
                        tdv = td[:, 0:T].rearrange("p (c s) -> p c s", s=CL)
                        bref = tdv[:, :, CL // 2:CL // 2 + 1]
                        S.op("dve", lambda e: e.tensor_tensor(tb_[:, 0:T].rearrange("p (c s) -> p c s", s=CL), tdv,
                                                              bref.to_broadcast([128, nch, CL]), ALU.subtract),
                             reads=[tkd], writes=[tkb])
                    st(s4)
                    yield

                    def s5(h, p, ta, tka, tb_, tkb, tc, tkc, td, tkd):
                        E1 = A(h)
                        S.op("act", lambda e: e.activation(E1[:, 0:T], tb_[:, 0:T], AF.Exp), reads=[tkb], writes=[("A", h)])
                        S.op("act", lambda e: e.activation(tc[:, 0:T], tb_[:, 0:T], AF.Exp, scale=-1.0), reads=[tkb], writes=[tkc])
                    st(s5)
                    yield

                    def s6(h, p, ta, tka, tb_, tkb, tc, tkc, td, tkd):
                        tdv = td[:, 0:T].rearrange("p (c s) -> p c s", s=CL)
                        bref = tdv[:, :, CL // 2:CL // 2 + 1]
                        blast = tdv[:, :, CL - 1:CL]
                        S.op("pool", lambda e: e.tensor_tensor(krT[:, h, 0:T], ta[:, 0:T], tc[:, 0:T], ALU.mult),
                             reads=[tka, tkc], writes=[("krT", h)])
                        S.op("act", lambda e: e.activation(sc_e[:, h, 0:nch].unsqueeze(2), bref, AF.Exp), reads=[tkd], writes=[("sc_e", h)])
                        S.op("act", lambda e: e.activation(sc_a[:, h, 0:nch].unsqueeze(2), blast, AF.Exp), reads=[tkd], writes=[("sc_a", h)])
                        S.op("dve", lambda e: e.tensor_tensor(sc_g[:, h, 0:nch].unsqueeze(2), blast, bref, ALU.subtract),
                             reads=[tkd], writes=[("sc_g", h)])
                        S.op("act", lambda e: e.activation(sc_g[:, h, 0:nch], sc_g[:, h, 0:nch], AF.Exp), reads=[("sc_g", h)], writes=[("sc_g", h)])
                    st(s6)
                    yield
                wi = w_group(0, 512)
                for h in range(4):
                    p = rotH()
                    mm_feat(wi, h, p)
                    S.op("dve", lambda e, h=h, p=p: e.tensor_tensor(qrT[:, h, 0:T], PS[p][:, 0:T], A(h)[:, 0:T], ALU.mult),
                         reads=[("ps", p), ("A", h)], writes=[("qrT", h)])
                    yield
                wi = w_group(1024, 512)
                for tb in range(nblk):
                    p = rotH()
                    mm_tok(wi, tb, p, 512)
                    S.op("act", lambda e, tb=tb, p=p: e.activation(vtok[0:TB, tb, :], PS[p][0:TB, :], AF.Copy),
                         reads=[("ps", p)], writes=[("vtok", tb)])
                    yield
                for tb in range(nblk):
                    p = rotH()
                    for h in range(4):
                        S.op("pe", lambda e, h=h, p=p, tb=tb: e.transpose(PSB[p][0:TB, h * 128:(h + 1) * 128], krT[:, h, tb * 128:tb * 128 + TB],
                                                                        identb[:, :]),
                             reads=[("krT", h), "identb"], writes=[("ps", p)])
                    S.op("act", lambda e, p=p, tb=tb: e.activation(krTok[0:TB, tb, :], PSB[p][0:TB, 0:512], AF.Copy),
                         reads=[("ps", p)], writes=[("krTok", tb)])
                    p2 = rotH()
                    for h in range(4):
                        S.op("pe", lambda e, h=h, p2=p2, tb=tb: e.matmul(PS[p2][0:TB, h * 128:h * 128 + TB], krT[:, h, tb * 128:tb * 128 + TB],
                                                                       qrT[:, h, tb * 128:tb * 128 + TB], start=True, stop=True),
                             reads=[("krT", h), ("qrT", h)], writes=[("ps", p2)])
                    S.op("dve", lambda e, p2=p2, tb=tb: e.tensor_tensor(AmT[0:TB, tb, :, 0:TB],
                                                                        PS[p2][0:TB, :].rearrange("p (h c) -> p h c", c=128)[:, :, 0:TB],
                                                                        maskA[0:TB, 0:TB].unsqueeze(1).to_broadcast([TB, 4, TB]), ALU.mult),
                         reads=[("ps", p2), "maskA"], writes=[("AmT", tb)])
                    yield

            chk("p1b")
            def recurrence():
                for c in range(nch):
                    tb = (c * CL) // 128
                    off = (c * CL) % 128
                    S.op("pool", lambda e, c=c: e.tensor_tensor(Sbf[:, :, :], S32[:, :, :],
                                                                sc_e[:, :, c:c + 1].to_broadcast([128, 4, 128]), ALU.mult),
                         reads=["S32"] + [("sc_e", h) for h in range(4)], writes=["Sbf"])
                    for h in range(4):
                        S.op("pe", lambda e, h=h, c=c: e.matmul(PS[h][:, c * CL:(c + 1) * CL], Sbf[:, h, :], qrT[:, h, c * CL:(c + 1) * CL],
                                                                start=True, stop=False),
                             reads=["Sbf", ("qrT", h)], writes=[("ps", h)])
                        S.op("pe", lambda e, h=h, c=c, tb=tb, off=off: e.matmul(PS[h][:, c * CL:(c + 1) * CL],
                                                                               vtok[off:off + CL, tb, h * 128:(h + 1) * 128],
                                                                               AmT[off:off + CL, tb, h, off:off + CL], start=False, stop=True),
                             reads=[("vtok", tb), ("AmT", tb)], writes=[("ps", h)])
                    pu = rotH()
                    for h in range(4):
                        S.op("pe", lambda e, h=h, pu=pu, tb=tb, off=off: e.matmul(PS[pu][:, h * 128:(h + 1) * 128],
                                                                                 krTok[off:off + CL, tb, h * 128:(h + 1) * 128],
                                                                                 vtok[off:off + CL, tb, h * 128:(h + 1) * 128], start=True, stop=True),
                             reads=[("krTok", tb), ("vtok", tb)], writes=[("ps", pu)])
                    S.op("dve", lambda e, pu=pu, c=c: e.tensor_tensor(utmp[:, :, :], PS[pu][:, :].rearrange("p (h v) -> p h v", v=128),
                                                                      sc_g[:, :, c:c + 1].to_broadcast([128, 4, 128]), ALU.mult),
                         reads=[("ps", pu)] + [("sc_g", h) for h in range(4)], writes=["utmp"])
                    S.op("dve", lambda e, c=c: e.tensor_tensor(S32[:, :, :], S32[:, :, :], sc_a[:, :, c:c + 1].to_broadcast([128, 4, 128]), ALU.mult),
                         reads=["S32"] + [("sc_a", h) for h in range(4)], writes=["S32"])
                    S.op("dve", lambda e: e.tensor_tensor(S32[:, :, :], S32[:, :, :], utmp[:, :, :], ALU.add),
                         reads=["S32", "utmp"], writes=["S32"])
                    yield

            def fox_proj():
                kb_of = lambda tb: kb0 + tb
                wi = w_group(2048, 512, "f")
                for tb in range(nblk):
                    p = rotX()
                    mm_tok(wi, tb, p, 512)
                    j = cnt["p"] % 2; cnt["p"] += 1
                    qn = scrB[:, 5 + j, 0:256].bitcast(BF16)
                    norm_qk(PS[p], p, TB, qgb, "qgb", None, None, qn, ("Bq", 5 + j))
                    p2 = rotX()
                    for pr in range(4):
                        S.op("pe", lambda e, pr=pr, p2=p2, qn=qn: e.transpose(PSB[p2][:, pr * 128:pr * 128 + TB], qn[0:TB, pr * 128:(pr + 1) * 128],
                                                                            identb[0:TB, 0:TB]),
                             reads=[("Bq", 5 + j), "identb"], writes=[("ps", p2)])
                    S.op("act", lambda e, p2=p2, tb=tb: e.activation(qT[0:64, 0, :, tb * 128:tb * 128 + TB],
                                                                     PSB[p2][0:64, 0:512].rearrange("p (a b) -> p a b", b=128)[:, :, 0:TB], AF.Copy),
                         reads=[("ps", p2)], writes=["qT"])
                    S.op("pool" if False else "dve", lambda e, p2=p2, tb=tb: e.tensor_copy(qT[64:128, 1, :, tb * 128:tb * 128 + TB],
                                                                     PSB[p2][64:128, 0:512].rearrange("p (a b) -> p a b", b=128)[:, :, 0:TB]),
                         reads=[("ps", p2)], writes=["qT"])
                    yield
                wi = w_group(2560, 512, "f")
                for tb in range(nblk):
                    p = rotX()
                    mm_tok(wi, tb, p, 512)
                    j = cnt["p"] % 2; cnt["p"] += 1
                    kn32 = B(1 + j)
                    knb = scrB[:, 5 + j, 256:512].bitcast(BF16)
                    norm_qk(PS[p], p, TB, kgb, "kgb", kn32, ("B", 1 + j), knb, ("Bk", 5 + j))
                    r0 = t0 + tb * TB
                    S.dma("pool", nk_dst[r0:r0 + TB, :], kn32[0:TB, :], reads=[("B", 1 + j)], is_output=True)
                    kT_transposes(knb, TB, tb * 128, ("Bk", 5 + j))
                    yield
                if stop == "foxn3a":
                    raise _Stop()
                S.dma("sp" if stop == "foxn3b" else "pool", kT_dram[:, :, key0:key0 + T].rearrange("q p k -> p q k"), ktst[:, :, 0:T], reads=["ktst"],
                      writes=[("kTd", sample, key0 // 512)])
                if stop == "foxn3c":
                    raise _Stop()
                wi = w_group(3072, 512, "f")
                for tb in range(nblk):
                    p = rotX()
                    mm_tok(wi, tb, p, 512)
                    j = cnt["p"] % 2; cnt["p"] += 1
                    v32 = B(3 + j)
                    S.op("act", lambda e, p=p, v32=v32: e.activation(v32[0:TB, :], PS[p][0:TB, :], AF.Copy), reads=[("ps", p)], writes=[("B", 3 + j)])
                    r0 = t0 + tb * TB
                    S.dma("pool", nv_dst[r0:r0 + TB, :], v32[0:TB, :], reads=[("B", 3 + j)], is_output=True)
                    S.op("dve", lambda e, v32=v32, tb=tb: e.tensor_copy(vaug[0:TB, kb_of(tb), :, 0:64],
                                                                        v32[0:TB, :].rearrange("p (h d) -> p h d", d=64)),
                         reads=[("B", 3 + j)], writes=[("vaug", kb_of(tb))])
                    yield
                wi = w_group(3584, 8, "f")
                for tb in range(nblk):
                    p = rotX()
                    mm_tok(wi, tb, p, 8)
                    j = cnt["p"] % 2; cnt["p"] += 1
                    S.op("dve", lambda e, p=p: e.tensor_tensor(lft[0:TB, :], PS[p][0:TB, 0:8], bfb[0:TB, :], ALU.add),
                         reads=[("ps", p), "bfb"], writes=["lft"])
                    S.op("act", lambda e: e.activation(lft[0:TB, :], lft[0:TB, :], AF.Exp, scale=-1.0), reads=["lft"], writes=["lft"])
                    S.op("act", lambda e: e.activation(lft[0:TB, :], lft[0:TB, :], AF.Ln, bias=1.0), reads=["lft"], writes=["lft"])
                    S.op("dve", lambda e, j=j: e.tensor_scalar(lf[j][0:TB, :], lft[0:TB, :], -1.0, None, ALU.mult),
                         reads=["lft"], writes=[("lf", j)])
                    r0 = t0 + tb * TB
                    S.dma("pool", nl_dst[r0:r0 + TB, :], lf[j][0:TB, :], reads=[("lf", j)], is_output=True)
                    fox_cumsum(lf[j], TB, kb_of(tb), ("lf", j))
                    if (not sample) and tb == 1:
                        S.op("pool", lambda e: e.tensor_copy(cref[:, :], carry[:, :]), reads=["carry"], writes=["cref"])
                    yield

            gh, gr, gf = hgrn_pre(), recurrence(), fox_proj()
            if stop == "p1b":
                for _ in gh:
                    pass
                raise _Stop()
            fox_left = [4 * nblk]

            def fox_step():
                if fox_left[0] > 0:
                    try:
                        next(gf)
                    except StopIteration:
                        fox_left[0] = 0
                        return
                    fox_left[0] -= 1

            nstep = 0
            if SEQ_P1:
                for _ in gh:
                    pass
                for _ in gr:
                    pass
            for _ in gh:
                nstep += 1
                if nstep % 3 == 0 and fox_left[0] > nch:
                    fox_step()
            for _ in gr:
                fox_step()
            for _ in gf:
                pass
            chk("p1c")
            if stop in ("rec_only", "fox_only") or (stop is not None and stop.startswith("foxn")):
                raise _Stop()
            wi = w_group(1536, 512)
            for hp in range(2):
                hs = [2 * hp, 2 * hp + 1]
                pgs = {}

                def st(fn):
                    for h in hs:
                        (ta, tka), (tb_, tkb), (tc, tkc), (td, tkd) = TS[h % 2]
                        fn(h, ta, tka, tb_, tkb, tc, tkc, td, tkd)

                def o1(h, ta, tka, tb_, tkb, tc, tkc, td, tkd):
                    S.op("act", lambda e: e.activation(ta[:, 0:T], PS[h][:, 0:T], AF.Square), reads=[("ps", h)], writes=[tka])
                    pn = rot()
                    S.op("pe", lambda e: e.matmul(PS[pn][:, 0:T], onesf[:, :], ta[:, 0:T], start=True, stop=True),
                         reads=["onesf", tka], writes=[("ps", pn)])
                    S.op("act", lambda e: e.activation(tb_[:, 0:T], PS[pn][:, 0:T], AF.Ln, bias=EPS, scale=1.0 / 128),
                         reads=[("ps", pn)], writes=[tkb])
                    S.op("act", lambda e: e.activation(tb_[:, 0:T], tb_[:, 0:T], AF.Exp, scale=-0.5), reads=[tkb], writes=[tkb])
                    pg = rot()
                    pgs[h] = pg
                    mm_feat(wi, h, pg)
                    S.op("act", lambda e: e.activation(tc[:, 0:T], PS[pg][:, 0:T], AF.Exp, scale=-1.0), reads=[("ps", pg)], writes=[tkc])
                    S.op("act", lambda e: e.activation(tc[:, 0:T], tc[:, 0:T], AF.Ln, bias=1.0), reads=[tkc], writes=[tkc])
                    S.op("act", lambda e: e.activation(tc[:, 0:T], tc[:, 0:T], AF.Exp, scale=-1.0), reads=[tkc], writes=[tkc])
                st(o1)

                def o2(h, ta, tka, tb_, tkb, tc, tkc, td, tkd):
                    pg = pgs[h]
                    S.op("dve", lambda e: e.tensor_tensor(tc[:, 0:T], PS[pg][:, 0:T], tc[:, 0:T], ALU.mult),
                         reads=[("ps", pg), tkc], writes=[tkc])
                    S.op("dve", lambda e: e.tensor_tensor(td[:, 0:T], PS[h][:, 0:T], tb_[:, 0:T], ALU.mult),
                         reads=[("ps", h), tkb], writes=[tkd])
                    S.op("dve", lambda e: e.scalar_tensor_tensor(mixA[:, h, 0:T], td[:, 0:T], gns[:, h:h + 1], tc[:, 0:T], ALU.mult, ALU.mult),
                         reads=[tkd, tkc, "gns"], writes=[("mixA", mb, h)])
                st(o2)

            if DBG == "dumpA" and sample:
                for h in range(4):
                    S.op("act", lambda e, h=h: e.activation(A(4)[:, h * 32:(h + 1) * 32], mixA[:, h, 0:32], AF.Copy),
                         reads=[("mixA", mb, h)], writes=[("A", 4)])
                S.dma("pool", y_dst.rearrange("t (a b) -> (t a) b", b=256)[:, 0:128], A(4)[:, 0:128], reads=[("A", 4)], is_output=True)
                raise _Stop()
            chk("hout")
            if SIM_PARITY is not None:
                SIM_PARITY[1] = False
            yield "p1"
            nkb = kb0 + nblk
            TB5 = [("B", 5), ("Bq", 5), ("Bk", 5)]
            TB6 = [("B", 6), ("Bq", 6), ("Bk", 6)]
            biasg = scrB[:, 6, :].rearrange("p (k h) -> p k h", h=8)
            S.op("dve", lambda e: e.tensor_tensor(biasg[:, 0:nkb, :], cref[:, :].unsqueeze(1).to_broadcast([128, nkb, 8]),
                                                  cT[:, 0:nkb, :], ALU.subtract),
                 reads=["cref"] + [("cT", k) for k in range(nkb)], writes=TB6)
            nkt = (nkb + 3) // 4
            rden = B(4); rdb = B(5)

            LOOK = ATT_LOOK
            loads = [(pr, kt) for pr in range(4) for kt in range(nkt)]
            ktbuf = {}
            nl = [0]

            def ensure_loaded(idx):
                while nl[0] <= min(idx, len(loads) - 1):
                    pr_, kt_ = loads[nl[0]]
                    bi = cnt["kt"] % NKTB; cnt["kt"] += 1
                    kcols = min(512, (nkb - kt_ * 4) * 128)
                    if sample and kt_ == nkt - 1:
                        kcols = T
                    S.dma("sp", ktb[bi][:, 0:kcols], kT_dram[pr_, :, kt_ * 512:kt_ * 512 + kcols],
                          reads=[("kTd", sample, kt_)], writes=[("ktb", bi)])
                    ktbuf[(pr_, kt_)] = bi
                    nl[0] += 1

            def fin(pr):
                accb = (0, 1)
                for e2 in range(2):
                    h = 2 * pr + e2
                    ab = accb[e2]
                    slot = 4 + e2
                    stok = [("B", 4)] if e2 == 0 else TB5
                    rden_ = B(slot); rdb_ = B(slot)
                    S.op("dve", lambda e: e.reciprocal(rden_[64:65, 0:T], PS[ab][64:65, 0:T]), reads=[("ps", ab)], writes=stok)
                    pb_ = rotA()
                    S.op("pe", lambda e: e.matmul(PS[pb_][0:64, 0:T], onesf[64:65, 0:64], rden_[64:65, 0:T], start=True, stop=True),
                         reads=["onesf"] + stok, writes=[("ps", pb_)])
                    S.op("dve", lambda e: e.tensor_copy(rdb_[0:64, 0:T], PS[pb_][0:64, 0:T]), reads=[("ps", pb_)], writes=stok)
                    S.op("dve", lambda e: e.tensor_tensor(mixB[0:64, h, 0:T], PS[ab][0:64, 0:T], rdb_[0:64, 0:T], ALU.mult),
                         reads=[("ps", ab)] + stok, writes=[("mixB", h)])

            pending = None
            for pr in range(4):
                accb = (0, 1)
                units = []
                for kt in range(nkt):
                    for kbl in range(4):
                        kb = kt * 4 + kbl
                        if kb >= nkb:
                            break
                        for e2 in range(2):
                            units.append((kt, kbl, kb, e2))

                def emit_qk(u):
                    kt, kbl, kb, e2 = u
                    li = loads.index((pr, kt))
                    ensure_loaded(li + 1)
                    bi = ktbuf[(pr, kt)]
                    nkeys = TB if kb >= kb0 else 128
                    diag = kb - kb0 if kb >= kb0 else -1
                    h = 2 * pr + e2
                    ps_ = rotA()
                    c0 = 128 * diag if (diag > 0 and T == 512) else 0
                    S.op("pe", lambda e: e.matmul(PS[ps_][0:nkeys, c0:T], ktb[bi][:, kbl * 128:kbl * 128 + nkeys],
                                                  qT[:, e2, pr, c0:T], start=True, stop=True),
                         reads=[("ktb", bi), "qT"], writes=[("ps", ps_)])
                    pi = cnt["p"] % 3; cnt["p"] += 1
                    S.op("act", lambda e: e.activation(pT[pi][0:nkeys, c0:T], PS[ps_][0:nkeys, c0:T], AF.Exp,
                                                       bias=biasg[0:nkeys, kb, h:h + 1], scale=1.0),
                         reads=[("ps", ps_)] + TB6, writes=[("pT", pi)])
                    if diag >= 0:
                        S.op("dve", lambda e: e.tensor_tensor(pT[pi][0:nkeys, c0:T], pT[pi][0:nkeys, c0:T], maskD[0:nkeys, diag, c0:T], ALU.mult),
                             reads=[("pT", pi), "maskD"], writes=[("pT", pi)])
                    return (pi, nkeys, c0)

                def emit_pv(u, info):
                    kt, kbl, kb, e2 = u
                    pi, nkeys, c0 = info
                    h = 2 * pr + e2
                    last = (kb == nkb - 1)
                    ab = accb[e2]
                    S.op("pe", lambda e: e.matmul(PS[ab][0:65, c0:T], vaug[0:nkeys, kb, h, :], pT[pi][0:nkeys, c0:T],
                                                  start=(kb == 0), stop=last),
                         reads=[("vaug", kb), ("pT", pi)], writes=[("ps", ab)])

                infos = []
                for i, u in enumerate(units):
                    infos.append(emit_qk(u))
                    if i >= LOOK:
                        emit_pv(units[i - LOOK], infos[i - LOOK])
                for i in range(max(0, len(units) - LOOK), len(units)):
                    emit_pv(units[i], infos[i])
                fin(pr)

            if DBG == "dumpB" and sample:
                for h in range(8):
                    S.op("act", lambda e, h=h: e.activation(A(4)[0:64, h * 32:(h + 1) * 32], mixB[0:64, h, 0:32], AF.Copy),
                         reads=[("mixB", h)], writes=[("A", 4)])
                S.dma("pool", y_dst.rearrange("t (a b) -> (t a) b", b=256)[0:64, :], A(4)[0:64, 0:256], reads=[("A", 4)], is_output=True)
                raise _Stop()
            chk("attn")
            yield "att"
            x2 = lambda tb: scrA[:, 2 * tb:2 * tb + 2, :].rearrange("p a b -> p (a b)")
            x2t = lambda tb: [("A", 2 * tb), ("A", 2 * tb + 1)]
            for tb in range(nblk):
                r0 = t0 + tb * TB
                S.dma("sp", x2(tb)[0:TB, :], x_src[r0:r0 + TB, :], writes=x2t(tb))
            for nh_ in range(2):
                wa = load_w(lambda w: w[:, 0:2048], woA[nh_][:, :], "wob")
                wb = load_w(lambda w: w[0:64, 0:4096], woB[nh_][:, :], "wob")
                wav = wbuf[wa][:, 0:2048].rearrange("p (k n) -> p k n", n=512)
                wbv = wbuf[wb][:, 0:4096].rearrange("p (k n) -> p k n", n=512)
                for tb in range(nblk):
                    p = rotF()
                    hsA = [] if DBG in ("noA", "noAB") else list(range(4))
                    hsB = [] if DBG in ("noB", "noAB") else list(range(8))
                    if DBG == "noAB":
                        continue
                    allh = [("A", h) for h in hsA] + [("B", h) for h in hsB]
                    for ii, (kind, h) in enumerate(allh):
                        if kind == "A":
                            S.op("pe", lambda e, h=h, p=p, tb=tb, ii=ii: e.matmul(PS[p][0:TB, :], mixA[:, h, tb * 128:tb * 128 + TB], wav[:, h, :],
                                                                                 start=(ii == 0), stop=(ii == len(allh) - 1)),
                                 reads=[("mixA", mb, h), ("wbuf", wa)], writes=[("ps", p)])
                        else:
                            S.op("pe", lambda e, h=h, p=p, tb=tb, ii=ii: e.matmul(PS[p][0:TB, :], mixB[:, h, tb * 128:tb * 128 + TB], wbv[:, h, :],
                                                                                 start=(ii == 0), stop=(ii == len(allh) - 1)),
                                 reads=[("mixB", h), ("wbuf", wb)], writes=[("ps", p)])
                    S.op("dve", lambda e, p=p, tb=tb, nh_=nh_: e.tensor_tensor(x2(tb)[0:TB, nh_ * 512:(nh_ + 1) * 512], PS[p][0:TB, :],
                                                                                x2(tb)[0:TB, nh_ * 512:(nh_ + 1) * 512], ALU.add),
                         reads=[("ps", p)] + x2t(tb), writes=x2t(tb))
            for tb in range(nblk):
                i = cnt["x"] % 2; cnt["x"] += 1
                S.op("pool", lambda e, i=i: e.memset(ss1[i][:, :], 0.0), reads=[("ss1", i)], writes=[("ss1", i)])
                S.op("act", lambda e, i=i, tb=tb: e.activation(hb[i][0:TB, :], x2(tb)[0:TB, :], AF.Square, accum_out=ss1[i][0:TB, :]),
                     reads=x2t(tb), writes=[("hb", 0), ("ss1", i)])
                S.op("act", lambda e, i=i: e.activation(ss1[i][0:TB, :], ss1[i][0:TB, :], AF.Ln, bias=EPS, scale=1.0 / D),
                     reads=[("ss1", i)], writes=[("ss1", i)])
                S.op("act", lambda e, i=i: e.activation(ss1[i][0:TB, :], ss1[i][0:TB, :], AF.Exp, scale=-0.5),
                     reads=[("ss1", i)], writes=[("ss1", i)])
                S.op("dve", lambda e, i=i, tb=tb: e.tensor_scalar(hb[i][0:TB, :], x2(tb)[0:TB, :], ss1[i][0:TB, 0:1], None, ALU.mult),
                     reads=x2t(tb) + [("ss1", i)], writes=[("hb", 0)])
                p = rotF()
                for kc in range(8):
                    S.op("pe", lambda e, i=i, kc=kc, p=p: e.transpose(PSB[p][:, kc * 128:kc * 128 + TB], hb[i][0:TB, kc * 128:(kc + 1) * 128],
                                                                    identb[0:TB, 0:TB]),
                         reads=[("hb", 0), "identb"], writes=[("ps", p)])
                S.op("dve", lambda e, p=p, tb=tb: e.tensor_tensor(hT[:, :, tb * 128:tb * 128 + TB],
                                                                  PSB[p][:, :].rearrange("p (a b) -> p a b", b=128)[:, :, 0:TB],
                                                                  n2s[:, :].unsqueeze(2).to_broadcast([128, 8, TB]), ALU.mult),
                     reads=[("ps", p), "n2s"], writes=["hT"])
            aT = [scrB[:, 0:2, :].rearrange("p a b -> p (a b)").bitcast(BF16).rearrange("p (c t) -> p c t", t=512),
                  scrB[:, 2:4, :].rearrange("p a b -> p (a b)").bitcast(BF16).rearrange("p (c t) -> p c t", t=512)]
            aTt = [[("B", 0), ("B", 1)], [("B", 2), ("B", 3)]]

            def ffn_up(fg):
                wu = load_w(lambda w: w[:, 0:4096], wub[fg][:, :], "wub")
                wv = wbuf[wu][:, 0:4096].rearrange("p (k n) -> p k n", n=512)
                a = fg % 2
                for fc in range(4):
                    p = rotF()
                    for kc in range(8):
                        S.op("pe", lambda e, kc=kc, p=p, fc=fc: e.matmul(PS[p][:, 0:T], wv[:, kc, fc * 128:(fc + 1) * 128], hT[:, kc, 0:T],
                                                                        start=(kc == 0), stop=(kc == 7)),
                             reads=[("wbuf", wu), "hT"], writes=[("ps", p)])
                    S.op("dve", lambda e, p=p, fc=fc, a=a: e.tensor_scalar(aT[a][:, fc, 0:T], PS[p][:, 0:T], 0.0, None, ALU.max),
                         reads=[("ps", p)], writes=aTt[a])
                    S.op("pool", lambda e, fc=fc, a=a: e.tensor_tensor(aT[a][:, fc, 0:T], aT[a][:, fc, 0:T], aT[a][:, fc, 0:T], ALU.mult),
                         reads=aTt[a], writes=aTt[a])

            def ffn_down(fg):
                wd = load_w(lambda w: w[:, 0:4096], wdb[fg][:, :], "wdb")
                wv = wbuf[wd][:, 0:4096].rearrange("p (c n) -> p c n", n=1024)
                a = fg % 2
                for tb in range(nblk):
                    for nh_ in range(2):
                        p = rotF()
                        for fc in range(4):
                            S.op("pe", lambda e, fc=fc, p=p, tb=tb, nh_=nh_: e.matmul(PS[p][0:TB, :], aT[a][:, fc, tb * 128:tb * 128 + TB],
                                                                                     wv[:, fc, nh_ * 512:(nh_ + 1) * 512],
                                                                                     start=(fc == 0), stop=(fc == 3)),
                                 reads=aTt[a] + [("wbuf", wd)], writes=[("ps", p)])
                        S.op("dve", lambda e, p=p, tb=tb, nh_=nh_: e.tensor_tensor(x2(tb)[0:TB, nh_ * 512:(nh_ + 1) * 512], PS[p][0:TB, :],
                                                                                    x2(tb)[0:TB, nh_ * 512:(nh_ + 1) * 512], ALU.add),
                             reads=[("ps", p)] + x2t(tb), writes=x2t(tb))

            chk("wout")
            yield "wout"
            ffn_up(0)
            for fg in range(8):
                if fg + 1 < 8:
                    ffn_up(fg + 1)
                ffn_down(fg)
            for tb in range(nblk):
                r0 = t0 + tb * TB
                S.dma("pool", y_dst[r0:r0 + TB, :], x2(tb)[0:TB, :], reads=x2t(tb), is_output=True)

        def main_seq():
            chk("conv")
            if with_sample:
                S.op("pool", lambda e: e.memset(carry[:, :], 0.0), writes=["carry"])
                S.dma("sp", S32[:, :, :], sth.rearrange("h k v -> k h v"), writes=["S32"])
                for kb in range(8):
                    j = kb % 2
                    kn32 = B(1 + j); knb = scrB[:, 5 + j, 256:512].bitcast(BF16); v32 = B(3 + j)
                    S.dma("sp", kn32[:, :], ck[kb * 128:(kb + 1) * 128, :], writes=[("B", 1 + j)])
                    S.op("pool", lambda e, kn32=kn32, knb=knb: e.tensor_copy(knb[:, :], kn32[:, :]), reads=[("B", 1 + j)], writes=[("Bk", 5 + j)])
                    kT_transposes(knb, 128, (kb % 4) * 128, ("Bk", 5 + j))
                    if kb % 4 == 3:
                        k0 = (kb // 4) * 512
                        S.dma("pool", kTs[:, :, k0:k0 + 512].rearrange("q p k -> p q k"), ktst[:, :, :], reads=["ktst"],
                              writes=[("kTd", True, kb // 4)])
                    S.dma("sp", v32[:, :], cv[kb * 128:(kb + 1) * 128, :], writes=[("B", 3 + j)])
                    S.op("dve", lambda e, kb=kb, v32=v32: e.tensor_copy(vaug[:, kb, :, 0:64], v32[:, :].rearrange("p (h d) -> p h d", d=64)),
                         reads=[("B", 3 + j)], writes=[("vaug", kb)])
                    S.dma("sp", lf[j][:, :], cl[kb * 128:(kb + 1) * 128, :], writes=[("lf", j)])
                    fox_cumsum(lf[j], 128, kb, ("lf", j))
                S.op("pool", lambda e: e.tensor_copy(cref[:, :], carry[:, :]), reads=["carry"], writes=["cref"])
                chk("sprep")
                for _ in tile(32, xs, 0, 8, kTs, 1024, (ys, nks, nvs, nls), "s", True):
                    pass
                S.dma("pool", nhs.rearrange("h k v -> k h v"), S32[:, :, :], reads=["S32"], is_output=True)
                chk("sample")
            S.op("pool", lambda e: e.memset(carry[:, :], 0.0), reads=["carry"], writes=["carry"])
            S.op("pool", lambda e: e.memset(S32[:, :, :].rearrange("p a b -> p (a b)"), 0.0), reads=["S32"], writes=["S32"])
            gens = [tile(512, xp, g * 512, 4 * g, kTp, g * 512, (yp, nk, nv, nl), g, False) for g in range(NT)]

            def fin_gen(gen):
                for _ in gen:
                    pass

            if not PIPE:
                for gen in gens:
                    fin_gen(gen)
            else:
                def nx(i, fin_=False):
                    if SIM_PARITY is not None:
                        SIM_PARITY[0] = i % 2
                    if fin_:
                        fin_gen(gens[i])
                    else:
                        next(gens[i])

                nx(0)
                nx(0)
                for g in range(1, NT):
                    nx(g)
                    nx(g - 1)
                    nx(g)
                    nx(g - 1, True)
                nx(NT - 1, True)
            S.dma("pool", nh.rearrange("h k v -> k h v"), S32[:, :, :], reads=["S32"], is_output=True)

        try:
            main_seq()
        except _Stop:
            pass
        S.finish()
        S.emit(block)
    return nc


def host_consts():
    ident = np.eye(128, dtype=np.float32)
    ii = np.arange(128)
    tri = (ii[:, None] <= ii[None, :]).astype(np.float32)
    maskA = ((ii[:, None] <= ii[None, :]) & ((ii[:, None] // 64) == (ii[None, :] // 64))).astype(np.float32)
    q = np.arange(512)
    maskD = np.concatenate([(q[None, :] >= (128 * j + ii[:, None])).astype(np.float32) for j in range(4)], axis=1)
    return ident, tri, maskA, maskD


def make_in_maps(inp, n_cores, NT):
    SEQ = NT * 512
    ident, tri, maskA, maskD = host_consts()
    f = lambda a: np.ascontiguousarray(np.asarray(a, dtype=np.float32))
    lbl = f(inp["hgrn_lb_logits"])
    lblT = np.concatenate([lbl[0].reshape(4, 128).T, lbl[1].reshape(4, 128).T], axis=1)
    shared = {
        "w_in": f(inp["w_in"][0]), "w_out": f(inp["w_out"][0]), "w_up": f(inp["w_up"][0]), "w_down": f(inp["w_down"][0]),
        "n1T": f(np.asarray(inp["norm1"][0]).reshape(8, 128).T), "n2T": f(np.asarray(inp["norm2"][0]).reshape(8, 128).T),
        "gnT": f(np.asarray(inp["hgrn_out_norm"][0]).reshape(4, 128).T), "lbl": f(lblT),
        "qg": f(np.asarray(inp["q_norm_gain"][0]).reshape(1, 64)), "kg": f(np.asarray(inp["k_norm_gain"][0]).reshape(1, 64)),
        "bfx": f(np.asarray(inp["b_fox_f"][0]).reshape(1, 8)),
        "c_ident": ident, "c_tri": tri, "c_maskA": maskA, "c_maskD": maskD,
    }
    maps = []
    for c in range(n_cores):
        m = dict(shared)
        m["xp"] = f(inp["x_prompt"][c][:SEQ])
        m["xs"] = f(inp["x_sample"][c])
        m["ck"] = f(np.asarray(inp["cache_fox_k"][0, c]).reshape(1024, 512))
        m["cv"] = f(np.asarray(inp["cache_fox_v"][0, c]).reshape(1024, 512))
        m["cl"] = f(inp["cache_fox_logf"][0, c])
        m["sth"] = f(inp["state_hgrn"][0, c])
        maps.append(m)
    return maps


def run(inp, n_cores=8, NT=16, stop=None):
    nc = build_program(NT, stop=stop)
    maps = make_in_maps(inp, n_cores, NT)
    res = run_bass_kernel_spmd(nc, maps, core_ids=list(range(n_cores)))
    R = res.results
    SEQ = NT * 512
    st = lambda k: np.stack([np.asarray(R[c][k], dtype=np.float32) for c in range(n_cores)], axis=0)
    y_p = st("yp"); y_s = st("ys")
    nk_ = st("nk").reshape(1, n_cores, SEQ, 8, 64); nv_ = st("nv").reshape(1, n_cores, SEQ, 8, 64)
    nl_ = st("nl").reshape(1, n_cores, SEQ, 8); nh_ = st("nh").reshape(1, n_cores, 4, 128, 128)
    nks_ = st("nks").reshape(1, n_cores, 32, 8, 64); nvs_ = st("nvs").reshape(1, n_cores, 32, 8, 64)
    nls_ = st("nls").reshape(1, n_cores, 32, 8); nhs_ = st("nhs").reshape(1, n_cores, 4, 128, 128)
    return (y_p, y_s, nk_, nv_, nl_, nh_, nks_, nvs_, nls_, nhs_)


def kernel(**inputs):
    return run(inputs, n_cores=8, NT=16)
```

```python
import numpy as np
from contextlib import ExitStack
import concourse.bass as bass
import concourse.mybir as mybir
from concourse.bass_utils import run_bass_kernel_spmd

F32 = mybir.dt.float32
BF16 = mybir.dt.bfloat16
AF = mybir.ActivationFunctionType
ALU = mybir.AluOpType
AX = mybir.AxisListType

SEM_CAP = 12000
N_DMA_SEMS = 40
EPS = 1e-6
DBG = None
REORDER = True
BANKS = {"rot": (4, 5, 6, 7), "H": (4, 5), "X": (6, 7), "A": (4, 5, 6), "F": (2, 3, 7)}
ATT_LOOK = 2
SIM_PARITY = None
CP_PRIO = False
SEQ_P1 = False
COST_SCALE = {}
PIPE = True
DBG2 = None
D = 1024
INC = 3592
DFF = 4096


import types


def _freeze(fn):
    if fn.__closure__ is None:
        return fn
    cells = []
    for c in fn.__closure__:
        try:
            cells.append(types.CellType(c.cell_contents))
        except ValueError:
            cells.append(c)
    return types.FunctionType(fn.__code__, fn.__globals__, fn.__name__, fn.__defaults__, tuple(cells))


class Op:
    __slots__ = ("id", "eng", "fn", "is_dma", "deps", "dur", "lat", "is_output", "sem", "val", "waits", "inc")


class Tok:
    __slots__ = ("w", "r")

    def __init__(self):
        self.w = None
        self.r = []


class _Fake:
    def __init__(self):
        self.rec = None

    def __getattr__(self, name):
        def f(*a, **k):
            self.rec = (name, a, k)
            return self
        return f


def _prod(xs):
    r = 1
    for x in xs:
        r *= int(x)
    return r


def _estimate(eng, fn):
    fk = _Fake()
    try:
        fn(fk)
        name, a, k = fk.rec
        out = k.get("out", a[0] if a else None)
        shape = tuple(out.shape)
        free = _prod(shape[1:]) if len(shape) > 1 else 1
    except Exception:
        return 500.0
    if eng == "pe":
        if name == "transpose":
            return 70.0
        lhsT = k.get("lhsT", a[1] if len(a) > 1 else None)
        mult = 4.0 if (lhsT is not None and lhsT.dtype == F32) else 1.0
        return (max(free, 48) / 2.4 + 12.0) * mult
    if eng == "act":
        return 224.0 + 0.833 * free
    if eng == "dve":
        if name == "reciprocal":
            return 60.0 + 6.2 * free
        if name == "tensor_tensor_scan":
            return 60.0 + 2.1 * free
        return 60.0 + 1.04 * free
    if name == "memset":
        return 100.0 + 0.5 * free
    return 100.0 + 2.0 * free


class Sched:
    ENGS = ("pe", "act", "dve", "pool", "sp")

    def __init__(self, nc, stack, reorder=True):
        self.nc = nc
        self.stack = stack
        self.reorder = reorder
        self.ops = []
        self.toks = {}

    def _tok(self, t):
        k = self.toks.get(t)
        if k is None:
            k = self.toks[t] = Tok()
        return k

    def _record(self, eng, fn, reads, writes, is_dma, dur, lat, is_output):
        if SIM_PARITY is not None:
            def tr(t):
                if isinstance(t, tuple) and t and t[0] in ("A", "B", "Bq", "Bk", "xb", "hb", "ss1", "ss8", "lf", "krT", "qrT", "krTok", "AmT", "vtok", "sc_e", "sc_a", "sc_g", "wbuf", "ktst"):
                    return t + ("par", SIM_PARITY[0])
                if isinstance(t, tuple) and t and t[0] == "ps" and SIM_PARITY[1]:
                    return t + ("p1", SIM_PARITY[0])
                if t in ("hT", "qT", "lft", "utmp", "Sbf", "ktst"):
                    return (t, "par", SIM_PARITY[0])
                return t
            reads = [tr(t) for t in reads]
            writes = [tr(t) for t in writes]
        deps = set()
        for t in reads:
            k = self._tok(t)
            if k.w is not None:
                deps.add(k.w)
        for t in writes:
            k = self._tok(t)
            if k.w is not None:
                deps.add(k.w)
            deps.update(k.r)
        o = Op()
        o.id = len(self.ops)
        o.eng, o.fn, o.is_dma, o.deps, o.dur, o.lat, o.is_output = eng, fn, is_dma, deps, dur, lat, is_output
        self.ops.append(o)
        for t in reads:
            self._tok(t).r.append(o.id)
        for t in writes:
            k = self._tok(t)
            k.w = o.id
            k.r = []
        return o

    def op(self, eng, fn, reads=(), writes=()):
        fn = _freeze(fn)
        return self._record(eng, fn, reads, writes, False, _estimate(eng, fn) * COST_SCALE.get(eng, 1.0), 0.0, False)

    def dma(self, q, out, in_, reads=(), writes=(), is_output=False):
        nbytes = _prod(out.shape) * (2 if out.dtype == BF16 else 4)
        fn = lambda e, o=out, i=in_: e.dma_start(out=o, in_=i)
        return self._record(q, fn, reads, writes, True, 60.0, (2000.0 + nbytes / 150.0) * COST_SCALE.get("dma", 1.0), is_output)

    def _schedule(self):
        import heapq
        ops = self.ops
        n = len(ops)
        order = {e: [] for e in self.ENGS}
        if not self.reorder:
            for o in ops:
                order[o.eng].append(o.id)
            return order
        succ = [[] for _ in range(n)]
        indeg = [0] * n
        for o in ops:
            indeg[o.id] = len(o.deps)
            for d in o.deps:
                succ[d].append(o.id)
        prio = [0.0] * n
        if CP_PRIO:
            for o in reversed(ops):
                best = 0.0
                for sid in succ[o.id]:
                    so = ops[sid]
                    v = prio[sid] + (60.0 if so.eng == o.eng else 160.0)
                    if v > best:
                        best = v
                prio[o.id] = best + o.dur + o.lat
        ready_t = [0.0] * n
        avail = {e: [] for e in self.ENGS}
        free_at = {e: 0.0 for e in self.ENGS}
        busy = {e: False for e in self.ENGS}
        events = []

        def try_start(e, t):
            if busy[e] or not avail[e]:
                return
            cand = [x for x in avail[e] if x[0] <= t]
            if cand:
                if CP_PRIO:
                    pick = min(cand, key=lambda x: (-prio[x[1]], x[1]))
                else:
                    pick = min(cand, key=lambda x: x[1])
            else:
                pick = min(avail[e], key=lambda x: (x[0], x[1]))
                heapq.heappush(events, (pick[0], 2, e))
                return
            avail[e].remove(pick)
            oid = pick[1]
            o = ops[oid]
            order[e].append(oid)
            busy[e] = True
            tend = t + o.dur
            heapq.heappush(events, (tend, 1, oid))
            heapq.heappush(events, (tend + o.lat, 0, oid))

        for o in ops:
            if indeg[o.id] == 0:
                avail[o.eng].append((0.0, o.id))
        for e in self.ENGS:
            try_start(e, 0.0)
        while events:
            t, kind, x = heapq.heappop(events)
            if kind == 1:
                e = ops[x].eng
                busy[e] = False
                try_start(e, t)
            elif kind == 2:
                try_start(x, t)
            else:
                o = ops[x]
                for sid in succ[x]:
                    so = ops[sid]
                    lat = 60.0 if so.eng == o.eng else 160.0
                    if t + lat > ready_t[sid]:
                        ready_t[sid] = t + lat
                    indeg[sid] -= 1
                    if indeg[sid] == 0:
                        avail[so.eng].append((ready_t[sid], sid))
                        try_start(so.eng, t)
        import os as _os
        if _os.environ.get("SCHED_DEBUG"):
            print("SIM makespan us", max(free_at.values()) / 1000.0 if False else t / 1000.0, "ops", n, {e: len(v) for e, v in order.items()})
        assert sum(len(v) for v in order.values()) == n, "scheduler lost ops"
        return order

    def finish(self):
        order = self._schedule()
        ops = self.ops
        nc = self.nc
        self.sem_id = {}

        def newsem(name):
            sm = self.stack.enter_context(nc.semaphore(name))
            self.sem_id[id(sm)] = len(self.sem_id)
            return sm

        dma_sems = {q: [newsem(f"d{q}{i}") for i in range(N_DMA_SEMS)] for q in ("sp", "pool")}
        self.order = order
        for e in self.ENGS:
            ci = 0
            dj = 0
            esems = []
            for oid in order[e]:
                o = ops[oid]
                if o.is_dma:
                    o.sem = dma_sems[e][dj % N_DMA_SEMS]
                    o.val = 16 * (dj // N_DMA_SEMS + 1)
                    o.inc = 16
                    dj += 1
                else:
                    ep, v = divmod(ci, SEM_CAP)
                    while len(esems) <= ep:
                        esems.append(newsem(f"e{e}{len(esems)}"))
                    o.sem = esems[ep]
                    o.val = v + 1
                    o.inc = 1
                    ci += 1
        for e in self.ENGS:
            wd = {}
            for oid in order[e]:
                o = ops[oid]
                waits = []
                for d in sorted(o.deps):
                    do = ops[d]
                    if (not do.is_dma) and (not o.is_dma) and do.eng == e and e == "pe":
                        continue
                    sid = self.sem_id[id(do.sem)]
                    if wd.get(sid, 0) >= do.val:
                        continue
                    wd[sid] = do.val
                    waits.append((do.sem, do.val))
                if o.is_dma and o.val > 16:
                    sid = self.sem_id[id(o.sem)]
                    if wd.get(sid, 0) < o.val - 16:
                        wd[sid] = o.val - 16
                        waits.append((o.sem, o.val - 16))
                o.waits = waits
        fin = {}
        for o in ops:
            if o.is_output:
                sid = self.sem_id[id(o.sem)]
                if fin.get(sid, (None, 0))[1] < o.val:
                    fin[sid] = (o.sem, o.val)
        self.final_waits = list(fin.values())

    def emit(self, block):
        S = self

        def run(engobj, name, final=False):
            for oid in S.order[name]:
                o = S.ops[oid]
                for (sm, v) in o.waits:
                    engobj.wait_ge(sm, v)
                o.fn(engobj).then_inc(o.sem, o.inc)
            if final:
                for (sm, v) in S.final_waits:
                    engobj.wait_ge(sm, v)

        @block.tensor
        def _(e):
            run(e, "pe")

        @block.scalar
        def _(e):
            run(e, "act")

        @block.vector
        def _(e):
            run(e, "dve")

        @block.gpsimd
        def _(e):
            run(e, "pool")

        @block.sync
        def _(e):
            run(e, "sp", final=True)


class _Stop(Exception):
    pass


def build_program(NT, with_sample=True, stop=None):
    SEQ = NT * 512
    NKB = max(4 * NT, 9)
    nc = bass.Bass("TRN2", target_bir_lowering=False)

    def din(name, shape, dt=F32):
        return nc.dram_tensor(name, shape, dt, kind="ExternalInput").ap()

    def dout(name, shape):
        return nc.dram_tensor(name, shape, F32, kind="ExternalOutput").ap()

    xp = din("xp", [SEQ, D]); xs = din("xs", [32, D])
    ck = din("ck", [1024, 512]); cv = din("cv", [1024, 512]); cl = din("cl", [1024, 8])
    sth = din("sth", [4, 128, 128])
    w_in = din("w_in", [D, INC]); w_out = din("w_out", [D, D]); w_up = din("w_up", [D, DFF]); w_down = din("w_down", [DFF, D])
    n1T = din("n1T", [128, 8]); n2T = din("n2T", [128, 8]); gnT = din("gnT", [128, 4]); lbl = din("lbl", [128, 8])
    qg = din("qg", [1, 64]); kg = din("kg", [1, 64]); bfx = din("bfx", [1, 8])
    c_ident = din("c_ident", [128, 128]); c_tri = din("c_tri", [128, 128]); c_maskA = din("c_maskA", [128, 128])
    c_maskD = din("c_maskD", [128, 2048])

    yp = dout("yp", [SEQ, D]); ys = dout("ys", [32, D])
    nk = dout("nk", [SEQ, 512]); nv = dout("nv", [SEQ, 512]); nl = dout("nl", [SEQ, 8]); nh = dout("nh", [4, 128, 128])
    nks = dout("nks", [32, 512]); nvs = dout("nvs", [32, 512]); nls = dout("nls", [32, 8]); nhs = dout("nhs", [4, 128, 128])

    wib = nc.dram_tensor("wib", [7, 128, 4096], BF16).ap()
    wif = nc.dram_tensor("wif", [128, 64], BF16).ap()
    woA = nc.dram_tensor("woA", [2, 128, 2048], BF16).ap()
    woB = nc.dram_tensor("woB", [2, 64, 4096], BF16).ap()
    wub = nc.dram_tensor("wub", [8, 128, 4096], BF16).ap()
    wdb = nc.dram_tensor("wdb", [8, 128, 4096], BF16).ap()
    kTp = nc.dram_tensor("kTp", [4, 128, SEQ], BF16).ap()
    kTs = nc.dram_tensor("kTs", [4, 128, 1536], BF16).ap()

    with ExitStack() as st:
        def sb(name, shape, dt=F32):
            return st.enter_context(nc.sbuf_tensor(name, shape, dt))

        S = Sched(nc, st, reorder=REORDER)
        vaug = sb("vaug", [128, NKB, 8, 65], BF16)
        cT = sb("cT", [128, NKB, 8])
        carry = sb("carry", [128, 8]); cref = sb("cref", [128, 8])
        NKTB = 3
        ktb = [sb(f"ktb{i}", [128, 512], BF16) for i in range(NKTB)]
        qT = sb("qT", [128, 2, 4, 512], BF16)
        ktst = sb("ktst", [128, 4, 512], BF16)
        xblk = [sb(f"xblk{i}", [128, D]) for i in range(2)]
        hb0_ = sb("hb0", [128, D], BF16)
        hb = [hb0_, hb0_]
        hT = sb("hT", [128, 8, 512], BF16)
        NW = 3
        wbuf = [sb(f"wbuf{i}", [128, 4096], BF16) for i in range(NW)]
        scrA = sb("scrA", [128, 8, 512])
        scrB = sb("scrB", [128, 7, 512])
        qrT = sb("qrT", [128, 4, 512], BF16); krT = sb("krT", [128, 4, 512], BF16)
        krTok = sb("krTok", [128, 4, 512], BF16)
        AmT = sb("AmT", [128, 4, 4, 128], BF16)
        vtok = sb("vtok", [128, 4, 512], BF16)
        S32 = sb("S32", [128, 4, 128]); Sbf = sb("Sbf", [128, 4, 128], BF16); utmp = sb("utmp", [128, 4, 128])
        mixA2 = sb("mixA", [128, 2, 4, 512], BF16)
        mixB = sb("mixB", [128, 8, 512], BF16)
        pT = [sb(f"pT{i}", [128, 512], BF16) for i in range(3)]
        identb = sb("identb", [128, 128], BF16); tri = sb("tri", [128, 128]); onesf = sb("onesf", [128, 128])
        maskA = sb("maskA", [128, 128]); maskD = sb("maskD", [128, 4, 512], BF16); rmask = sb("rmask", [128, 512])
        n1s = sb("n1s", [128, 8]); n2s = sb("n2s", [128, 8]); gns = sb("gns", [128, 4]); lbls = sb("lbls", [128, 8])
        lb = sb("lb", [128, 4]); oml = sb("oml", [128, 4])
        qgb = sb("qgb", [128, 64]); kgb = sb("kgb", [128, 64]); bfb = sb("bfb", [128, 8])
        sc_e = sb("sc_e", [128, 4, 8]); sc_a = sb("sc_a", [128, 4, 8]); sc_g = sb("sc_g", [128, 4, 8])
        ss1 = [sb(f"ss1_{i}", [128, 1]) for i in range(2)]
        ss8 = [sb(f"ss8_{i}", [128, 8]) for i in range(2)]
        lf = [sb(f"lf{i}", [128, 8]) for i in range(2)]
        lft = sb("lft", [128, 8])

        PS = [st.enter_context(nc.psum_tensor(f"ps{i}", [128, 512], F32)) for i in range(8)]
        PSB = [p.bitcast(BF16) for p in PS]
        rot_state = [0]

        def rot():
            i = BANKS["rot"][rot_state[0] % len(BANKS["rot"])]
            rot_state[0] += 1
            return i

        rotH_state = [0]
        rotX_state = [0]

        def rotH():
            i = BANKS["H"][rotH_state[0] % len(BANKS["H"])]
            rotH_state[0] += 1
            return i

        def rotX():
            i = BANKS["X"][rotX_state[0] % len(BANKS["X"])]
            rotX_state[0] += 1
            return i

        rotA_state = [0]
        rotF_state = [0]

        def rotA():
            i = BANKS["A"][rotA_state[0] % len(BANKS["A"])]
            rotA_state[0] += 1
            return i

        def rotF():
            i = BANKS["F"][rotF_state[0] % len(BANKS["F"])]
            rotF_state[0] += 1
            return i

        block = st.enter_context(nc.Block())

        def chk(name):
            if stop == name:
                raise _Stop()
        cnt = {"w": 0, "kt": 0, "x": 0, "p": 0, "s": 0, "cast": 0}

        XT = lambda i: [("xb", i, 0), ("xb", i, 1)]

        def A(i):
            return scrA[:, i, :]

        def B(i):
            return scrB[:, i, :]

        S.dma("sp", tri[:, :], c_tri, writes=["tri"])
        S.dma("sp", maskA[:, :], c_maskA, writes=["maskA"])
        S.dma("sp", A(0)[:, 0:128], c_ident, writes=[("A", 0)])
        S.op("dve", lambda e: e.tensor_copy(identb[:, :], A(0)[:, 0:128]), reads=[("A", 0)], writes=["identb"])
        for j in range(4):
            S.dma("sp", A(1 + j), c_maskD[:, j * 512:(j + 1) * 512], writes=[("A", 1 + j)])
            S.op("dve", lambda e, j=j: e.tensor_copy(maskD[:, j, :], A(1 + j)), reads=[("A", 1 + j)], writes=["maskD"])
        S.op("pool", lambda e: e.memset(onesf[:, :], 1.0), writes=["onesf"])
        S.op("pool", lambda e: e.memset(cT[:, :, :].rearrange("p a b -> p (a b)"), 0.0), writes=[("cT", k) for k in range(NKB)])
        S.op("pool", lambda e: e.memset(mixB[:, :, :].rearrange("p a b -> p (a b)"), 0.0), writes=[("mixB", h) for h in range(8)])
        for wi_ in (1, 2):
            S.op("pool", lambda e, wi_=wi_: e.memset(wbuf[wi_][:, :], 0.0), writes=[("wbuf", wi_)])
        S.op("pool", lambda e: e.memset(qT[:, :, :, :].rearrange("p a b c -> p (a b c)"), 0.0), writes=["qT"])
        S.op("pool", lambda e: e.memset(rmask[:, :], 1.0), writes=["rmask"])
        S.op("pool", lambda e: e.memset(rmask[:, :].rearrange("p (c s) -> p c s", s=64)[:, :, 0:1], 0.0),
             reads=["rmask"], writes=["rmask"])
        S.op("pool", lambda e: e.memset(vaug[:, :, :, :].rearrange("p a b c -> p (a b c)"), 1.0),
             writes=[("vaug", k) for k in range(NKB)])
        for (t, src, nm) in [(n1s, n1T, "n1s"), (n2s, n2T, "n2s"), (gns, gnT, "gns"), (lbls, lbl, "lbls")]:
            S.dma("sp", t[:, :], src, writes=[nm])
        S.dma("sp", qgb[:, :], qg.partition_broadcast(128), writes=["qgb"])
        S.dma("sp", kgb[:, :], kg.partition_broadcast(128), writes=["kgb"])
        S.dma("sp", bfb[:, :], bfx.partition_broadcast(128), writes=["bfb"])
        S.op("dve", lambda e: e.tensor_scalar(qgb[:, :], qgb[:, :], 0.125, None, ALU.mult), reads=["qgb"], writes=["qgb"])
        S.op("dve", lambda e: e.tensor_tensor(lb[:, :], lbls[:, 4:8], lbls[:, 0:4], ALU.subtract), reads=["lbls"], writes=["lb"])
        S.op("act", lambda e: e.activation(lb[:, :], lb[:, :], AF.Exp), reads=["lb"], writes=["lb"])
        S.op("dve", lambda e: e.tensor_scalar(lb[:, :], lb[:, :], 1.0, None, ALU.add), reads=["lb"], writes=["lb"])
        S.op("dve", lambda e: e.reciprocal(lb[:, :], lb[:, :]), reads=["lb"], writes=["lb"])
        S.op("dve", lambda e: e.tensor_scalar(oml[:, :], lb[:, :], -1.0, 1.0, ALU.mult, ALU.add), reads=["lb"], writes=["oml"])

        wtoks = {}

        def convert(src, rows, cols, tokname, stores):
            for r0 in range(0, rows, 128):
                for c0 in range(0, cols, 1024):
                    n = min(1024, cols - c0)
                    i = cnt["cast"]; cnt["cast"] += 1
                    sa = i % 4
                    fst = scrA[:, 2 * sa:2 * sa + 2, :].rearrange("p a b -> p (a b)")
                    bst = wbuf[0][:, sa * 1024:(sa + 1) * 1024]
                    S.dma("sp", fst[:, 0:n], src[r0:r0 + 128, c0:c0 + n], writes=[("A", 2 * sa), ("A", 2 * sa + 1)])
                    eng = ("dve", "act")[i % 2]
                    if eng == "act":
                        fn = lambda e, fst=fst, bst=bst, n=n: e.activation(bst[:, 0:n], fst[:, 0:n], AF.Copy)
                    else:
                        fn = lambda e, fst=fst, bst=bst, n=n: e.tensor_copy(bst[:, 0:n], fst[:, 0:n])
                    S.op(eng, fn, reads=[("A", 2 * sa), ("A", 2 * sa + 1)], writes=[("wst", sa)])
                    for j, (d_ap, s_ap) in enumerate(stores(r0, c0, n, bst)):
                        S.dma("pool", d_ap, s_ap, reads=[("wst", sa)], writes=[(tokname, i, j)])
                        wtoks.setdefault(tokname, []).append((tokname, i, j))

        def st_in(r0, c0, n, bst):
            kc = r0 // 128
            g0 = c0 // 512
            res = [(wib[g0][:, kc * 512:(kc + 1) * 512], bst[:, 0:512])]
            if n == 1024:
                res.append((wib[g0 + 1][:, kc * 512:(kc + 1) * 512], bst[:, 512:1024]))
            else:
                res.append((wif[:, kc * 8:(kc + 1) * 8], bst[:, 512:520]))
            return res

        def st_out(r0, c0, n, bst):
            if r0 < 512:
                h = r0 // 128
                return [(woA[nh_][:, h * 512:(h + 1) * 512], bst[:, nh_ * 512:(nh_ + 1) * 512]) for nh_ in range(2)]
            j = (r0 - 512) // 128
            res = []
            for nh_ in range(2):
                res.append((woB[nh_][:, (2 * j) * 512:(2 * j + 1) * 512], bst[0:64, nh_ * 512:(nh_ + 1) * 512]))
                res.append((woB[nh_][:, (2 * j + 1) * 512:(2 * j + 2) * 512], bst[64:128, nh_ * 512:(nh_ + 1) * 512]))
            return res

        def st_up(r0, c0, n, bst):
            kc = r0 // 128
            g0 = c0 // 512
            return [(wub[g0 + t][:, kc * 512:(kc + 1) * 512], bst[:, t * 512:(t + 1) * 512]) for t in range(2)]

        def st_down(r0, c0, n, bst):
            fg, c = divmod(r0 // 128, 4)
            return [(wdb[fg][:, c * 1024:(c + 1) * 1024], bst[:, 0:1024])]

        if stop == "setup":
            wtoks.update({k: [] for k in ("wib", "wob", "wub", "wdb")})
        else:
            convert(w_in, D, INC, "wib", st_in)
            convert(w_out, D, D, "wob", st_out)
            convert(w_up, D, DFF, "wub", st_up)
            convert(w_down, DFF, D, "wdb", st_down)
        WT = lambda i: [("wbuf", i)] + ([("wst", s) for s in range(4)] if i == 0 else [])

        WPOOLS = {"all": [0, 1, 2], "h": [0, 1], "f": [2]}
        wcnt = {"all": 0, "h": 0, "f": 0}

        def load_w(view_out, view_in, srctok, pool="all"):
            i = WPOOLS[pool][wcnt[pool] % len(WPOOLS[pool])]
            wcnt[pool] += 1
            S.dma("sp", view_out(wbuf[i]), view_in, reads=wtoks[srctok], writes=WT(i))
            return i

        def rstd_small(t, n_parts, cols, inv_n):
            pass

        def fox_cumsum(lfb, TB, kb, tokl):
            p = rotX()
            S.op("pe", lambda e: e.matmul(PS[p][0:TB, 0:8], tri[0:TB, 0:TB], lfb[0:TB, :], start=True, stop=True),
                 reads=["tri", tokl], writes=[("ps", p)])
            S.op("pe", lambda e: e.matmul(PS[p][:, 8:16], onesf[0:TB, :], lfb[0:TB, :], start=True, stop=True),
                 reads=["onesf", tokl], writes=[("ps", p)])
            S.op("dve", lambda e: e.tensor_tensor(cT[0:TB, kb, :], PS[p][0:TB, 0:8], carry[0:TB, :], ALU.add),
                 reads=[("ps", p), "carry"], writes=[("cT", kb)])
            S.op("dve", lambda e: e.tensor_tensor(carry[:, :], PS[p][:, 8:16], carry[:, :], ALU.add),
                 reads=[("ps", p), "carry"], writes=["carry"])

        def kT_transposes(knb_ap, TB, col0, tokk):
            p = rotX()
            for pr in range(4):
                S.op("pe", lambda e, pr=pr: e.transpose(PSB[p][:, pr * 128:pr * 128 + TB], knb_ap[0:TB, pr * 128:(pr + 1) * 128],
                                                      identb[0:TB, 0:TB]),
                     reads=[tokk, "identb"], writes=[("ps", p)])
            S.op("act", lambda e: e.activation(ktst[:, :, col0:col0 + TB],
                                               PSB[p][:, 0:512].rearrange("p (a b) -> p a b", b=128)[:, :, 0:TB], AF.Copy),
                 reads=[("ps", p)], writes=["ktst"])

        def norm_qk(src_ps, p, TB, gain, gtok, out32, out32tok, outbf, outbftok):
            i = cnt["s"] % 2; cnt["s"] += 1
            sq = B(0)
            S.op("act", lambda e: e.activation(sq[0:TB, :], src_ps[0:TB, :], AF.Square), reads=[("ps", p)], writes=[("B", 0)])
            S.op("dve", lambda e: e.tensor_reduce(ss8[i][0:TB, :], sq[0:TB, :].rearrange("p (h d) -> p h d", d=64), AX.X, ALU.add),
                 reads=[("B", 0)], writes=[("ss8", i)])
            S.op("act", lambda e: e.activation(ss8[i][0:TB, :], ss8[i][0:TB, :], AF.Ln, bias=EPS, scale=1.0 / 64),
                 reads=[("ss8", i)], writes=[("ss8", i)])
            S.op("act", lambda e: e.activation(ss8[i][0:TB, :], ss8[i][0:TB, :], AF.Exp, scale=-0.5),
                 reads=[("ss8", i)], writes=[("ss8", i)])
            S.op("dve", lambda e: e.tensor_tensor(sq[0:TB, :].rearrange("p (h d) -> p h d", d=64),
                                                  src_ps[0:TB, :].rearrange("p (h d) -> p h d", d=64),
                                                  ss8[i][0:TB, :].unsqueeze(2).to_broadcast([TB, 8, 64]), ALU.mult),
                 reads=[("ps", p), ("ss8", i)], writes=[("B", 0)])
            dst = out32 if out32 is not None else outbf
            dtok = out32tok if out32 is not None else outbftok
            S.op("dve", lambda e: e.tensor_tensor(dst[0:TB, :].rearrange("p (h d) -> p h d", d=64),
                                                  sq[0:TB, :].rearrange("p (h d) -> p h d", d=64),
                                                  gain[0:TB, :].unsqueeze(1).to_broadcast([TB, 8, 64]), ALU.mult),
                 reads=[("B", 0), gtok], writes=[dtok])
            if out32 is not None:
                S.op("pool", lambda e: e.tensor_copy(outbf[0:TB, :], out32[0:TB, :]), reads=[out32tok], writes=[outbftok])

        def tile(T, x_src, t0, kb0, kT_dram, key0, outs, g, sample):
            y_dst, nk_dst, nv_dst, nl_dst = outs
            TB = min(T, 128)
            nblk = T // TB
            CL = 64 if T >= 64 else T
            nch = T // CL
            tg = ("t", g)
            mb = (g % 2) if isinstance(g, int) else 0
            if SIM_PARITY is not None and not isinstance(g, int):
                SIM_PARITY[0] = 0
            if SIM_PARITY is not None:
                SIM_PARITY[1] = True
            mixA = mixA2[:, mb, :, :]

            for tb in range(nblk):
                i = cnt["x"] % 2; cnt["x"] += 1
                r0 = t0 + tb * TB
                S.dma("sp", xblk[i][0:TB, :], x_src[r0:r0 + TB, :], writes=XT(i))
                S.op("pool", lambda e, i=i: e.memset(ss1[i][:, :], 0.0), reads=[("ss1", i)], writes=[("ss1", i)])
                S.op("act", lambda e, i=i: e.activation(hb[i][0:TB, :], xblk[i][0:TB, :], AF.Square, accum_out=ss1[i][0:TB, :]),
                     reads=XT(i), writes=[("hb", 0), ("ss1", i)])
                S.op("act", lambda e, i=i: e.activation(ss1[i][0:TB, :], ss1[i][0:TB, :], AF.Ln, bias=EPS, scale=1.0 / D),
                     reads=[("ss1", i)], writes=[("ss1", i)])
                S.op("act", lambda e, i=i: e.activation(ss1[i][0:TB, :], ss1[i][0:TB, :], AF.Exp, scale=-0.5),
                     reads=[("ss1", i)], writes=[("ss1", i)])
                S.op("dve", lambda e, i=i: e.tensor_scalar(hb[i][0:TB, :], xblk[i][0:TB, :], ss1[i][0:TB, 0:1], None, ALU.mult),
                     reads=XT(i) + [("ss1", i)], writes=[("hb", 0)])
                p = rotA()
                for kc in range(8):
                    S.op("pe", lambda e, i=i, kc=kc, p=p: e.transpose(PSB[p][:, kc * 128:kc * 128 + TB], hb[i][0:TB, kc * 128:(kc + 1) * 128],
                                                                    identb[0:TB, 0:TB]),
                         reads=[("hb", 0), "identb"], writes=[("ps", p)])
                S.op("dve", lambda e, p=p, tb=tb: e.tensor_tensor(hT[:, :, tb * 128:tb * 128 + TB],
                                                                  PSB[p][:, :].rearrange("p (a b) -> p a b", b=128)[:, :, 0:TB],
                                                                  n1s[:, :].unsqueeze(2).to_broadcast([128, 8, TB]), ALU.mult),
                     reads=[("ps", p), "n1s"], writes=["hT"])

            chk("p1a")

            def w_group(c0, ncol, pool="h"):
                if SEQ_P1:
                    pool = "all"
                if ncol == 8:
                    return load_w(lambda w: w[:, 0:64], wif[:, :], "wib", pool)
                return load_w(lambda w: w[:, 0:4096], wib[c0 // 512][:, :], "wib", pool)

            def mm_feat(wi, h, p):
                wv = wbuf[wi][:, 0:4096].rearrange("p (k n) -> p k n", n=512)
                for kc in range(8):
                    S.op("pe", lambda e, kc=kc: e.matmul(PS[p][:, 0:T], wv[:, kc, h * 128:(h + 1) * 128], hT[:, kc, 0:T],
                                                         start=(kc == 0), stop=(kc == 7)),
                         reads=[("wbuf", wi), "hT"], writes=[("ps", p)])

            def mm_tok(wi, tb, p, ncol):
                wv = wbuf[wi][:, 0:8 * ncol].rearrange("p (k n) -> p k n", n=ncol)
                for kc in range(8):
                    S.op("pe", lambda e, kc=kc: e.matmul(PS[p][0:TB, 0:ncol], hT[:, kc, tb * 128:tb * 128 + TB], wv[:, kc, 0:ncol],
                                                         start=(kc == 0), stop=(kc == 7)),
                         reads=[("wbuf", wi), "hT"], writes=[("ps", p)])

            TS = [[(A(4), ("A", 4)), (A(5), ("A", 5)), (A(6), ("A", 6)), (A(7), ("A", 7))],
                  [(xblk[0][:, 0:512], ("xb", 0, 0)), (xblk[0][:, 512:1024], ("xb", 0, 1)),
                   (xblk[1][:, 0:512], ("xb", 1, 0)), (xblk[1][:, 512:1024], ("xb", 1, 1))]]
            def hgrn_pre():
                wi = w_group(512, 512)
                for hp in range(2):
                    hs = [2 * hp, 2 * hp + 1]
                    pp = {}
                    for h in hs:
                        pp[h] = rotH()
                        mm_feat(wi, h, pp[h])

                    def st(fn):
                        for h in hs:
                            (ta, tka), (tb_, tkb), (tc, tkc), (td, tkd) = TS[h % 2]
                            fn(h, pp[h], ta, tka, tb_, tkb, tc, tkc, td, tkd)

                    def s1(h, p, ta, tka, tb_, tkb, tc, tkc, td, tkd):
                        S.op("act", lambda e: e.activation(ta[:, 0:T], PS[p][:, 0:T], AF.Exp, scale=-1.0), reads=[("ps", p)], writes=[tka])
                        S.op("act", lambda e: e.activation(ta[:, 0:T], ta[:, 0:T], AF.Ln, bias=1.0), reads=[tka], writes=[tka])
                        S.op("act", lambda e: e.activation(ta[:, 0:T], ta[:, 0:T], AF.Exp, scale=-1.0), reads=[tka], writes=[tka])
                    st(s1)
                    yield

                    def s2(h, p, ta, tka, tb_, tkb, tc, tkc, td, tkd):
                        S.op("pool", lambda e: e.tensor_scalar(tb_[:, 0:T], ta[:, 0:T], oml[:, h:h + 1], lb[:, h:h + 1], ALU.mult, ALU.add),
                             reads=[tka, "oml", "lb"], writes=[tkb])
                    st(s2)
                    yield

                    def s3(h, p, ta, tka, tb_, tkb, tc, tkc, td, tkd):
                        S.op("act", lambda e: e.activation(tc[:, 0:T], tb_[:, 0:T], AF.Ln), reads=[tkb], writes=[tkc])
                        S.op("pool", lambda e: e.tensor_scalar(ta[:, 0:T], tb_[:, 0:T], -1.0, 1.0, ALU.mult, ALU.add), reads=[tkb], writes=[tka])
                    st(s3)
                    yield

                    def s4(h, p, ta, tka, tb_, tkb, tc, tkc, td, tkd):
                        S.op("dve", lambda e: e.tensor_tensor_scan(td[:, 0:T], rmask[:, 0:T], tc[:, 0:T], 0.0, ALU.mult, ALU.add),
                             reads=[tkc, "rmask"], writes=[tkd])
                        tdv = td[:, 0:T].rearrange("p (c s) -> p c s", s=CL)
                        bref = tdv[:, :, CL // 2:CL // 2 + 1]
                        S.op("dve", lambda e: e.tensor_tensor(tb_[:, 0:T].rearrange("p (c s) -> p c s", s=CL), tdv,
                                                              bref.to_broadcast([128, nch, CL]), ALU.subtract),
                             reads=[tkd], writes=[tkb])
                    st(s4)
                    yield

                    def s5(h, p, ta, tka, tb_, tkb, tc, tkc, td, tkd):
                        E1 = A(h)
                        S.op("act", lambda e: e.activation(E1[:, 0:T], tb_[:, 0:T], AF.Exp), reads=[tkb], writes=[("A", h)])
                        S.op("act", lambda e: e.activation(tc[:, 0:T], tb_[:, 0:T], AF.Exp, scale=-1.0), reads=[tkb], writes=[tkc])
                    st(s5)
                    yield

                    def s6(h, p, ta, tka, tb_, tkb, tc, tkc, td, tkd):
                        tdv = td[:, 0:T].rearrange("p (c s) -> p c s", s=CL)
                        bref = tdv[:, :, CL // 2:CL // 2 + 1]
                        blast = tdv[:, :, CL - 1:CL]
                        S.op("pool", lambda e: e.tensor_tensor(krT[:, h, 0:T], ta[:, 0:T], tc[:, 0:T], ALU.mult),
                             reads=[tka, tkc], writes=[("krT", h)])
                        S.op("act", lambda e: e.activation(sc_e[:, h, 0:nch].unsqueeze(2), bref, AF.Exp), reads=[tkd], writes=[("sc_e", h)])
                        S.op("act", lambda e: e.activation(sc_a[:, h, 0:nch].unsqueeze(2), blast, AF.Exp), reads=[tkd], writes=[("sc_a", h)])
                        S.op("dve", lambda e: e.tensor_tensor(sc_g[:, h, 0:nch].unsqueeze(2), blast, bref, ALU.subtract),
                             reads=[tkd], writes=[("sc_g", h)])
                        S.op("act", lambda e: e.activation(sc_g[:, h, 0:nch], sc_g[:, h, 0:nch], AF.Exp), reads=[("sc_g", h)], writes=[("sc_g", h)])
                    st(s6)
                    yield
                wi = w_group(0, 512)
                for h in range(4):
                    p = rotH()
                    mm_feat(wi, h, p)
                    S.op("dve", lambda e, h=h, p=p: e.tensor_tensor(qrT[:, h, 0:T], PS[p][:, 0:T], A(h)[:, 0:T], ALU.mult),
                         reads=[("ps", p), ("A", h)], writes=[("qrT", h)])
                    yield
                wi = w_group(1024, 512)
                for tb in range(nblk):
                    p = rotH()
                    mm_tok(wi, tb, p, 512)
                    S.op("act", lambda e, tb=tb, p=p: e.activation(vtok[0:TB, tb, :], PS[p][0:TB, :], AF.Copy),
                         reads=[("ps", p)], writes=[("vtok", tb)])
                    yield
                for tb in range(nblk):
                    p = rotH()
                    for h in range(4):
                        S.op("pe", lambda e, h=h, p=p, tb=tb: e.transpose(PSB[p][0:TB, h * 128:(h + 1) * 128], krT[:, h, tb * 128:tb * 128 + TB],
                                                                        identb[:, :]),
                             reads=[("krT", h), "identb"], writes=[("ps", p)])
                    S.op("act", lambda e, p=p, tb=tb: e.activation(krTok[0:TB, tb, :], PSB[p][0:TB, 0:512], AF.Copy),
                         reads=[("ps", p)], writes=[("krTok", tb)])
                    p2 = rotH()
                    for h in range(4):
                        S.op("pe", lambda e, h=h, p2=p2, tb=tb: e.matmul(PS[p2][0:TB, h * 128:h * 128 + TB], krT[:, h, tb * 128:tb * 128 + TB],
                                                                       qrT[:, h, tb * 128:tb * 128 + TB], start=True, stop=True),
                             reads=[("krT", h), ("qrT", h)], writes=[("ps", p2)])
                    S.op("dve", lambda e, p2=p2, tb=tb: e.tensor_tensor(AmT[0:TB, tb, :, 0:TB],
                                                                        PS[p2][0:TB, :].rearrange("p (h c) -> p h c", c=128)[:, :, 0:TB],
                                                                        maskA[0:TB, 0:TB].unsqueeze(1).to_broadcast([TB, 4, TB]), ALU.mult),
                         reads=[("ps", p2), "maskA"], writes=[("AmT", tb)])
                    yield

            chk("p1b")
            def recurrence():
                for c in range(nch):
                    tb = (c * CL) // 128
                    off = (c * CL) % 128
                    S.op("pool", lambda e, c=c: e.tensor_tensor(Sbf[:, :, :], S32[:, :, :],
                                                                sc_e[:, :, c:c + 1].to_broadcast([128, 4, 128]), ALU.mult),
                         reads=["S32"] + [("sc_e", h) for h in range(4)], writes=["Sbf"])
                    for h in range(4):
                        S.op("pe", lambda e, h=h, c=c: e.matmul(PS[h][:, c * CL:(c + 1) * CL], Sbf[:, h, :], qrT[:, h, c * CL:(c + 1) * CL],
                                                                start=True, stop=False),
                             reads=["Sbf", ("qrT", h)], writes=[("ps", h)])
                        S.op("pe", lambda e, h=h, c=c, tb=tb, off=off: e.matmul(PS[h][:, c * CL:(c + 1) * CL],
                                                                               vtok[off:off + CL, tb, h * 128:(h + 1) * 128],
                                                                               AmT[off:off + CL, tb, h, off:off + CL], start=False, stop=True),
                             reads=[("vtok", tb), ("AmT", tb)], writes=[("ps", h)])
                    pu = rotH()
                    for h in range(4):
                        S.op("pe", lambda e, h=h, pu=pu, tb=tb, off=off: e.matmul(PS[pu][:, h * 128:(h + 1) * 128],
                                                                                 krTok[off:off + CL, tb, h * 128:(h + 1) * 128],
                                                                                 vtok[off:off + CL, tb, h * 128:(h + 1) * 128], start=True, stop=True),
                             reads=[("krTok", tb), ("vtok", tb)], writes=[("ps", pu)])
                    S.op("dve", lambda e, pu=pu, c=c: e.tensor_tensor(utmp[:, :, :], PS[pu][:, :].rearrange("p (h v) -> p h v", v=128),
                                                                      sc_g[:, :, c:c + 1].to_broadcast([128, 4, 128]), ALU.mult),
                         reads=[("ps", pu)] + [("sc_g", h) for h in range(4)], writes=["utmp"])
                    S.op("dve", lambda e, c=c: e.tensor_tensor(S32[:, :, :], S32[:, :, :], sc_a[:, :, c:c + 1].to_broadcast([128, 4, 128]), ALU.mult),
                         reads=["S32"] + [("sc_a", h) for h in range(4)], writes=["S32"])
                    S.op("dve", lambda e: e.tensor_tensor(S32[:, :, :], S32[:, :, :], utmp[:, :, :], ALU.add),
                         reads=["S32", "utmp"], writes=["S32"])
                    yield

            def fox_proj():
                kb_of = lambda tb: kb0 + tb
                wi = w_group(2048, 512, "f")
                for tb in range(nblk):
                    p = rotX()
                    mm_tok(wi, tb, p, 512)
                    j = cnt["p"] % 2; cnt["p"] += 1
                    qn = scrB[:, 5 + j, 0:256].bitcast(BF16)
                    norm_qk(PS[p], p, TB, qgb, "qgb", None, None, qn, ("Bq", 5 + j))
                    p2 = rotX()
                    for pr in range(4):
                        S.op("pe", lambda e, pr=pr, p2=p2, qn=qn: e.transpose(PSB[p2][:, pr * 128:pr * 128 + TB], qn[0:TB, pr * 128:(pr + 1) * 128],
                                                                            identb[0:TB, 0:TB]),
                             reads=[("Bq", 5 + j), "identb"], writes=[("ps", p2)])
                    S.op("act", lambda e, p2=p2, tb=tb: e.activation(qT[0:64, 0, :, tb * 128:tb * 128 + TB],
                                                                     PSB[p2][0:64, 0:512].rearrange("p (a b) -> p a b", b=128)[:, :, 0:TB], AF.Copy),
                         reads=[("ps", p2)], writes=["qT"])
                    S.op("pool" if False else "dve", lambda e, p2=p2, tb=tb: e.tensor_copy(qT[64:128, 1, :, tb * 128:tb * 128 + TB],
                                                                     PSB[p2][64:128, 0:512].rearrange("p (a b) -> p a b", b=128)[:, :, 0:TB]),
                         reads=[("ps", p2)], writes=["qT"])
                    yield
                wi = w_group(2560, 512, "f")
                for tb in range(nblk):
                    p = rotX()
                    mm_tok(wi, tb, p, 512)
                    j = cnt["p"] % 2; cnt["p"] += 1
                    kn32 = B(1 + j)
                    knb = scrB[:, 5 + j, 256:512].bitcast(BF16)
                    norm_qk(PS[p], p, TB, kgb, "kgb", kn32, ("B", 1 + j), knb, ("Bk", 5 + j))
                    r0 = t0 + tb * TB
                    S.dma("pool", nk_dst[r0:r0 + TB, :], kn32[0:TB, :], reads=[("B", 1 + j)], is_output=True)
                    kT_transposes(knb, TB, tb * 128, ("Bk", 5 + j))
                    yield
                if stop == "foxn3a":
                    raise _Stop()
                S.dma("sp" if stop == "foxn3b" else "pool", kT_dram[:, :, key0:key0 + T].rearrange("q p k -> p q k"), ktst[:, :, 0:T], reads=["ktst"],
                      writes=[("kTd", sample, key0 // 512)])
                if stop == "foxn3c":
                    raise _Stop()
                wi = w_group(3072, 512, "f")
                for tb in range(nblk):
                    p = rotX()
                    mm_tok(wi, tb, p, 512)
                    j = cnt["p"] % 2; cnt["p"] += 1
                    v32 = B(3 + j)
                    S.op("act", lambda e, p=p, v32=v32: e.activation(v32[0:TB, :], PS[p][0:TB, :], AF.Copy), reads=[("ps", p)], writes=[("B", 3 + j)])
                    r0 = t0 + tb * TB
                    S.dma("pool", nv_dst[r0:r0 + TB, :], v32[0:TB, :], reads=[("B", 3 + j)], is_output=True)
                    S.op("dve", lambda e, v32=v32, tb=tb: e.tensor_copy(vaug[0:TB, kb_of(tb), :, 0:64],
                                                                        v32[0:TB, :].rearrange("p (h d) -> p h d", d=64)),
                         reads=[("B", 3 + j)], writes=[("vaug", kb_of(tb))])
                    yield
                wi = w_group(3584, 8, "f")
                for tb in range(nblk):
                    p = rotX()
                    mm_tok(wi, tb, p, 8)
                    j = cnt["p"] % 2; cnt["p"] += 1
                    S.op("dve", lambda e, p=p: e.tensor_tensor(lft[0:TB, :], PS[p][0:TB, 0:8], bfb[0:TB, :], ALU.add),
                         reads=[("ps", p), "bfb"], writes=["lft"])
                    S.op("act", lambda e: e.activation(lft[0:TB, :], lft[0:TB, :], AF.Exp, scale=-1.0), reads=["lft"], writes=["lft"])
                    S.op("act", lambda e: e.activation(lft[0:TB, :], lft[0:TB, :], AF.Ln, bias=1.0), reads=["lft"], writes=["lft"])
                    S.op("dve", lambda e, j=j: e.tensor_scalar(lf[j][0:TB, :], lft[0:TB, :], -1.0, None, ALU.mult),
                         reads=["lft"], writes=[("lf", j)])
                    r0 = t0 + tb * TB
                    S.dma("pool", nl_dst[r0:r0 + TB, :], lf[j][0:TB, :], reads=[("lf", j)], is_output=True)
                    fox_cumsum(lf[j], TB, kb_of(tb), ("lf", j))
                    if (not sample) and tb == 1:
                        S.op("pool", lambda e: e.tensor_copy(cref[:, :], carry[:, :]), reads=["carry"], writes=["cref"])
                    yield

            gh, gr, gf = hgrn_pre(), recurrence(), fox_proj()
            if stop == "p1b":
                for _ in gh:
                    pass
                raise _Stop()
            fox_left = [4 * nblk]

            def fox_step():
                if fox_left[0] > 0:
                    try:
                        next(gf)
                    except StopIteration:
                        fox_left[0] = 0
                        return
                    fox_left[0] -= 1

            nstep = 0
            if SEQ_P1:
                for _ in gh:
                    pass
                for _ in gr:
                    pass
            for _ in gh:
                nstep += 1
                if nstep % 3 == 0 and fox_left[0] > nch:
                    fox_step()
            for _ in gr:
                fox_step()
            for _ in gf:
                pass
            chk("p1c")
            if stop in ("rec_only", "fox_only") or (stop is not None and stop.startswith("foxn")):
                raise _Stop()
            wi = w_group(1536, 512)
            for hp in range(2):
                hs = [2 * hp, 2 * hp + 1]
                pgs = {}

                def st(fn):
                    for h in hs:
                        (ta, tka), (tb_, tkb), (tc, tkc), (td, tkd) = TS[h % 2]
                        fn(h, ta, tka, tb_, tkb, tc, tkc, td, tkd)

                def o1(h, ta, tka, tb_, tkb, tc, tkc, td, tkd):
                    S.op("act", lambda e: e.activation(ta[:, 0:T], PS[h][:, 0:T], AF.Square), reads=[("ps", h)], writes=[tka])
                    pn = rot()
                    S.op("pe", lambda e: e.matmul(PS[pn][:, 0:T], onesf[:, :], ta[:, 0:T], start=True, stop=True),
                         reads=["onesf", tka], writes=[("ps", pn)])
                    S.op("act", lambda e: e.activation(tb_[:, 0:T], PS[pn][:, 0:T], AF.Ln, bias=EPS, scale=1.0 / 128),
                         reads=[("ps", pn)], writes=[tkb])
                    S.op("act", lambda e: e.activation(tb_[:, 0:T], tb_[:, 0:T], AF.Exp, scale=-0.5), reads=[tkb], writes=[tkb])
                    pg = rot()
                    pgs[h] = pg
                    mm_feat(wi, h, pg)
                    S.op("act", lambda e: e.activation(tc[:, 0:T], PS[pg][:, 0:T], AF.Exp, scale=-1.0), reads=[("ps", pg)], writes=[tkc])
                    S.op("act", lambda e: e.activation(tc[:, 0:T], tc[:, 0:T], AF.Ln, bias=1.0), reads=[tkc], writes=[tkc])
                    S.op("act", lambda e: e.activation(tc[:, 0:T], tc[:, 0:T], AF.Exp, scale=-1.0), reads=[tkc], writes=[tkc])
                st(o1)

                def o2(h, ta, tka, tb_, tkb, tc, tkc, td, tkd):
                    pg = pgs[h]
                    S.op("dve", lambda e: e.tensor_tensor(tc[:, 0:T], PS[pg][:, 0:T], tc[:, 0:T], ALU.mult),
                         reads=[("ps", pg), tkc], writes=[tkc])
                    S.op("dve", lambda e: e.tensor_tensor(td[:, 0:T], PS[h][:, 0:T], tb_[:, 0:T], ALU.mult),
                         reads=[("ps", h), tkb], writes=[tkd])
                    S.op("dve", lambda e: e.scalar_tensor_tensor(mixA[:, h, 0:T], td[:, 0:T], gns[:, h:h + 1], tc[:, 0:T], ALU.mult, ALU.mult),
                         reads=[tkd, tkc, "gns"], writes=[("mixA", mb, h)])
                st(o2)

            if DBG == "dumpA" and sample:
                for h in range(4):
                    S.op("act", lambda e, h=h: e.activation(A(4)[:, h * 32:(h + 1) * 32], mixA[:, h, 0:32], AF.Copy),
                         reads=[("mixA", mb, h)], writes=[("A", 4)])
                S.dma("pool", y_dst.rearrange("t (a b) -> (t a) b", b=256)[:, 0:128], A(4)[:, 0:128], reads=[("A", 4)], is_output=True)
                raise _Stop()
            chk("hout")
            if SIM_PARITY is not None:
                SIM_PARITY[1] = False
            yield "p1"
            nkb = kb0 + nblk
            TB5 = [("B", 5), ("Bq", 5), ("Bk", 5)]
            TB6 = [("B", 6), ("Bq", 6), ("Bk", 6)]
            biasg = scrB[:, 6, :].rearrange("p (k h) -> p k h", h=8)
            S.op("dve", lambda e: e.tensor_tensor(biasg[:, 0:nkb, :], cref[:, :].unsqueeze(1).to_broadcast([128, nkb, 8]),
                                                  cT[:, 0:nkb, :], ALU.subtract),
                 reads=["cref"] + [("cT", k) for k in range(nkb)], writes=TB6)
            nkt = (nkb + 3) // 4
            rden = B(4); rdb = B(5)

            LOOK = ATT_LOOK
            loads = [(pr, kt) for pr in range(4) for kt in range(nkt)]
            ktbuf = {}
            nl = [0]

            def ensure_loaded(idx):
                while nl[0] <= min(idx, len(loads) - 1):
                    pr_, kt_ = loads[nl[0]]
                    bi = cnt["kt"] % NKTB; cnt["kt"] += 1
                    kcols = min(512, (nkb - kt_ * 4) * 128)
                    if sample and kt_ == nkt - 1:
                        kcols = T
                    S.dma("sp", ktb[bi][:, 0:kcols], kT_dram[pr_, :, kt_ * 512:kt_ * 512 + kcols],
                          reads=[("kTd", sample, kt_)], writes=[("ktb", bi)])
                    ktbuf[(pr_, kt_)] = bi
                    nl[0] += 1

            def fin(pr):
                accb = (0, 1)
                for e2 in range(2):
                    h = 2 * pr + e2
                    ab = accb[e2]
                    slot = 4 + e2
                    stok = [("B", 4)] if e2 == 0 else TB5
                    rden_ = B(slot); rdb_ = B(slot)
                    S.op("dve", lambda e: e.reciprocal(rden_[64:65, 0:T], PS[ab][64:65, 0:T]), reads=[("ps", ab)], writes=stok)
                    pb_ = rotA()
                    S.op("pe", lambda e: e.matmul(PS[pb_][0:64, 0:T], onesf[64:65, 0:64], rden_[64:65, 0:T], start=True, stop=True),
                         reads=["onesf"] + stok, writes=[("ps", pb_)])
                    S.op("dve", lambda e: e.tensor_copy(rdb_[0:64, 0:T], PS[pb_][0:64, 0:T]), reads=[("ps", pb_)], writes=stok)
                    S.op("dve", lambda e: e.tensor_tensor(mixB[0:64, h, 0:T], PS[ab][0:64, 0:T], rdb_[0:64, 0:T], ALU.mult),
                         reads=[("ps", ab)] + stok, writes=[("mixB", h)])

            pending = None
            for pr in range(4):
                accb = (0, 1)
                units = []
                for kt in range(nkt):
                    for kbl in range(4):
                        kb = kt * 4 + kbl
                        if kb >= nkb:
                            break
                        for e2 in range(2):
                            units.append((kt, kbl, kb, e2))

                def emit_qk(u):
                    kt, kbl, kb, e2 = u
                    li = loads.index((pr, kt))
                    ensure_loaded(li + 1)
                    bi = ktbuf[(pr, kt)]
                    nkeys = TB if kb >= kb0 else 128
                    diag = kb - kb0 if kb >= kb0 else -1
                    h = 2 * pr + e2
                    ps_ = rotA()
                    c0 = 128 * diag if (diag > 0 and T == 512) else 0
                    S.op("pe", lambda e: e.matmul(PS[ps_][0:nkeys, c0:T], ktb[bi][:, kbl * 128:kbl * 128 + nkeys],
                                                  qT[:, e2, pr, c0:T], start=True, stop=True),
                         reads=[("ktb", bi), "qT"], writes=[("ps", ps_)])
                    pi = cnt["p"] % 3; cnt["p"] += 1
                    S.op("act", lambda e: e.activation(pT[pi][0:nkeys, c0:T], PS[ps_][0:nkeys, c0:T], AF.Exp,
                                                       bias=biasg[0:nkeys, kb, h:h + 1], scale=1.0),
                         reads=[("ps", ps_)] + TB6, writes=[("pT", pi)])
                    if diag >= 0:
                        S.op("dve", lambda e: e.tensor_tensor(pT[pi][0:nkeys, c0:T], pT[pi][0:nkeys, c0:T], maskD[0:nkeys, diag, c0:T], ALU.mult),
                             reads=[("pT", pi), "maskD"], writes=[("pT", pi)])
                    return (pi, nkeys, c0)

                def emit_pv(u, info):
                    kt, kbl, kb, e2 = u
                    pi, nkeys, c0 = info
                    h = 2 * pr + e2
                    last = (kb == nkb - 1)
                    ab = accb[e2]
                    S.op("pe", lambda e: e.matmul(PS[ab][0:65, c0:T], vaug[0:nkeys, kb, h, :], pT[pi][0:nkeys, c0:T],
                                                  start=(kb == 0), stop=last),
                         reads=[("vaug", kb), ("pT", pi)], writes=[("ps", ab)])

                infos = []
                for i, u in enumerate(units):
                    infos.append(emit_qk(u))
                    if i >= LOOK:
                        emit_pv(units[i - LOOK], infos[i - LOOK])
                for i in range(max(0, len(units) - LOOK), len(units)):
                    emit_pv(units[i], infos[i])
                fin(pr)

            if DBG == "dumpB" and sample:
                for h in range(8):
                    S.op("act", lambda e, h=h: e.activation(A(4)[0:64, h * 32:(h + 1) * 32], mixB[0:64, h, 0:32], AF.Copy),
                         reads=[("mixB", h)], writes=[("A", 4)])
                S.dma("pool", y_dst.rearrange("t (a b) -> (t a) b", b=256)[0:64, :], A(4)[0:64, 0:256], reads=[("A", 4)], is_output=True)
                raise _Stop()
            chk("attn")
            yield "att"
            x2 = lambda tb: scrA[:, 2 * tb:2 * tb + 2, :].rearrange("p a b -> p (a b)")
            x2t = lambda tb: [("A", 2 * tb), ("A", 2 * tb + 1)]
            for tb in range(nblk):
                r0 = t0 + tb * TB
                S.dma("sp", x2(tb)[0:TB, :], x_src[r0:r0 + TB, :], writes=x2t(tb))
            for nh_ in range(2):
                wa = load_w(lambda w: w[:, 0:2048], woA[nh_][:, :], "wob")
                wb = load_w(lambda w: w[0:64, 0:4096], woB[nh_][:, :], "wob")
                wav = wbuf[wa][:, 0:2048].rearrange("p (k n) -> p k n", n=512)
                wbv = wbuf[wb][:, 0:4096].rearrange("p (k n) -> p k n", n=512)
                for tb in range(nblk):
                    p = rotF()
                    hsA = [] if DBG in ("noA", "noAB") else list(range(4))
                    hsB = [] if DBG in ("noB", "noAB") else list(range(8))
                    if DBG == "noAB":
                        continue
                    allh = [("A", h) for h in hsA] + [("B", h) for h in hsB]
                    for ii, (kind, h) in enumerate(allh):
                        if kind == "A":
                            S.op("pe", lambda e, h=h, p=p, tb=tb, ii=ii: e.matmul(PS[p][0:TB, :], mixA[:, h, tb * 128:tb * 128 + TB], wav[:, h, :],
                                                                                 start=(ii == 0), stop=(ii == len(allh) - 1)),
                                 reads=[("mixA", mb, h), ("wbuf", wa)], writes=[("ps", p)])
                        else:
                            S.op("pe", lambda e, h=h, p=p, tb=tb, ii=ii: e.matmul(PS[p][0:TB, :], mixB[:, h, tb * 128:tb * 128 + TB], wbv[:, h, :],
                                                                                 start=(ii == 0), stop=(ii == len(allh) - 1)),
                                 reads=[("mixB", h), ("wbuf", wb)], writes=[("ps", p)])
                    S.op("dve", lambda e, p=p, tb=tb, nh_=nh_: e.tensor_tensor(x2(tb)[0:TB, nh_ * 512:(nh_ + 1) * 512], PS[p][0:TB, :],
                                                                                x2(tb)[0:TB, nh_ * 512:(nh_ + 1) * 512], ALU.add),
                         reads=[("ps", p)] + x2t(tb), writes=x2t(tb))
            for tb in range(nblk):
                i = cnt["x"] % 2; cnt["x"] += 1
                S.op("pool", lambda e, i=i: e.memset(ss1[i][:, :], 0.0), reads=[("ss1", i)], writes=[("ss1", i)])
                S.op("act", lambda e, i=i, tb=tb: e.activation(hb[i][0:TB, :], x2(tb)[0:TB, :], AF.Square, accum_out=ss1[i][0:TB, :]),
                     reads=x2t(tb), writes=[("hb", 0), ("ss1", i)])
                S.op("act", lambda e, i=i: e.activation(ss1[i][0:TB, :], ss1[i][0:TB, :], AF.Ln, bias=EPS, scale=1.0 / D),
                     reads=[("ss1", i)], writes=[("ss1", i)])
                S.op("act", lambda e, i=i: e.activation(ss1[i][0:TB, :], ss1[i][0:TB, :], AF.Exp, scale=-0.5),
                     reads=[("ss1", i)], writes=[("ss1", i)])
                S.op("dve", lambda e, i=i, tb=tb: e.tensor_scalar(hb[i][0:TB, :], x2(tb)[0:TB, :], ss1[i][0:TB, 0:1], None, ALU.mult),
                     reads=x2t(tb) + [("ss1", i)], writes=[("hb", 0)])
                p = rotF()
                for kc in range(8):
                    S.op("pe", lambda e, i=i, kc=kc, p=p: e.transpose(PSB[p][:, kc * 128:kc * 128 + TB], hb[i][0:TB, kc * 128:(kc + 1) * 128],
                                                                    identb[0:TB, 0:TB]),
                         reads=[("hb", 0), "identb"], writes=[("ps", p)])
                S.op("dve", lambda e, p=p, tb=tb: e.tensor_tensor(qrT[:, :, tb * 128:tb * 128 + TB],
                                                                  PSB[p][:, :].rearrange("p (a b) -> p a b", b=128)[:, 0:4, 0:TB],
                                                                  n2s[:, 0:4].unsqueeze(2).to_broadcast([128, 4, TB]), ALU.mult),
                     reads=[("ps", p), "n2s"], writes=[("qrT", h) for h in range(4)])
                S.op("dve", lambda e, p=p, tb=tb: e.tensor_tensor(krT[:, :, tb * 128:tb * 128 + TB],
                                                                  PSB[p][:, :].rearrange("p (a b) -> p a b", b=128)[:, 4:8, 0:TB],
                                                                  n2s[:, 4:8].unsqueeze(2).to_broadcast([128, 4, TB]), ALU.mult),
                     reads=[("ps", p), "n2s"], writes=[("krT", h) for h in range(4)])
            aT = [scrB[:, 0:2, :].rearrange("p a b -> p (a b)").bitcast(BF16).rearrange("p (c t) -> p c t", t=512),
                  scrB[:, 2:4, :].rearrange("p a b -> p (a b)").bitcast(BF16).rearrange("p (c t) -> p c t", t=512)]
            aTt = [[("B", 0), ("B", 1)], [("B", 2), ("B", 3)]]

            def ffn_up(fg):
                wu = load_w(lambda w: w[:, 0:4096], wub[fg][:, :], "wub")
                wv = wbuf[wu][:, 0:4096].rearrange("p (k n) -> p k n", n=512)
                a = fg % 2
                for fc in range(4):
                    p = rotF()
                    for kc in range(8):
                        h2src = qrT if kc < 4 else krT
                        h2tok = ("qrT", kc) if kc < 4 else ("krT", kc - 4)
                        S.op("pe", lambda e, kc=kc, p=p, fc=fc: e.matmul(PS[p][:, 0:T], wv[:, kc, fc * 128:(fc + 1) * 128], h2src[:, kc % 4, 0:T],
                                                                        start=(kc == 0), stop=(kc == 7)),
                             reads=[("wbuf", wu), h2tok], writes=[("ps", p)])
                    S.op("dve", lambda e, p=p, fc=fc, a=a: e.tensor_scalar(aT[a][:, fc, 0:T], PS[p][:, 0:T], 0.0, None, ALU.max),
                         reads=[("ps", p)], writes=aTt[a])
                    S.op("pool", lambda e, fc=fc, a=a: e.tensor_tensor(aT[a][:, fc, 0:T], aT[a][:, fc, 0:T], aT[a][:, fc, 0:T], ALU.mult),
                         reads=aTt[a], writes=aTt[a])

            def ffn_down(fg):
                wd = load_w(lambda w: w[:, 0:4096], wdb[fg][:, :], "wdb")
                wv = wbuf[wd][:, 0:4096].rearrange("p (c n) -> p c n", n=1024)
                a = fg % 2
                for tb in range(nblk):
                    for nh_ in range(2):
                        p = rotF()
                        for fc in range(4):
                            S.op("pe", lambda e, fc=fc, p=p, tb=tb, nh_=nh_: e.matmul(PS[p][0:TB, :], aT[a][:, fc, tb * 128:tb * 128 + TB],
                                                                                     wv[:, fc, nh_ * 512:(nh_ + 1) * 512],
                                                                                     start=(fc == 0), stop=(fc == 3)),
                                 reads=aTt[a] + [("wbuf", wd)], writes=[("ps", p)])
                        S.op("dve", lambda e, p=p, tb=tb, nh_=nh_: e.tensor_tensor(x2(tb)[0:TB, nh_ * 512:(nh_ + 1) * 512], PS[p][0:TB, :],
                                                                                    x2(tb)[0:TB, nh_ * 512:(nh_ + 1) * 512], ALU.add),
                             reads=[("ps", p)] + x2t(tb), writes=x2t(tb))

            chk("wout")
            yield "wout"
            ffn_up(0)
            for fg in range(8):
                if fg + 1 < 8:
                    ffn_up(fg + 1)
                ffn_down(fg)
            for tb in range(nblk):
                r0 = t0 + tb * TB
                S.dma("pool", y_dst[r0:r0 + TB, :], x2(tb)[0:TB, :], reads=x2t(tb), is_output=True)

        def main_seq():
            chk("conv")
            if with_sample:
                S.op("pool", lambda e: e.memset(carry[:, :], 0.0), writes=["carry"])
                S.dma("sp", S32[:, :, :], sth.rearrange("h k v -> k h v"), writes=["S32"])
                for kb in range(8):
                    j = kb % 2
                    kn32 = B(1 + j); knb = scrB[:, 5 + j, 256:512].bitcast(BF16); v32 = B(3 + j)
                    S.dma("sp", kn32[:, :], ck[kb * 128:(kb + 1) * 128, :], writes=[("B", 1 + j)])
                    S.op("pool", lambda e, kn32=kn32, knb=knb: e.tensor_copy(knb[:, :], kn32[:, :]), reads=[("B", 1 + j)], writes=[("Bk", 5 + j)])
                    kT_transposes(knb, 128, (kb % 4) * 128, ("Bk", 5 + j))
                    if kb % 4 == 3:
                        k0 = (kb // 4) * 512
                        S.dma("pool", kTs[:, :, k0:k0 + 512].rearrange("q p k -> p q k"), ktst[:, :, :], reads=["ktst"],
                              writes=[("kTd", True, kb // 4)])
                    S.dma("sp", v32[:, :], cv[kb * 128:(kb + 1) * 128, :], writes=[("B", 3 + j)])
                    S.op("dve", lambda e, kb=kb, v32=v32: e.tensor_copy(vaug[:, kb, :, 0:64], v32[:, :].rearrange("p (h d) -> p h d", d=64)),
                         reads=[("B", 3 + j)], writes=[("vaug", kb)])
                    S.dma("sp", lf[j][:, :], cl[kb * 128:(kb + 1) * 128, :], writes=[("lf", j)])
                    fox_cumsum(lf[j], 128, kb, ("lf", j))
                S.op("pool", lambda e: e.tensor_copy(cref[:, :], carry[:, :]), reads=["carry"], writes=["cref"])
                chk("sprep")
                for _ in tile(32, xs, 0, 8, kTs, 1024, (ys, nks, nvs, nls), "s", True):
                    pass
                S.dma("pool", nhs.rearrange("h k v -> k h v"), S32[:, :, :], reads=["S32"], is_output=True)
                chk("sample")
            S.op("pool", lambda e: e.memset(carry[:, :], 0.0), reads=["carry"], writes=["carry"])
            S.op("pool", lambda e: e.memset(S32[:, :, :].rearrange("p a b -> p (a b)"), 0.0), reads=["S32"], writes=["S32"])
            gens = [tile(512, xp, g * 512, 4 * g, kTp, g * 512, (yp, nk, nv, nl), g, False) for g in range(NT)]

            def fin_gen(gen):
                for _ in gen:
                    pass

            if not PIPE:
                for gen in gens:
                    fin_gen(gen)
            else:
                def nx(i, fin_=False):
                    if SIM_PARITY is not None:
                        SIM_PARITY[0] = i % 2
                    if fin_:
                        fin_gen(gens[i])
                    else:
                        next(gens[i])

                nx(0)
                nx(0)
                for g in range(1, NT):
                    nx(g)
                    nx(g - 1)
                    nx(g)
                    nx(g - 1, True)
                nx(NT - 1, True)
            S.dma("pool", nh.rearrange("h k v -> k h v"), S32[:, :, :], reads=["S32"], is_output=True)

        try:
            main_seq()
        except _Stop:
            pass
        S.finish()
        S.emit(block)
    return nc


def host_consts():
    ident = np.eye(128, dtype=np.float32)
    ii = np.arange(128)
    tri = (ii[:, None] <= ii[None, :]).astype(np.float32)
    maskA = ((ii[:, None] <= ii[None, :]) & ((ii[:, None] // 64) == (ii[None, :] // 64))).astype(np.float32)
    q = np.arange(512)
    maskD = np.concatenate([(q[None, :] >= (128 * j + ii[:, None])).astype(np.float32) for j in range(4)], axis=1)
    return ident, tri, maskA, maskD


def make_in_maps(inp, n_cores, NT):
    SEQ = NT * 512
    ident, tri, maskA, maskD = host_consts()
    f = lambda a: np.ascontiguousarray(np.asarray(a, dtype=np.float32))
    lbl = f(inp["hgrn_lb_logits"])
    lblT = np.concatenate([lbl[0].reshape(4, 128).T, lbl[1].reshape(4, 128).T], axis=1)
    shared = {
        "w_in": f(inp["w_in"][0]), "w_out": f(inp["w_out"][0]), "w_up": f(inp["w_up"][0]), "w_down": f(inp["w_down"][0]),
        "n1T": f(np.asarray(inp["norm1"][0]).reshape(8, 128).T), "n2T": f(np.asarray(inp["norm2"][0]).reshape(8, 128).T),
        "gnT": f(np.asarray(inp["hgrn_out_norm"][0]).reshape(4, 128).T), "lbl": f(lblT),
        "qg": f(np.asarray(inp["q_norm_gain"][0]).reshape(1, 64)), "kg": f(np.asarray(inp["k_norm_gain"][0]).reshape(1, 64)),
        "bfx": f(np.asarray(inp["b_fox_f"][0]).reshape(1, 8)),
        "c_ident": ident, "c_tri": tri, "c_maskA": maskA, "c_maskD": maskD,
    }
    maps = []
    for c in range(n_cores):
        m = dict(shared)
        m["xp"] = f(inp["x_prompt"][c][:SEQ])
        m["xs"] = f(inp["x_sample"][c])
        m["ck"] = f(np.asarray(inp["cache_fox_k"][0, c]).reshape(1024, 512))
        m["cv"] = f(np.asarray(inp["cache_fox_v"][0, c]).reshape(1024, 512))
        m["cl"] = f(inp["cache_fox_logf"][0, c])
        m["sth"] = f(inp["state_hgrn"][0, c])
        maps.append(m)
    return maps


def run(inp, n_cores=8, NT=16, stop=None):
    nc = build_program(NT, stop=stop)
    maps = make_in_maps(inp, n_cores, NT)
    res = run_bass_kernel_spmd(nc, maps, core_ids=list(range(n_cores)))
    R = res.results
    SEQ = NT * 512
    st = lambda k: np.stack([np.asarray(R[c][k], dtype=np.float32) for c in range(n_cores)], axis=0)
    y_p = st("yp"); y_s = st("ys")
    nk_ = st("nk").reshape(1, n_cores, SEQ, 8, 64); nv_ = st("nv").reshape(1, n_cores, SEQ, 8, 64)
    nl_ = st("nl").reshape(1, n_cores, SEQ, 8); nh_ = st("nh").reshape(1, n_cores, 4, 128, 128)
    nks_ = st("nks").reshape(1, n_cores, 32, 8, 64); nvs_ = st("nvs").reshape(1, n_cores, 32, 8, 64)
    nls_ = st("nls").reshape(1, n_cores, 32, 8); nhs_ = st("nhs").reshape(1, n_cores, 4, 128, 128)
    return (y_p, y_s, nk_, nv_, nl_, nh_, nks_, nvs_, nls_, nhs_)


def kernel(**inputs):
    return run(inputs, n_cores=8, NT=16)
```
